# Optimizing a Trainium2 kernel written in Bass

```python
import math
import jax
import jax.numpy as jnp
from jax import lax
import numpy as np

D_MODEL = 1024
BATCH = 16
SEQ = 2048
DEPTH = 2
DEC_BATCH = 128
DEC_SEQ = 1
PAST_LEN = 16384
PAGE_SIZE = 128

N_AB = (DEPTH + 1) // 2
N_C = DEPTH // 2

CONV_WIDTH = 4
GDN_HEADS = 4
GDN_DK = 128
GDN_DV = 128
GDN_CHUNK = 64
GDN_QK_W = GDN_HEADS * GDN_DK
GDN_QKV_W = 2 * GDN_QK_W + GDN_HEADS * GDN_DV
GDN_V_W = GDN_HEADS * GDN_DV
LRU_WIDTH = D_MODEL // 2
LRU_BLOCKS = 8
LRU_BW = LRU_WIDTH // LRU_BLOCKS
LRU_C = 8.0
AB_IN_W = GDN_QKV_W + GDN_V_W + 2 * GDN_HEADS + 2 * LRU_WIDTH
AB_MIX_W = GDN_V_W + LRU_WIDTH
SWA_HEADS = 16
SWA_KV_HEADS = 4
SWA_GROUP = SWA_HEADS // SWA_KV_HEADS
SWA_HEAD_DIM = 64
WINDOW = 128
SWA_KV_W = SWA_KV_HEADS * SWA_HEAD_DIM
SWA_OUT_W = SWA_HEADS * SWA_HEAD_DIM
SWA_QKV_W = SWA_OUT_W + 2 * SWA_KV_W
D_FF = -(-8 * D_MODEL // (3 * 256)) * 256
NORM_EPS = 1e-6

kernel_name = 'hybrid_gdn_rglru_swa_decoder_step'


def rmsnorm(x, w):
    xf = x.astype(jnp.float32)
    y = xf * lax.rsqrt(jnp.mean(xf * xf, axis=-1, keepdims=True) + NORM_EPS)
    return (y * w.astype(jnp.float32)).astype(x.dtype)


def l2norm(x):
    return x * lax.rsqrt(jnp.sum(x * x, axis=-1, keepdims=True) + NORM_EPS)


def causal_conv(x, buf, w, b=None):
    xp = jnp.concatenate([buf.astype(x.dtype), x], axis=1)
    y = lax.conv_general_dilated(xp, w[:, None, :].astype(x.dtype), window_strides=(1,), padding='VALID',
                                 dimension_numbers=('NWC', 'WIO', 'NWC'), feature_group_count=x.shape[-1])
    if b is not None:
        y = y + b.astype(x.dtype)
    return y, xp[:, xp.shape[1] - (CONV_WIDTH - 1):]


def gdn_chunked(q, k, v, beta, g, s0):
    bsz, t, h, dk = q.shape
    dv = v.shape[-1]
    c = GDN_CHUNK
    n = t // c

    def blk(a):
        return jnp.moveaxis(a.reshape((bsz, n, c, h) + a.shape[3:]), 3, 1)

    q, k, v, beta, g = blk(q), blk(k), blk(v), blk(beta), blk(g)
    gam = jnp.cumsum(g, axis=-1)
    causal = jnp.tril(jnp.ones((c, c), dtype=bool))
    strict = jnp.tril(jnp.ones((c, c), dtype=bool), -1)
    decay = jnp.exp(jnp.where(causal, gam[..., :, None] - gam[..., None, :], -jnp.inf))
    kk = jnp.einsum('bhnid,bhnjd->bhnij', k, k)
    a_mat = jnp.where(strict, beta[..., None] * kk * decay, 0.0) + jnp.eye(c, dtype=q.dtype)
    rhs = jnp.concatenate([beta[..., None] * v, (beta * jnp.exp(gam))[..., None] * k], axis=-1)
    sol = lax.linalg.triangular_solve(a_mat, rhs, left_side=True, lower=True, unit_diagonal=True)
    w_val, w_key = sol[..., :dv], sol[..., dv:]
    qk = jnp.einsum('bhnid,bhnjd->bhnij', q, k) * decay
    q_dec = q * jnp.exp(gam)[..., None]
    k_tail = k * jnp.exp(gam[..., -1:] - gam)[..., None]
    g_tot = jnp.exp(gam[..., -1])

    def step(s, xs):
        wv, wk, qkn, qd, kt, gt = xs
        u = wv - jnp.einsum('bhcd,bhde->bhce', wk, s)
        o = jnp.einsum('bhcd,bhde->bhce', qd, s) + jnp.einsum('bhcj,bhje->bhce', qkn, u)
        s = s * gt[..., None, None] + jnp.einsum('bhcd,bhce->bhde', kt, u)
        return s, o

    xs = tuple(jnp.moveaxis(a, 2, 0) for a in (w_val, w_key, qk, q_dec, k_tail, g_tot))
    s_fin, o = lax.scan(step, s0, xs)
    o = jnp.transpose(o, (1, 0, 3, 2, 4)).reshape(bsz, t, h, dv)
    return o, s_fin


def gdn_stepwise(q, k, v, beta, g, s0):
    def step(s, xs):
        qt, kt, vt, bt, gt = xs
        s = s * jnp.exp(gt)[..., None, None]
        pred = jnp.einsum('bhd,bhde->bhe', kt, s)
        s = s + jnp.einsum('bhd,bhe->bhde', kt, bt[..., None] * (vt - pred))
        return s, jnp.einsum('bhd,bhde->bhe', qt, s)

    xs = tuple(jnp.swapaxes(a, 0, 1) for a in (q, k, v, beta, g))
    s_fin, o = lax.scan(step, s0, xs)
    return jnp.swapaxes(o, 0, 1), s_fin


def rglru(x, h0, wa, ba, wx, bx, lam):
    bsz, t, w = x.shape
    xb = x.reshape(bsz, t, LRU_BLOCKS, LRU_BW)
    r = jax.nn.sigmoid(jnp.einsum('btni,nij->btnj', xb, wa).reshape(bsz, t, w) + ba)
    i = jax.nn.sigmoid(jnp.einsum('btni,nij->btnj', xb, wx).reshape(bsz, t, w) + bx)
    log_a = -LRU_C * r * jax.nn.softplus(-lam)
    a = jnp.exp(log_a)
    b = jnp.sqrt(-jnp.expm1(2.0 * log_a)) * (i * x)

    def comb(lhs, rhs):
        return lhs[0] * rhs[0], rhs[0] * lhs[1] + rhs[1]

    a_cum, hs = lax.associative_scan(comb, (a, b), axis=1)
    hs = hs + a_cum * h0[:, None, :]
    return hs, hs[:, -1]


def mixer_ab(hn, w_in, conv_gdn_w, a_log, dt_bias, gdn_norm_w, conv_lru_w, conv_lru_b, lru_wa, lru_ba,
             lru_wx, lru_bx, lru_lambda, w_out, s0, gdn_buf, h0, lru_buf, chunked):
    f32 = jnp.float32
    bsz, t, _ = hn.shape
    c1 = GDN_QKV_W
    c2 = c1 + GDN_V_W
    c3 = c2 + GDN_HEADS
    c4 = c3 + GDN_HEADS
    c5 = c4 + LRU_WIDTH
    proj = hn @ w_in
    qkv, z, b_in, a_in, gate_in, xr_in = jnp.split(proj, [c1, c2, c3, c4, c5], axis=-1)
    qkv, gdn_buf_new = causal_conv(qkv, gdn_buf, conv_gdn_w)
    qkv = jax.nn.silu(qkv).astype(f32)
    q = l2norm(qkv[..., :GDN_QK_W].reshape(bsz, t, GDN_HEADS, GDN_DK)) * GDN_DK ** -0.5
    k = l2norm(qkv[..., GDN_QK_W:2 * GDN_QK_W].reshape(bsz, t, GDN_HEADS, GDN_DK))
    v = qkv[..., 2 * GDN_QK_W:].reshape(bsz, t, GDN_HEADS, GDN_DV)
    beta = jax.nn.sigmoid(b_in.astype(f32))
    g = -jnp.exp(a_log.astype(f32)) * jax.nn.softplus(a_in.astype(f32) + dt_bias.astype(f32))
    core = gdn_chunked if chunked else gdn_stepwise
    o, s_new = core(q, k, v, beta, g, s0.astype(f32))
    o = o * lax.rsqrt(jnp.mean(o * o, axis=-1, keepdims=True) + NORM_EPS) * gdn_norm_w.astype(f32)
    o = o * jax.nn.silu(z.astype(f32).reshape(bsz, t, GDN_HEADS, GDN_DV))
    o = o.reshape(bsz, t, GDN_V_W).astype(hn.dtype)
    xr, lru_buf_new = causal_conv(xr_in, lru_buf, conv_lru_w, conv_lru_b)
    hs, h_new = rglru(xr.astype(f32), h0.astype(f32), lru_wa.astype(f32), lru_ba.astype(f32),
                      lru_wx.astype(f32), lru_bx.astype(f32), lru_lambda.astype(f32))
    y_lru = jax.nn.gelu(gate_in) * hs.astype(hn.dtype)
    out = jnp.concatenate([o, y_lru], axis=-1) @ w_out
    return out, s_new.astype(hn.dtype), gdn_buf_new, h_new.astype(hn.dtype), lru_buf_new


def alibi_slopes():
    return jnp.exp2(-8.0 * jnp.arange(1, SWA_HEADS + 1, dtype=jnp.float32) / SWA_HEADS)


def window_attend(qg, kk, vv, rel, keep, sinks, slopes):
    f32 = jnp.float32
    s = jnp.einsum('bqhgd,bkhd->bhgqk', qg, kk).astype(f32) * SWA_HEAD_DIM ** -0.5
    s = s - slopes.reshape(SWA_KV_HEADS, SWA_GROUP, 1, 1) * rel.astype(f32)
    s = jnp.where(keep, s, -jnp.inf)
    sink = sinks.astype(f32).reshape(SWA_KV_HEADS, SWA_GROUP, 1, 1)
    m = jnp.maximum(jnp.max(s, axis=-1, keepdims=True), sink)
    p = jnp.exp(s - m)
    p = p / (jnp.sum(p, axis=-1, keepdims=True) + jnp.exp(sink - m))
    return jnp.einsum('bhgqk,bkhd->bqhgd', p.astype(vv.dtype), vv)


def mixer_c(hn, w_qkv, w_out, sinks, k_buf, v_buf):
    bsz, t, _ = hn.shape
    qkv = hn @ w_qkv
    q = qkv[..., :SWA_OUT_W].reshape(bsz, t, SWA_KV_HEADS, SWA_GROUP, SWA_HEAD_DIM)
    k = qkv[..., SWA_OUT_W:SWA_OUT_W + SWA_KV_W].reshape(bsz, t, SWA_KV_HEADS, SWA_HEAD_DIM)
    v = qkv[..., SWA_OUT_W + SWA_KV_W:].reshape(bsz, t, SWA_KV_HEADS, SWA_HEAD_DIM)
    slopes = alibi_slopes()
    if k_buf is None:
        nb = t // WINDOW
        qb = jnp.moveaxis(q.reshape(bsz, nb, WINDOW, SWA_KV_HEADS, SWA_GROUP, SWA_HEAD_DIM), 1, 0)

        def band(a):
            ab = jnp.moveaxis(a.reshape(bsz, nb, WINDOW, SWA_KV_HEADS, SWA_HEAD_DIM), 1, 0)
            prev = jnp.concatenate([jnp.zeros_like(ab[:1]), ab[:-1]], axis=0)
            return jnp.concatenate([prev, ab], axis=2)

        kb, vb = band(k), band(v)
        kj = jnp.arange(2 * WINDOW)[None, :]
        rel = jnp.arange(WINDOW)[:, None] + WINDOW - kj
        in_band = (rel >= 0) & (rel <= WINDOW)

        def one_block(args):
            qn, kn, vn, n = args
            keep = in_band & (n * WINDOW + kj - WINDOW >= 0)
            return window_attend(qn, kn, vn, rel, keep, sinks, slopes)

        o = lax.map(one_block, (qb, kb, vb, jnp.arange(nb)))
        o = jnp.moveaxis(o, 0, 1).reshape(bsz, t, SWA_OUT_W)
        k_new, v_new = k[:, t - WINDOW:], v[:, t - WINDOW:]
    else:
        lb = k_buf.shape[1]
        kk = jnp.concatenate([k_buf.astype(k.dtype), k], axis=1)
        vv = jnp.concatenate([v_buf.astype(v.dtype), v], axis=1)
        rel = (lb + jnp.arange(t))[:, None] - jnp.arange(lb + t)[None, :]
        keep = (rel >= 0) & (rel <= WINDOW)
        o = window_attend(q, kk, vv, rel, keep, sinks, slopes).reshape(bsz, t, SWA_OUT_W)
        k_new, v_new = kk[:, t:], vv[:, t:]
    return o @ w_out, k_new, v_new


def swiglu(hn, w_gate_up, w_down):
    gu = hn @ w_gate_up
    return (jax.nn.silu(gu[..., :D_FF]) * gu[..., D_FF:]) @ w_down


def run_trunk(x, gdn_s, gdn_cb, lru_h, lru_cb, swa_k, swa_v, prm, is_prompt):
    n_s, n_gcb, n_h, n_lcb, n_k, n_v = [], [], [], [], [], []
    ia = 0
    ic = 0
    for layer in range(DEPTH):
        hn = rmsnorm(x, prm['norm_mix'][layer])
        if layer % 2 == 0:
            out, s_new, gcb, h_new, lcb = mixer_ab(
                hn, prm['w_in_ab'][ia], prm['conv_gdn_w'][ia], prm['gdn_a_log'][ia], prm['gdn_dt_bias'][ia],
                prm['gdn_norm_w'][ia], prm['conv_lru_w'][ia], prm['conv_lru_b'][ia], prm['lru_wa'][ia],
                prm['lru_ba'][ia], prm['lru_wx'][ia], prm['lru_bx'][ia], prm['lru_lambda'][ia], prm['w_out_ab'][ia],
                gdn_s[ia], gdn_cb[ia], lru_h[ia], lru_cb[ia], is_prompt)
            n_s.append(s_new)
            n_gcb.append(gcb)
            n_h.append(h_new)
            n_lcb.append(lcb)
            ia += 1
        else:
            kb = None if is_prompt else swa_k[ic]
            vb = None if is_prompt else swa_v[ic]
            out, k_new, v_new = mixer_c(hn, prm['w_qkv_c'][ic], prm['w_out_c'][ic], prm['sinks_c'][ic], kb, vb)
            n_k.append(k_new)
            n_v.append(v_new)
            ic += 1
        x = x + out
        x = x + swiglu(rmsnorm(x, prm['norm_ffn'][layer]), prm['w_gate_up'][layer], prm['w_down'][layer])
    y = rmsnorm(x, prm['norm_final'])
    return (y, jnp.stack(n_s), jnp.stack(n_gcb), jnp.stack(n_h), jnp.stack(n_lcb),
            jnp.stack(n_k), jnp.stack(n_v))


def setup_inputs(seed: int = 0) -> dict:
    key = jax.random.key(seed)
    sub = jax.random.split(key, 32)
    idx = list(range(32))
    f32 = jnp.float32

    def nk():
        return sub[idx.pop()]

    def nrm(shape, scale):
        return jax.random.normal(nk(), shape, f32) * scale

    def gain(shape):
        return 1.0 + nrm(shape, 0.02)

    lb = min(WINDOW, PAST_LEN)
    dt = jnp.exp(jax.random.uniform(nk(), (N_AB, GDN_HEADS), f32, math.log(1e-3), math.log(1e-1)))
    a_pow = jax.random.uniform(nk(), (N_AB, LRU_WIDTH), f32, 0.9, 0.999)
    a_base = a_pow ** (1.0 / LRU_C)
    return {
        'x_prompt': nrm((BATCH, SEQ, D_MODEL), 1.0),
        'x_sample': nrm((DEC_BATCH, DEC_SEQ, D_MODEL), 1.0),
        'state_gdn': nrm((N_AB, DEC_BATCH, GDN_HEADS, GDN_DK, GDN_DV), 0.3),
        'state_gdn_conv': nrm((N_AB, DEC_BATCH, CONV_WIDTH - 1, GDN_QKV_W), 1.0),
        'state_lru': nrm((N_AB, DEC_BATCH, LRU_WIDTH), 0.5),
        'state_lru_conv': nrm((N_AB, DEC_BATCH, CONV_WIDTH - 1, LRU_WIDTH), 1.0),
        'cache_swa_k': nrm((N_C, DEC_BATCH, lb, SWA_KV_HEADS, SWA_HEAD_DIM), 1.0),
        'cache_swa_v': nrm((N_C, DEC_BATCH, lb, SWA_KV_HEADS, SWA_HEAD_DIM), 1.0),
        'norm_mix': gain((DEPTH, D_MODEL)),
        'norm_ffn': gain((DEPTH, D_MODEL)),
        'norm_final': gain((D_MODEL,)),
        'w_in_ab': nrm((N_AB, D_MODEL, AB_IN_W), D_MODEL ** -0.5),
        'conv_gdn_w': nrm((N_AB, CONV_WIDTH, GDN_QKV_W), CONV_WIDTH ** -0.5),
        'gdn_a_log': jnp.log(jax.random.uniform(nk(), (N_AB, GDN_HEADS), f32, 1.0, 16.0)),
        'gdn_dt_bias': dt + jnp.log(-jnp.expm1(-dt)),
        'gdn_norm_w': gain((N_AB, GDN_DV)),
        'conv_lru_w': nrm((N_AB, CONV_WIDTH, LRU_WIDTH), CONV_WIDTH ** -0.5),
        'conv_lru_b': nrm((N_AB, LRU_WIDTH), 0.02),
        'lru_wa': nrm((N_AB, LRU_BLOCKS, LRU_BW, LRU_BW), LRU_BW ** -0.5),
        'lru_ba': nrm((N_AB, LRU_WIDTH), 0.02),
        'lru_wx': nrm((N_AB, LRU_BLOCKS, LRU_BW, LRU_BW), LRU_BW ** -0.5),
        'lru_bx': nrm((N_AB, LRU_WIDTH), 0.02),
        'lru_lambda': jnp.log(a_base) - jnp.log1p(-a_base),
        'w_out_ab': nrm((N_AB, AB_MIX_W, D_MODEL), AB_MIX_W ** -0.5),
        'w_qkv_c': nrm((N_C, D_MODEL, SWA_QKV_W), D_MODEL ** -0.5),
        'w_out_c': nrm((N_C, SWA_OUT_W, D_MODEL), SWA_OUT_W ** -0.5),
        'sinks_c': nrm((N_C, SWA_HEADS), 1.0),
        'w_gate_up': nrm((DEPTH, D_MODEL, 2 * D_FF), D_MODEL ** -0.5),
        'w_down': nrm((DEPTH, D_FF, D_MODEL), D_FF ** -0.5),
    }


def reference(x_prompt, x_sample, state_gdn, state_gdn_conv, state_lru, state_lru_conv, cache_swa_k, cache_swa_v,
              norm_mix, norm_ffn, norm_final, w_in_ab, conv_gdn_w, gdn_a_log, gdn_dt_bias, gdn_norm_w,
              conv_lru_w, conv_lru_b, lru_wa, lru_ba, lru_wx, lru_bx, lru_lambda, w_out_ab, w_qkv_c, w_out_c,
              sinks_c, w_gate_up, w_down):
    prm = dict(norm_mix=norm_mix, norm_ffn=norm_ffn, norm_final=norm_final, w_in_ab=w_in_ab,
               conv_gdn_w=conv_gdn_w, gdn_a_log=gdn_a_log, gdn_dt_bias=gdn_dt_bias, gdn_norm_w=gdn_norm_w,
               conv_lru_w=conv_lru_w, conv_lru_b=conv_lru_b, lru_wa=lru_wa, lru_ba=lru_ba, lru_wx=lru_wx,
               lru_bx=lru_bx, lru_lambda=lru_lambda, w_out_ab=w_out_ab, w_qkv_c=w_qkv_c, w_out_c=w_out_c,
               sinks_c=sinks_c, w_gate_up=w_gate_up, w_down=w_down)
    bp = x_prompt.shape[0]
    dt = x_prompt.dtype
    z_s = jnp.zeros((N_AB, bp, GDN_HEADS, GDN_DK, GDN_DV), dt)
    z_gcb = jnp.zeros((N_AB, bp, CONV_WIDTH - 1, GDN_QKV_W), dt)
    z_h = jnp.zeros((N_AB, bp, LRU_WIDTH), dt)
    z_lcb = jnp.zeros((N_AB, bp, CONV_WIDTH - 1, LRU_WIDTH), dt)
    y_prompt, p_gdn, p_gdn_conv, p_lru, p_lru_conv, p_swa_k, p_swa_v = run_trunk(
        x_prompt, z_s, z_gcb, z_h, z_lcb, None, None, prm, True)
    y_sample, s_gdn, s_gdn_conv, s_lru, s_lru_conv, s_swa_k, s_swa_v = run_trunk(
        x_sample, state_gdn, state_gdn_conv, state_lru, state_lru_conv, cache_swa_k, cache_swa_v, prm, False)
    return (y_prompt, y_sample, p_gdn, p_gdn_conv, p_lru, p_lru_conv, p_swa_k, p_swa_v,
            s_gdn, s_gdn_conv, s_lru, s_lru_conv, s_swa_k, s_swa_v)
```

```python
import numpy as np
import concourse.bass as bass
import concourse.mybir as mybir
from concourse.bass_utils import run_bass_kernel_spmd

F32 = mybir.dt.float32
BF16 = mybir.dt.bfloat16
AF = mybir.ActivationFunctionType
ALU = mybir.AluOpType
AX = mybir.AxisListType

NCORES = 8
D = 1024
SEQ = 2048
NB = 2
NS = 16
DFF = 2816
NFC = DFF // 128
T = 512
DBG = {}


class _Stop(Exception):
    pass


def _chk(tag):
    if DBG.get('stop') == tag:
        raise _Stop()
EPS = 1e-6
W_IN = 3080


class Prog:
    SEM_MAX = 60000
    ENG = ("sp", "act", "dve", "pool", "pe")

    def __init__(self, nc):
        self.nc = nc
        self.q = {n: [] for n in self.ENG}
        self.semh = {}
        self.cur = {}
        self.cnt = {}
        self.gen = {n: 0 for n in self.ENG}
        for n in self.ENG:
            self._new_eng_sem(n)
        self.waited = {n: {} for n in self.ENG}
        self.lastw = {}
        self.readers = {}
        self.ndma = 12
        self.dma_slots = {}
        self.dma_rr = {}
        self.dma_val = {}
        for n in ("sp", "act", "pool"):
            names = []
            for i in range(self.ndma):
                nm = f"d_{n}_{i}"
                self.semh[nm] = nc.alloc_semaphore(nm)
                self.dma_val[nm] = 0
                names.append(nm)
            self.dma_slots[n] = names
            self.dma_rr[n] = 0

    def _new_eng_sem(self, n):
        nm = f"e_{n}_{self.gen[n]}"
        self.gen[n] += 1
        self.semh[nm] = self.nc.alloc_semaphore(nm)
        self.cur[n] = nm
        self.cnt[n] = 0

    def _deps(self, reads, writes):
        deps = {}

        def add(t):
            if t is not None and deps.get(t[0], 0) < t[1]:
                deps[t[0]] = t[1]
        for k in reads:
            add(self.lastw.get(k))
        for k in writes:
            add(self.lastw.get(k))
            for s, v in self.readers.get(k, {}).items():
                add((s, v))
        return deps

    def _filter(self, eng, deps):
        waits = []
        w = self.waited[eng]
        for s, v in deps.items():
            if eng == "pe" and s.startswith("e_pe_"):
                continue
            if w.get(s, 0) >= v:
                continue
            w[s] = v
            waits.append((s, v))
        return waits

    def _commit(self, tok, reads, writes):
        for k in reads:
            r = self.readers.setdefault(k, {})
            if r.get(tok[0], 0) < tok[1]:
                r[tok[0]] = tok[1]
        for k in writes:
            self.lastw[k] = tok
            self.readers[k] = {}

    def op(self, eng, fn, reads=(), writes=()):
        writes = list(writes) + [k for k in reads if k.startswith("ps") and k not in writes]
        deps = self._deps(reads, writes)
        waits = self._filter(eng, deps)
        if self.cnt[eng] >= self.SEM_MAX:
            self._new_eng_sem(eng)
        self.cnt[eng] += 1
        tok = (self.cur[eng], self.cnt[eng])
        self.q[eng].append((waits, fn, tok, 1))
        self._commit(tok, reads, writes)

    def dma(self, eng, out, in_, reads=(), writes=(), **kw):
        deps = self._deps(reads, writes)
        slot = self.dma_slots[eng][self.dma_rr[eng] % self.ndma]
        self.dma_rr[eng] += 1
        if self.dma_val[slot] >= self.SEM_MAX:
            raise RuntimeError("dma sem overflow")
        if self.dma_val[slot] > 0:
            if deps.get(slot, 0) < self.dma_val[slot]:
                deps[slot] = self.dma_val[slot]
        waits = self._filter(eng, deps)
        self.dma_val[slot] += 16
        tok = (slot, self.dma_val[slot])
        self.q[eng].append((waits, lambda e: e.dma_start(out=out, in_=in_, **kw), tok, 16))
        self._commit(tok, reads, writes)

    def barrier(self):
        deps = {}
        for n in self.ENG:
            if self.cnt[n] > 0:
                deps[self.cur[n]] = self.cnt[n]
        for s, v in self.dma_val.items():
            if v > 0:
                deps[s] = v
        for n in self.ENG:
            waits = self._filter(n, dict(deps))
            if waits:
                self.q[n].append((waits, None, None, 0))
        self.lastw = {}
        self.readers = {}

    def emit(self):
        self.barrier()
        nc = self.nc
        with nc.Block() as block:
            decos = {"sp": block.sync, "act": block.scalar, "dve": block.vector, "pool": block.gpsimd, "pe": block.tensor}
            for n in self.ENG:
                def body(e, n=n):
                    for waits, fn, tok, inc in self.q[n]:
                        for s, v in waits:
                            e.wait_ge(self.semh[s], v)
                        if fn is not None:
                            ins = fn(e)
                            ins.then_inc(self.semh[tok[0]], inc)
                decos[n](body)


class Arena:
    def __init__(self, nc, lo=16640, hi=229000):
        self.nc = nc
        self.lo = lo
        self.hi = hi
        self.off = lo
        self.n = 0

    def alloc(self, name, shape, dtype):
        per = 1
        for s in shape[1:]:
            per *= s
        nbytes = per * (2 if dtype == BF16 else 4)
        off = (self.off + 63) // 64 * 64
        if off + nbytes > self.hi:
            raise RuntimeError(f"SBUF arena overflow allocating {name}: {off}+{nbytes}")
        self.off = off + nbytes
        self.n += 1
        return self.nc.alloc_sbuf_tensor_at(f"{name}_{self.n}", list(shape), dtype, offset=off)

    def mark(self):
        return self.off

    def release(self, m):
        self.off = m


def build_program(stages=None):
    nc = bass.Bass("TRN2", target_bir_lowering=False)
    P = Prog(nc)
    A = Arena(nc)

    def din(name, shape):
        return nc.dram_tensor(name, list(shape), F32, kind="ExternalInput").ap()

    def dout(name, shape):
        return nc.dram_tensor(name, list(shape), F32, kind="ExternalOutput").ap()

    def dscr(name, shape):
        return nc.dram_tensor(name, list(shape), F32, kind="Internal").ap()

    x_prompt = din("x_prompt", [NB, SEQ, D])
    x_sample = din("x_sample", [NS, D])
    state_gdn = din("state_gdn", [NS, 4, 128, 128])
    state_gdn_conv = din("state_gdn_conv", [NS, 3, 1536])
    state_lru = din("state_lru", [NS, 512])
    state_lru_conv = din("state_lru_conv", [NS, 3, 512])
    cache_k = din("cache_swa_k", [NS, 128, 256])
    cache_v = din("cache_swa_v", [NS, 128, 256])
    norm_mix = din("norm_mix", [2, D])
    norm_ffn = din("norm_ffn", [2, D])
    norm_final = din("norm_final", [D])
    w_in_ab = din("w_in_ab", [D, W_IN])
    conv_gdn_w = din("conv_gdn_w", [4, 1536])
    gdn_a_log = din("gdn_a_log", [4])
    gdn_dt_bias = din("gdn_dt_bias", [4])
    gdn_norm_w = din("gdn_norm_w", [128])
    conv_lru_w = din("conv_lru_w", [4, 512])
    conv_lru_b = din("conv_lru_b", [512])
    lru_wa = din("lru_wa", [8, 64, 64])
    lru_ba = din("lru_ba", [512])
    lru_wx = din("lru_wx", [8, 64, 64])
    lru_bx = din("lru_bx", [512])
    lru_lambda = din("lru_lambda", [512])
    w_out_ab = din("w_out_ab", [D, D])
    w_qkv_c = din("w_qkv_c", [D, 1536])
    w_out_c = din("w_out_c", [D, D])
    sinks_c = din("sinks_c", [16])
    w_gate_up = din("w_gate_up", [2, D, 2 * DFF])
    w_down = din("w_down", [2, DFF, D])

    y_prompt = dout("y_prompt", [NB, SEQ, D])
    y_sample = dout("y_sample", [NS, D])
    p_gdn = dout("p_gdn", [NB, 4, 128, 128])
    p_gdn_conv = dout("p_gdn_conv", [NB, 3, 1536])
    p_lru = dout("p_lru", [NB, 512])
    p_lru_conv = dout("p_lru_conv", [NB, 3, 512])
    p_swa_k = dout("p_swa_k", [NB, 128, 256])
    p_swa_v = dout("p_swa_v", [NB, 128, 256])
    s_gdn = dout("s_gdn", [NS, 4, 128, 128])
    s_gdn_conv = dout("s_gdn_conv", [NS, 3, 1536])
    s_lru = dout("s_lru", [NS, 512])
    s_lru_conv = dout("s_lru_conv", [NS, 3, 512])
    s_swa_k = dout("s_swa_k", [NS, 128, 256])
    s_swa_v = dout("s_swa_v", [NS, 128, 256])

    xa_p = dscr("xa_p", [NB, SEQ, D])
    xb_p = dscr("xb_p", [NB, SEQ, D])
    xa_s = dscr("xa_s", [NS, D])
    xb_s = dscr("xb_s", [NS, D])
    sq_q = dscr("sq_q", [NS, 1024])
    sq_k = dscr("sq_k", [NS, 256])
    sq_v = dscr("sq_v", [NS, 256])
    so_s = dscr("so_s", [NS, 1024])
    slope_d = dscr("slope_d", [16])
    sg_qk = dscr("sg_qk", [NS * 4, 256])
    sg_v = dscr("sg_v", [NS, 512])
    sg_bg = dscr("sg_bg", [NS * 4, 2])
    so_g = dscr("so_g", [NS, 512])

    PS = [nc.alloc_psum_tensor(f"ps{i}", [128, 512], F32) for i in range(8)]

    def psk(i):
        return f"ps{i}"

    identf = A.alloc("identf", [128, 128], F32)
    identb = A.alloc("identb", [128, 128], BF16)
    ones_f = A.alloc("ones_f", [128, 128], F32)
    P.op("pool", lambda e: e.memset(ones_f[:], 1.0), writes=["ones_f"])
    P.op("pool", lambda e: e.affine_select(out=identf[:], in_=ones_f[:], pattern=[[-1, 128]], compare_op=ALU.is_equal,
                                           fill=0.0, base=0, channel_multiplier=1), reads=["ones_f"], writes=["identf"])
    P.op("dve", lambda e: e.tensor_copy(out=identb[:], in_=identf[:]), reads=["identf"], writes=["identb"])

    ctx = dict(nc=nc, P=P, A=A, PS=PS)

    def load_weight(dst, dst_key, src_rows, ncols, kchunks, stage, scale_tile=None, scale_key=None, col0=0):
        piece = int(stage[0].shape[1])
        i = 0
        for kc in range(kchunks):
            for c0 in range(0, ncols, piece):
                c1 = min(ncols, c0 + piece)
                st = stage[i % 2]
                sk = f"wstage{i % 2}"
                i += 1
                src = src_rows(kc)[:, col0 + c0:col0 + c1]
                P.dma("sp", st[:, 0:c1 - c0], src, writes=[sk])
                eng = ("dve", "pool", "act")[i % 3] if scale_tile is None else ("dve", "pool")[i % 2]
                o = dst[:, kc, c0:c1]
                s_in = st[:, 0:c1 - c0]
                if scale_tile is None:
                    if eng == "act":
                        P.op("act", lambda e, o=o, s_in=s_in: e.copy(out=o, in_=s_in), reads=[sk], writes=[dst_key])
                    else:
                        P.op(eng, lambda e, o=o, s_in=s_in: e.tensor_copy(out=o, in_=s_in), reads=[sk], writes=[dst_key])
                else:
                    sc = scale_tile[:, kc:kc + 1]
                    P.op(eng, lambda e, o=o, s_in=s_in, sc=sc: e.tensor_scalar(out=o, in0=s_in, scalar1=sc, scalar2=None, op0=ALU.mult),
                         reads=[sk, scale_key], writes=[dst_key])

    def wload(dst_ap, src_ap, key):
        P.dma("pool", dst_ap, src_ap, writes=[key])

    def load_colvec(dst, dst_key, src_1d, nchunk):
        P.dma("sp", dst[:, 0:nchunk], src_1d.rearrange("(c p) -> p c", p=128), writes=[dst_key], allow_slow_non_contiguous=True)

    def rms_rstd(xt_ap, rows, junk, ss, rstd, xkey, tag):
        P.op("act", lambda e: e.activation(out=junk[:rows, :], in_=xt_ap, func=AF.Square, accum_out=ss[:rows, :]),
             reads=[xkey], writes=[f"junk{tag}", f"ss{tag}"])
        P.op("act", lambda e: e.activation(out=ss[:rows, :], in_=ss[:rows, :], func=AF.Sqrt, scale=1.0 / D, bias=EPS),
             reads=[f"ss{tag}"], writes=[f"ss{tag}"])
        P.op("dve", lambda e: e.reciprocal(out=rstd[:rows, :], in_=ss[:rows, :]), reads=[f"ss{tag}"], writes=[f"rstd{tag}"])

    def norm_transpose(xt_ap, rows, xkey, xn, junk, ss, rstd, xnT, xnT_key, col0, tag, wrow, banks=(0, 1)):
        rms_rstd(xt_ap, rows, junk, ss, rstd, xkey, tag)
        P.op("dve", lambda e: e.scalar_tensor_tensor(out=xn[:rows, :], in0=xt_ap, scalar=rstd[:rows, :], in1=wrow[:rows, :], op0=ALU.mult, op1=ALU.mult),
             reads=[xkey, f"rstd{tag}", "wrow"], writes=[f"xn{tag}"])
        for half in range(2):
            b = banks[half]
            pb = PS[b][:].bitcast(BF16)
            for j in range(4):
                kc = half * 4 + j
                P.op("pe", lambda e, kc=kc, j=j, pb=pb: e.transpose(out=pb[:, j * 128:j * 128 + rows], in_=xn[:rows, kc * 128:(kc + 1) * 128],
                                                                   identity=identb[:rows, :rows]),
                     reads=[f"xn{tag}", "identb"], writes=[psk(b)])
            src = pb[:, 0:512].rearrange("p (j t) -> p j t", j=4)[:, :, 0:rows]
            dst = xnT[:, half * 4:half * 4 + 4, col0:col0 + rows]
            if half == 0:
                P.op("act", lambda e, src=src, dst=dst: e.copy(out=dst, in_=src), reads=[psk(b)], writes=[xnT_key])
            else:
                P.op("dve", lambda e, src=src, dst=dst: e.tensor_copy(out=dst, in_=src), reads=[psk(b)], writes=[xnT_key])

    def ffn_stage(li, src_p, src_s, dst_p, dst_s, final):
        m0 = A.mark()
        wgu = A.alloc("wgu", [128, 8, 2 * DFF], BF16)
        wdn = A.alloc("wdn", [128, NFC, D], BF16)
        wrow = A.alloc("wrow", [128, D], F32)
        xt = A.alloc("xt", [128, 4, D], F32)
        xn = A.alloc("xn", [128, D], BF16)
        junk = A.alloc("junk", [128, D], BF16)
        ss = A.alloc("ss", [128, 1], F32)
        rstd = A.alloc("rstd", [128, 1], F32)
        xnT = A.alloc("xnT", [128, 8, T], BF16)
        actT = A.alloc("actT", [128, NFC, T], BF16)
        sg = [A.alloc(f"sg{i}", [128, T], F32) for i in range(2)]
        wfin = A.alloc("wfin", [128, D], F32) if final else None
        P.dma("sp", wrow[:], norm_ffn[li].partition_broadcast(128), writes=["wrow"])
        for fc in range(NFC):
            wload(wgu[:, :, fc * 128:(fc + 1) * 128], w_gate_up[li][:, fc * 128:(fc + 1) * 128].rearrange("(kc p) c -> p kc c", p=128), f"wg{fc}")
            wload(wgu[:, :, DFF + fc * 128:DFF + (fc + 1) * 128], w_gate_up[li][:, DFF + fc * 128:DFF + (fc + 1) * 128].rearrange("(kc p) c -> p kc c", p=128), f"wu{fc}")
        for fc in range(NFC):
            wload(wdn[:, fc, :], w_down[li][fc * 128:(fc + 1) * 128, :], f"wd{fc}")
        if final:
            P.dma("sp", wfin[:], norm_final.partition_broadcast(128), writes=["wfin"])

        def load_sub(src_ap, rows, s):
            P.dma("sp", xt[:rows, s, :], src_ap[s * rows:(s + 1) * rows, :], writes=[f"xt{s}"])

        def do_tile(src_ap, dst_ap, rows, nsub, next_src=None, next_rows=None, next_nsub=0, preloaded=False):
            ntok = rows * nsub
            if not preloaded:
                for s in range(nsub):
                    load_sub(src_ap, rows, s)
            xkeys = [f"xnT{s}" for s in range(nsub)]
            for s in range(nsub):
                norm_transpose(xt[:rows, s, :], rows, f"xt{s}", xn, junk, ss, rstd, xnT, f"xnT{s}", s * rows, "f", wrow)
            for fc in range(NFC):
                bg = 2 + (fc % 2)
                bu = 4 + (fc % 2)
                for kc in range(8):
                    P.op("pe", lambda e, fc=fc, kc=kc, bg=bg: e.matmul(PS[bg][:, 0:ntok], lhsT=wgu[:, kc, fc * 128:(fc + 1) * 128],
                                                                      rhs=xnT[:, kc, 0:ntok], start=(kc == 0), stop=(kc == 7)),
                         reads=[f"wg{fc}"] + xkeys, writes=[psk(bg)])
                for kc in range(8):
                    P.op("pe", lambda e, fc=fc, kc=kc, bu=bu: e.matmul(PS[bu][:, 0:ntok], lhsT=wgu[:, kc, DFF + fc * 128:DFF + (fc + 1) * 128],
                                                                      rhs=xnT[:, kc, 0:ntok], start=(kc == 0), stop=(kc == 7)),
                         reads=[f"wu{fc}"] + xkeys, writes=[psk(bu)])
                sgt = sg[fc % 2]
                sgk = f"sg{fc % 2}"
                P.op("act", lambda e, bg=bg, sgt=sgt: e.activation(out=sgt[:, 0:ntok], in_=PS[bg][:, 0:ntok], func=AF.Silu),
                     reads=[psk(bg)], writes=[sgk])
                P.op("dve", lambda e, bu=bu, sgt=sgt, fc=fc: e.tensor_tensor(out=actT[:, fc, 0:ntok], in0=sgt[:, 0:ntok], in1=PS[bu][:, 0:ntok], op=ALU.mult),
                     reads=[sgk, psk(bu)], writes=["actT"])
            i = 0
            for s in range(nsub):
                xk = f"xt{s}"
                for half in range(2):
                    bd = 6 + (i % 2)
                    i += 1
                    for fc in range(NFC):
                        P.op("pe", lambda e, fc=fc, s=s, half=half, bd=bd: e.matmul(PS[bd][:rows, :], lhsT=actT[:, fc, s * rows:(s + 1) * rows],
                                                                                  rhs=wdn[:, fc, half * 512:(half + 1) * 512],
                                                                                  start=(fc == 0), stop=(fc == NFC - 1)),
                             reads=["actT", f"wd{fc}"], writes=[psk(bd)])
                    P.op("dve", lambda e, s=s, half=half, bd=bd: e.tensor_tensor(out=xt[:rows, s, half * 512:(half + 1) * 512],
                                                                                in0=xt[:rows, s, half * 512:(half + 1) * 512],
                                                                                in1=PS[bd][:rows, :], op=ALU.add),
                         reads=[psk(bd), xk], writes=[xk])
                if final:
                    rms_rstd(xt[:rows, s, :], rows, junk, ss, rstd, xk, "f")
                    P.op("dve", lambda e, s=s: e.scalar_tensor_tensor(out=xt[:rows, s, :], in0=xt[:rows, s, :], scalar=rstd[:rows, :],
                                                                     in1=wfin[:rows, :], op0=ALU.mult, op1=ALU.mult),
                         reads=[xk, "rstdf", "wfin"], writes=[xk])
                P.dma("sp", dst_ap[s * rows:(s + 1) * rows, :], xt[:rows, s, :], reads=[xk])
                if next_src is not None and s < next_nsub:
                    load_sub(next_src, next_rows, s)

        tiles = [(src_p[b, st * T:(st + 1) * T, :], dst_p[b, st * T:(st + 1) * T, :], 128, 4) for b in range(NB) for st in range(SEQ // T)]
        tiles.append((src_s, dst_s, NS, 1))
        for ti, (sa, da, rows_, nsub_) in enumerate(tiles):
            nxt = tiles[ti + 1] if ti + 1 < len(tiles) else None
            do_tile(sa, da, rows_, nsub_, next_src=(nxt[0] if nxt else None), next_rows=(nxt[2] if nxt else None),
                    next_nsub=(nxt[3] if nxt else 0), preloaded=(ti > 0))
        P.barrier()
        A.release(m0)


    def mix0_stage(src_p, src_s, dst_p, dst_s, do_samples=True, do_prompt=True):
        m0 = A.mark()
        win = A.alloc("win", [128, 8, W_IN], BF16)
        wout = A.alloc("wout", [128, 8, D], BF16)
        wrow = A.alloc("wrow", [128, D], F32)
        cwg = A.alloc("cwg", [128, 12, 4], F32)
        cwl = A.alloc("cwl", [128, 4, 4], F32)
        cbl = A.alloc("cbl", [128, 4], F32)
        bab = A.alloc("bab", [128, 4], F32)
        bxb = A.alloc("bxb", [128, 4], F32)
        lam = A.alloc("lam", [128, 4], F32)
        cl = A.alloc("cl", [128, 4], F32)
        cl2 = A.alloc("cl2", [128, 4], F32)
        WAbd = A.alloc("WAbd", [128, 4, 128], F32)
        WXbd = A.alloc("WXbd", [128, 4, 128], F32)
        gnw = A.alloc("gnw", [128, 1], F32)
        alog = A.alloc("alog", [4, 1], F32)
        dtb = A.alloc("dtb", [4, 1], F32)
        nega = A.alloc("nega", [4, 1], F32)
        TriU = A.alloc("TriU", [128, 128], F32)
        MASKB = A.alloc("MASKB", [128, 128], F32)
        MSTR = A.alloc("MSTR", [128, 128], F32)
        sel4 = A.alloc("sel4", [4, 4, 128], F32)
        BDm = A.alloc("BDm", [128, 128], F32)
        NBD = A.alloc("NBD", [128, 128], F32)
        xn = A.alloc("xn", [128, D], BF16)
        junk = A.alloc("junk", [128, D], BF16)
        ss = A.alloc("ss", [128, 1], F32)
        rstd = A.alloc("rstd", [128, 1], F32)
        m1 = A.mark()
        xt = A.alloc("xt", [128, 4, D], F32)
        xnT = A.alloc("xnT", [128, 8, T], BF16)
        mixT = A.alloc("mixT", [128, 8, T], BF16)
        Hg = A.alloc("Hg", [128, 12, 3], F32)
        Hl = A.alloc("Hl", [128, 4, 3], F32)
        ones_r = A.alloc("ones_r", [128, 128], F32)
        Sst = A.alloc("Sst", [128, 4, 128], F32)
        hst = A.alloc("hst", [128, 4], F32)
        wbufs = {}
        SMALL = {"bt": 16, "gt": 16, "gam": 16, "ngam": 16, "egam": 16, "cfk": 16, "u0": 128, "u1": 128, "Sr": 128}

        def wb(name, dtype=F32, n=None):
            if name not in wbufs:
                wbufs[name] = A.alloc(name, [128, n or SMALL.get(name, 512)], dtype)
            return wbufs[name]

        def v3(t):
            return t[:].rearrange("p (s i) -> p s i", s=4)

        P.dma("sp", wrow[:], norm_mix[0].partition_broadcast(128), writes=["wrow"])

        def winkey(col0):
            return "win2048" if 2048 <= col0 < 2056 else f"win{col0}"
        WIN_ALL = []

        def win_unit(col0, n=128):
            wload(win[:, :, col0:col0 + n], w_in_ab[:, col0:col0 + n].rearrange("(kc p) c -> p kc c", p=128), winkey(col0))
            WIN_ALL.append(winkey(col0))
        for c in range(4):
            win_unit(2056 + c * 128)
            win_unit(2568 + c * 128)
        win_unit(2048, 8)
        for h in range(4):
            for base in (0, 512, 1024, 1536):
                win_unit(base + h * 128)
        for c in range(8):
            wload(wout[:, c, :], w_out_ab[c * 128:(c + 1) * 128, :], f"wo{c}")
        for c in range(12):
            P.dma("sp", cwg[:, c, :], conv_gdn_w[:, c * 128:(c + 1) * 128].rearrange("j p -> p j"), writes=["cwg"], allow_slow_non_contiguous=True)
        for c in range(4):
            P.dma("sp", cwl[:, c, :], conv_lru_w[:, c * 128:(c + 1) * 128].rearrange("j p -> p j"), writes=["cwl"], allow_slow_non_contiguous=True)
        load_colvec(cbl, "cbl", conv_lru_b, 4)
        load_colvec(bab, "bab", lru_ba, 4)
        load_colvec(bxb, "bxb", lru_bx, 4)
        load_colvec(lam, "lam", lru_lambda, 4)
        load_colvec(gnw, "gnw", gdn_norm_w, 1)
        P.dma("sp", alog[:], gdn_a_log.rearrange("(p o) -> p o", o=1), writes=["alog"])
        P.dma("sp", dtb[:], gdn_dt_bias.rearrange("(p o) -> p o", o=1), writes=["dtb"])
        P.op("act", lambda e: e.activation(out=nega[:], in_=alog[:], func=AF.Exp), reads=["alog"], writes=["nega"])
        P.op("dve", lambda e: e.tensor_scalar(out=nega[:], in0=nega[:], scalar1=-1.0, scalar2=None, op0=ALU.mult), reads=["nega"], writes=["nega"])
        P.op("act", lambda e: e.activation(out=cl[:], in_=lam[:], func=AF.Exp, scale=-1.0), reads=["lam"], writes=["cl"])
        P.op("act", lambda e: e.activation(out=cl[:], in_=cl[:], func=AF.Ln, bias=1.0), reads=["cl"], writes=["cl"])
        P.op("dve", lambda e: e.tensor_scalar(out=cl2[:], in0=cl[:], scalar1=-16.0, scalar2=None, op0=ALU.mult), reads=["cl"], writes=["cl2"])
        P.op("dve", lambda e: e.tensor_scalar(out=cl[:], in0=cl[:], scalar1=-8.0, scalar2=None, op0=ALU.mult), reads=["cl", "cl2"], writes=["cl"])
        P.op("pool", lambda e: e.memset(WAbd[:], 0.0), writes=["WAbd"])
        P.op("pool", lambda e: e.memset(WXbd[:], 0.0), writes=["WXbd"])
        for n in range(8):
            c, o = n // 2, 64 * (n % 2)
            P.dma("sp", WAbd[o:o + 64, c, o:o + 64], lru_wa[n], writes=["WAbd"])
            P.dma("sp", WXbd[o:o + 64, c, o:o + 64], lru_wx[n], writes=["WXbd"])
        P.op("pool", lambda e: e.affine_select(out=TriU[:], in_=ones_f[:], pattern=[[1, 128]], compare_op=ALU.is_ge, fill=0.0, base=0,
                                               channel_multiplier=-1), reads=["ones_f"], writes=["TriU"])
        P.op("pool", lambda e: e.memset(MASKB[:], 0.0), writes=["MASKB"])
        P.op("pool", lambda e: e.affine_select(out=MASKB[:], in_=MASKB[:], pattern=[[1, 128]], compare_op=ALU.is_ge, fill=NEG, base=0,
                                               channel_multiplier=-1), reads=["MASKB"], writes=["MASKB"])
        P.op("pool", lambda e: e.affine_select(out=MSTR[:], in_=ones_f[:], pattern=[[1, 128]], compare_op=ALU.is_ge, fill=0.0, base=-1,
                                               channel_multiplier=-1), reads=["ones_f"], writes=["MSTR"])
        P.op("dve", lambda e: e.tensor_copy(out=sel4[:], in_=identf[0:4, 0:4].unsqueeze(2).to_broadcast([4, 4, 128])), reads=["identf"], writes=["sel4"])
        P.op("dve", lambda e: e.tensor_copy(out=ones_r[:].bitcast(mybir.dt.float32r), in_=ones_f[:]), reads=["ones_f"], writes=["ones_r"])
        P.op("pool", lambda e: e.memset(BDm[:], 0.0), writes=["BDm"])
        P.op("pool", lambda e: e.memset(BDm[0:64, 0:64], 1.0), reads=["BDm"], writes=["BDm"])
        P.op("pool", lambda e: e.memset(BDm[64:128, 64:128], 1.0), reads=["BDm"], writes=["BDm"])
        P.op("dve", lambda e: e.tensor_scalar(out=NBD[:], in0=BDm[:], scalar1=-1.0, scalar2=1.0, op0=ALU.mult, op1=ALU.add), reads=["BDm"], writes=["NBD"])

        def proj_fm(col0, bank, m=128):
            for kc in range(8):
                P.op("pe", lambda e, kc=kc: e.matmul(PS[bank][:m, 0:T], lhsT=win[:, kc, col0:col0 + m], rhs=xnT[:, kc, :], start=(kc == 0), stop=(kc == 7)),
                     reads=[winkey(col0), "xnT"], writes=[psk(bank)])

        def conv4(dst, dkey, hist, hkey, wts, wkey, bias=None):
            if bias is None:
                P.op("dve", lambda e: e.tensor_scalar(out=dst, in0=hist[:, 0:T], scalar1=wts[:, 0:1], scalar2=None, op0=ALU.mult),
                     reads=[hkey, wkey], writes=[dkey])
            else:
                P.op("dve", lambda e: e.tensor_scalar(out=dst, in0=hist[:, 0:T], scalar1=wts[:, 0:1], scalar2=bias, op0=ALU.mult, op1=ALU.add),
                     reads=[hkey, wkey, "cbl"], writes=[dkey])
            for j in range(1, 4):
                P.op("dve", lambda e, j=j: e.scalar_tensor_tensor(out=dst, in0=hist[:, j:j + T], scalar=wts[:, j:j + 1], in1=dst, op0=ALU.mult, op1=ALU.add),
                     reads=[hkey, wkey, dkey], writes=[dkey])

        def mm4(bank, lhs_fn, rhs_fn, reads):
            for s_ in range(4):
                P.op("pe", lambda e, s_=s_: e.matmul(PS[bank][:, s_ * 128:(s_ + 1) * 128], lhsT=lhs_fn(s_), rhs=rhs_fn(s_), start=True, stop=True),
                     reads=reads, writes=[psk(bank)])

        def ps3(bank):
            return PS[bank][:].rearrange("p (s i) -> p s i", s=4)

        F32R = mybir.dt.float32r

        def Rr(ap):
            return ap.bitcast(F32R)

        def hist_in(ptb, pk, H, hkey, ch, bank):
            P.op("act", lambda e: e.copy(out=ptb[:, 3:515], in_=PS[bank][:, :]), reads=[psk(bank)], writes=[pk])
            P.op("dve", lambda e: e.tensor_copy(out=ptb[:, 0:3], in_=H[:, ch, :]), reads=[hkey, pk], writes=[pk])
            P.op("dve", lambda e: e.tensor_copy(out=H[:, ch, :], in_=ptb[:, 512:515]), reads=[pk, hkey], writes=[hkey])

        def rsqrt_ps(dst, dkey, bank, scale):
            P.op("act", lambda e: e.activation(out=dst[:], in_=PS[bank][:, :], func=AF.Ln, scale=scale, bias=EPS), reads=[psk(bank)], writes=[dkey])
            P.op("act", lambda e: e.activation(out=dst[:], in_=dst[:], func=AF.Exp, scale=-0.5), reads=[dkey], writes=[dkey])

        def lru_group(c):
            gel, xr, rg, ig, av, a2, bv, hs = (wb(n) for n in ("l0", "l1", "l2", "l3", "l4", "l5", "l6", "l7"))
            ptb, pk = wb("pt2", n=515), "pt2"
            proj_fm(2056 + c * 128, 2)
            P.op("act", lambda e: e.activation(out=gel[:], in_=PS[2][:, :], func=AF.Gelu_apprx_tanh), reads=[psk(2)], writes=["l0"])
            yield
            proj_fm(2568 + c * 128, 3)
            hist_in(ptb, pk, Hl, "Hl", c, 3)
            conv4(xr[:], "l1", ptb, pk, cwl[:, c, :], "cwl", bias=cbl[:, c:c + 1])
            yield
            P.op("pe", lambda e: e.matmul(PS[2][:, :], lhsT=WAbd[:, c, :], rhs=xr[:], start=True, stop=True), reads=["WAbd", "l1"], writes=[psk(2)])
            P.op("act", lambda e: e.activation(out=rg[:], in_=PS[2][:, :], func=AF.Sigmoid, bias=bab[:, c:c + 1]), reads=[psk(2), "bab"], writes=["l2"])
            P.op("pe", lambda e: e.matmul(PS[3][:, :], lhsT=WXbd[:, c, :], rhs=xr[:], start=True, stop=True), reads=["WXbd", "l1"], writes=[psk(3)])
            P.op("act", lambda e: e.activation(out=ig[:], in_=PS[3][:, :], func=AF.Sigmoid, bias=bxb[:, c:c + 1]), reads=[psk(3), "bxb"], writes=["l3"])
            yield
            P.op("act", lambda e: e.activation(out=av[:], in_=rg[:], func=AF.Exp, scale=cl[:, c:c + 1]), reads=["l2", "cl"], writes=["l4"])
            P.op("act", lambda e: e.activation(out=a2[:], in_=rg[:], func=AF.Exp, scale=cl2[:, c:c + 1]), reads=["l2", "cl2"], writes=["l5"])
            P.op("dve", lambda e: e.tensor_scalar(out=a2[:], in0=a2[:], scalar1=1.0, scalar2=-1.0, op0=ALU.min, op1=ALU.mult), reads=["l5"], writes=["l5"])
            P.op("act", lambda e: e.activation(out=a2[:], in_=a2[:], func=AF.Sqrt, bias=1.0), reads=["l5"], writes=["l5"])
            P.op("dve", lambda e: e.tensor_tensor(out=bv[:], in0=ig[:], in1=xr[:], op=ALU.mult), reads=["l3", "l1"], writes=["l6"])
            yield
            P.op("dve", lambda e: e.tensor_tensor(out=bv[:], in0=bv[:], in1=a2[:], op=ALU.mult), reads=["l6", "l5"], writes=["l6"])
            P.op("dve", lambda e: e.tensor_tensor_scan(out=hs[:], data0=av[:], data1=bv[:], initial=hst[:, c:c + 1], op0=ALU.mult, op1=ALU.add),
                 reads=["l4", "l6", "hst"], writes=["l7"])
            P.op("dve", lambda e: e.tensor_copy(out=hst[:, c:c + 1], in_=hs[:, T - 1:T]), reads=["l7"], writes=["hst"])
            P.op("dve", lambda e: e.tensor_tensor(out=mixT[:, 4 + c, :], in0=gel[:], in1=hs[:], op=ALU.mult), reads=["l0", "l7"], writes=["mixT"])
            yield

        def gdn_common():
            beta_f, g_f = wb("bf"), wb("gf")
            bt, gt, gam, ngam, egam, cfk = (wb(n) for n in ("bt", "gt", "gam", "ngam", "egam", "cfk"))
            proj_fm(2048, 0, m=4)
            P.op("act", lambda e: e.activation(out=beta_f[0:4, :], in_=PS[0][0:4, :], func=AF.Sigmoid), reads=[psk(0)], writes=["bf"])
            proj_fm(2052, 1, m=4)
            P.op("act", lambda e: e.activation(out=g_f[0:4, :], in_=PS[1][0:4, :], func=AF.Exp, bias=dtb[:, 0:1]), reads=[psk(1), "dtb"], writes=["gf"])
            P.op("act", lambda e: e.activation(out=g_f[0:4, :], in_=g_f[0:4, :], func=AF.Ln, bias=1.0), reads=["gf"], writes=["gf"])
            P.op("dve", lambda e: e.tensor_scalar(out=g_f[0:4, :], in0=g_f[0:4, :], scalar1=nega[:, 0:1], scalar2=None, op0=ALU.mult),
                 reads=["gf", "nega"], writes=["gf"])
            for s_ in range(4):
                P.op("pe", lambda e, s_=s_: e.transpose(out=PS[0][:, s_ * 4:(s_ + 1) * 4], in_=beta_f[0:4, s_ * 128:(s_ + 1) * 128], identity=identf[0:4, 0:4]),
                     reads=["bf", "identf"], writes=[psk(0)])
                P.op("pe", lambda e, s_=s_: e.transpose(out=PS[0][:, 16 + s_ * 4:16 + (s_ + 1) * 4], in_=g_f[0:4, s_ * 128:(s_ + 1) * 128], identity=identf[0:4, 0:4]),
                     reads=["gf", "identf"], writes=[psk(0)])
            hs_view = lambda t: t[:, 0:16].rearrange("p (h s) -> p h s", h=4)
            P.op("dve", lambda e: e.tensor_copy(out=hs_view(bt), in_=PS[0][:, 0:16].rearrange("p (s h) -> p h s", s=4)), reads=[psk(0)], writes=["bt"])
            P.op("dve", lambda e: e.tensor_copy(out=hs_view(gt), in_=PS[0][:, 16:32].rearrange("p (s h) -> p h s", s=4)), reads=[psk(0)], writes=["gt"])
            for s_ in range(4):
                P.op("pe", lambda e, s_=s_: e.matmul(PS[1][:, s_ * 4:(s_ + 1) * 4], lhsT=TriU[:], rhs=hs_view(gt)[:, :, s_], start=True, stop=True),
                     reads=["TriU", "gt"], writes=[psk(1)])
            P.op("dve", lambda e: e.tensor_copy(out=hs_view(gam), in_=PS[1][:, 0:16].rearrange("p (s h) -> p h s", s=4)), reads=[psk(1)], writes=["gam"])
            P.op("dve", lambda e: e.tensor_scalar(out=ngam[:, 0:16], in0=gam[:, 0:16], scalar1=-1.0, scalar2=None, op0=ALU.mult), reads=["gam"], writes=["ngam"])
            P.op("act", lambda e: e.activation(out=egam[:, 0:16], in_=gam[:, 0:16], func=AF.Exp), reads=["gam"], writes=["egam"])
            P.op("dve", lambda e: e.tensor_tensor(out=cfk[:, 0:16], in0=egam[:, 0:16], in1=bt[:, 0:16], op=ALU.mult), reads=["egam", "bt"], writes=["cfk"])

        def gdn_A(h):
            beta_f = wb("bf")
            bt, gt, gam, ngam, egam, cfk = (wb(n) for n in ("bt", "gt", "gam", "ngam", "egam", "cfk"))
            qc, kc, vc, sq, rq = (wb(n) for n in ("w0", "w1", "w2", "w4", "w5"))
            zk = f"z{h % 2}"
            zs = wb(zk)
            sq2, rq2 = wb("n4"), wb("n5")
            GD = F32
            qn, kn, kbT = wb("q0", GD), wb("k0", GD), wb("kb0", GD)
            Gb, tmpGB, DT, DTs, EGB = (wb(n) for n in ("w7", "g0", "g1", "g2", "g3"))
            kb_tok, kt_tok, vb_tok, QKT, qdT, wkT = (wb(n, GD) for n in ("g4", "g5", "g6", "g7", "g8", "g10"))
            wv_ = wb("g9")
            vcb = wb("vcb", GD) if GD == BF16 else None
            Pb = [wb("pa", GD), wb("pb", GD)]
            Qb = [wb("qa", GD), wb("qb", GD)]
            Rb = [wb("ra", GD), wb("rb", GD)]
            Pk, Qk, Rk = ["pa", "pb"], ["qa", "qb"], ["ra", "rb"]
            ub = [wb("u0", GD), wb("u1", GD)]
            Sr = wb("Sr", GD)
            hsl = lambda t, s_=None: (t[:, h * 4:(h + 1) * 4] if s_ is None else t[:, h * 4 + s_:h * 4 + s_ + 1])
            cs = lambda t, s_: t[:, s_ * 128:(s_ + 1) * 128]
            if GD == BF16:
                csr = lambda t, s_: t[:, s_ * 128:(s_ + 1) * 128]
                Rr = lambda ap: ap
                psb = lambda bank: PS[bank][:].bitcast(BF16)
                psb3 = lambda bank: PS[bank][:].bitcast(BF16)[:, 0:512].rearrange("p (s i) -> p s i", s=4)
                identx, identk = identb, "identb"
            else:
                csr = lambda t, s_: t[:, s_ * 128:(s_ + 1) * 128].bitcast(F32R)
                Rr = lambda ap: ap.bitcast(F32R)
                psb = lambda bank: PS[bank][:]
                psb3 = lambda bank: PS[bank][:].rearrange("p (s i) -> p s i", s=4)
                identx, identk = identf, "identf"

            def warm():
                for _ in range(DBG.get("warm", 0)):
                    P.op("pe", lambda e: e.matmul(PS[0][:, :], lhsT=win[:, 0, 0:128], rhs=xnT[:, 0, :], start=True, stop=True), reads=["win0", "xnT"], writes=[psk(0)])
            for idx, (ch, dst, dk) in enumerate([(h, qc, "w0"), (4 + h, kc, "w1"), (8 + h, vc, "w2")]):
                bank = 2 + idx % 2
                ptb, pk = wb(f"pt{idx % 2}", n=515), f"pt{idx % 2}"
                proj_fm(ch * 128, bank)
                hist_in(ptb, pk, Hg, "Hg", ch, bank)
                conv4(dst[:], dk, ptb, pk, cwg[:, ch, :], "cwg")
                P.op("act", lambda e, dst=dst: e.activation(out=dst[:], in_=dst[:], func=AF.Silu), reads=[dk], writes=[dk])
                yield
            proj_fm(1536 + h * 128, 3)
            P.op("act", lambda e: e.activation(out=zs[:], in_=PS[3][:, :], func=AF.Silu), reads=[psk(3)], writes=[zk])
            for x, xk, xo, xok, scl, bank in ((qc, "w0", qn, "q0", 128.0 ** -0.5, 2), (kc, "w1", kn, "k0", 1.0, 3)):
                P.op("dve", lambda e, x=x: e.tensor_tensor(out=sq[:].bitcast(F32R), in0=x[:], in1=x[:], op=ALU.mult), reads=[xk], writes=["w4"])
                P.op("pe", lambda e, bank=bank: e.matmul(PS[bank][:, :], lhsT=ones_r[:].bitcast(F32R), rhs=sq[:].bitcast(F32R), start=True, stop=True), reads=["ones_r", "w4"], writes=[psk(bank)])
                rsqrt_ps(rq, "w5", bank, 1.0)
                P.op("dve", lambda e, x=x, xo=xo, scl=scl: e.scalar_tensor_tensor(out=Rr(xo[:]), in0=x[:], scalar=scl, in1=rq[:], op0=ALU.mult, op1=ALU.mult),
                     reads=[xk, "w5"], writes=[xok])
                yield

        def gdn_BCD(h):
            beta_f = wb("bf")
            bt, gt, gam, ngam, egam, cfk = (wb(n) for n in ("bt", "gt", "gam", "ngam", "egam", "cfk"))
            qc, kc, vc, sq, rq = (wb(n) for n in ("w0", "w1", "w2", "w4", "w5"))
            zk = f"z{h % 2}"
            zs = wb(zk)
            sq2, rq2 = wb("n4"), wb("n5")
            GD = F32
            qn, kn, kbT = wb("q0", GD), wb("k0", GD), wb("kb0", GD)
            Gb, tmpGB, DT, DTs, EGB = (wb(n) for n in ("w7", "g0", "g1", "g2", "g3"))
            kb_tok, kt_tok, vb_tok, QKT, qdT, wkT = (wb(n, GD) for n in ("g4", "g5", "g6", "g7", "g8", "g10"))
            wv_ = wb("g9")
            vcb = wb("vcb", GD) if GD == BF16 else None
            Pb = [wb("pa", GD), wb("pb", GD)]
            Qb = [wb("qa", GD), wb("qb", GD)]
            Rb = [wb("ra", GD), wb("rb", GD)]
            Pk, Qk, Rk = ["pa", "pb"], ["qa", "qb"], ["ra", "rb"]
            ub = [wb("u0", GD), wb("u1", GD)]
            Sr = wb("Sr", GD)
            hsl = lambda t, s_=None: (t[:, h * 4:(h + 1) * 4] if s_ is None else t[:, h * 4 + s_:h * 4 + s_ + 1])
            cs = lambda t, s_: t[:, s_ * 128:(s_ + 1) * 128]
            if GD == BF16:
                csr = lambda t, s_: t[:, s_ * 128:(s_ + 1) * 128]
                Rr = lambda ap: ap
                psb = lambda bank: PS[bank][:].bitcast(BF16)
                psb3 = lambda bank: PS[bank][:].bitcast(BF16)[:, 0:512].rearrange("p (s i) -> p s i", s=4)
                identx, identk = identb, "identb"
            else:
                csr = lambda t, s_: t[:, s_ * 128:(s_ + 1) * 128].bitcast(F32R)
                Rr = lambda ap: ap.bitcast(F32R)
                psb = lambda bank: PS[bank][:]
                psb3 = lambda bank: PS[bank][:].rearrange("p (s i) -> p s i", s=4)
                identx, identk = identf, "identf"

            def warm():
                for _ in range(DBG.get("warm", 0)):
                    P.op("pe", lambda e: e.matmul(PS[0][:, :], lhsT=win[:, 0, 0:128], rhs=xnT[:, 0, :], start=True, stop=True), reads=["win0", "xnT"], writes=[psk(0)])
            P.op("pe", lambda e: e.matmul(PS[1][:, :], lhsT=sel4[0:4, h, :], rhs=beta_f[0:4, :], start=True, stop=True), reads=["sel4", "bf"], writes=[psk(1)])
            P.op("dve", lambda e: e.tensor_tensor(out=Rr(kbT[:]), in0=kn[:], in1=PS[1][:, :], op=ALU.mult), reads=["k0", psk(1)], writes=["kb0"])
            P.op("dve", lambda e: e.tensor_tensor(out=v3(Gb), in0=ones_f[:].unsqueeze(1).to_broadcast([128, 4, 128]), in1=hsl(gam).unsqueeze(2).to_broadcast([128, 4, 128]), op=ALU.mult),
                 reads=["ones_f", "gam"], writes=["w7"])
            for s_ in range(4):
                P.op("pe", lambda e, s_=s_: e.transpose(out=PS[4][:, s_ * 128:(s_ + 1) * 128], in_=cs(Gb, s_), identity=identf[:]), reads=["w7", "identf"], writes=[psk(4)])
            P.op("dve", lambda e: e.tensor_tensor(out=v3(tmpGB), in0=ps3(4), in1=MASKB[:].unsqueeze(1).to_broadcast([128, 4, 128]), op=ALU.add),
                 reads=[psk(4), "MASKB"], writes=["g0"])
            P.op("act", lambda e: e.activation(out=EGB[:], in_=PS[4][:, :], func=AF.Exp), reads=[psk(4)], writes=["g3"])
            for s_ in range(4):
                P.op("act", lambda e, s_=s_: e.activation(out=cs(DT, s_), in_=cs(tmpGB, s_), func=AF.Exp, bias=hsl(ngam, s_)), reads=["g0", "ngam"], writes=["g1"])
            P.op("dve", lambda e: e.tensor_tensor(out=v3(DTs), in0=v3(DT), in1=MSTR[:].unsqueeze(1).to_broadcast([128, 4, 128]), op=ALU.mult),
                 reads=["g1", "MSTR"], writes=["g2"])
            yield
            mm_t = lambda bank, src, skey: [P.op("pe", lambda e, s_=s_: e.transpose(out=psb(bank)[:, s_ * 128:(s_ + 1) * 128], in_=cs(src, s_), identity=identx[:]),
                                                 reads=[skey, identk], writes=[psk(bank)]) for s_ in range(4)]
            mm_t(5, kn, "k0")
            P.op("dve", lambda e: e.tensor_tensor(out=Rr(v3(kb_tok)), in0=psb3(5), in1=hsl(cfk).unsqueeze(2).to_broadcast([128, 4, 128]), op=ALU.mult),
                 reads=[psk(5), "cfk"], writes=["g4"])
            P.op("dve", lambda e: e.tensor_tensor(out=Rr(v3(kt_tok)), in0=psb3(5), in1=v3(DT)[:, :, 127:128].to_broadcast([128, 4, 128]), op=ALU.mult),
                 reads=[psk(5), "g1"], writes=["g5"])
            if GD == BF16:
                P.op("act", lambda e: e.copy(out=vcb[:], in_=vc[:]), reads=["w2"], writes=["vcb"])
                mm_t(6, vcb, "vcb")
            else:
                mm_t(6, vc, "w2")
            P.op("dve", lambda e: e.tensor_tensor(out=Rr(v3(vb_tok)), in0=psb3(6), in1=hsl(bt).unsqueeze(2).to_broadcast([128, 4, 128]), op=ALU.mult),
                 reads=[psk(6), "bt"], writes=["g6"])
            yield
            mm4(5, lambda s_: csr(kn, s_), lambda s_: csr(kbT, s_), ["k0", "kb0"])
            P.op("dve", lambda e: e.scalar_tensor_tensor(out=wv_[:], in0=PS[5][:, :], scalar=-1.0, in1=DTs[:], op0=ALU.mult, op1=ALU.mult),
                 reads=[psk(5), "g2"], writes=["g9"])
            P.op("dve", lambda e: e.tensor_tensor(out=Rr(v3(Pb[0])), in0=v3(wv_), in1=BDm[:].unsqueeze(1).to_broadcast([128, 4, 128]), op=ALU.mult),
                 reads=["g9", "BDm"], writes=[Pk[0]])
            mm4(6, lambda s_: csr(kn, s_), lambda s_: csr(qn, s_), ["k0", "q0"])
            P.op("dve", lambda e: e.tensor_tensor(out=Rr(QKT[:]), in0=PS[6][:, :], in1=DT[:], op=ALU.mult), reads=[psk(6), "g1"], writes=["g7"])
            P.op("dve", lambda e: e.tensor_tensor(out=Rr(qdT[:]), in0=qn[:], in1=EGB[:], op=ALU.mult), reads=["q0", "g3"], writes=["g8"])
            yield "B_DONE"
            mm_t(4, wv_, "g9")
            P.op("dve", lambda e: e.tensor_tensor(out=Rr(v3(Qb[0])), in0=ps3(4), in1=BDm[:].unsqueeze(1).to_broadcast([128, 4, 128]), op=ALU.mult),
                 reads=[psk(4), "BDm"], writes=[Qk[0]])
            P.op("dve", lambda e: e.tensor_tensor(out=Rr(v3(wkT)), in0=ps3(4), in1=NBD[:].unsqueeze(1).to_broadcast([128, 4, 128]), op=ALU.mult),
                 reads=[psk(4), "NBD"], writes=["g10"])
            P.op("dve", lambda e: e.tensor_tensor(out=Rr(v3(Rb[0])), in0=v3(Pb[0]), in1=identf[:].unsqueeze(1).to_broadcast([128, 4, 128]), op=ALU.add),
                 reads=[Pk[0], "identf"], writes=[Rk[0]])
            cp, cq, cr = 0, 0, 0
            for m in range(1, 6):
                nq = 1 - cq
                mm4(4, lambda s_, cp=cp: csr(Pb[cp], s_), lambda s_, cq=cq: csr(Qb[cq], s_), [Pk[cp], Qk[cq]])
                P.op("act", lambda e, nq=nq: e.copy(out=Rr(Qb[nq][:]), in_=PS[4][:, :]), reads=[psk(4)], writes=[Qk[nq]])
                if m <= 4:
                    np_ = 1 - cp
                    mm4(5, lambda s_, cq=cq: csr(Qb[cq], s_), lambda s_, cp=cp: csr(Pb[cp], s_), [Pk[cp], Qk[cq]])
                    P.op("dve", lambda e, np_=np_: e.tensor_copy(out=Rr(Pb[np_][:]), in_=PS[5][:, :]), reads=[psk(5)], writes=[Pk[np_]])
                    cp = np_
                cq = nq
                warm()
                yield
                nr = 1 - cr
                mm4(6, lambda s_, cq=cq: csr(Qb[cq], s_), lambda s_, cr=cr: csr(Rb[cr], s_), [Qk[cq], Rk[cr]])
                P.op("dve", lambda e, cr=cr, nr=nr: e.tensor_tensor(out=Rr(Rb[nr][:]), in0=Rb[cr][:], in1=PS[6][:, :], op=ALU.add), reads=[psk(6), Rk[cr]], writes=[Rk[nr]])
                cr = nr
                warm()
                yield
            Rbd, Rbk = Rb[cr], Rk[cr]
            Yb, Yk = Pb[1 - cp], Pk[1 - cp]
            Tb, Tk = Qb[1 - cq], Qk[1 - cq]
            mm4(4, lambda s_: csr(wkT, s_), lambda s_: csr(Rbd, s_), ["g10", Rbk])
            P.op("act", lambda e: e.copy(out=Rr(Yb[:]), in_=PS[4][:, :]), reads=[psk(4)], writes=[Yk])
            mm_t(5, Rbd, Rbk)
            P.op("dve", lambda e: e.tensor_copy(out=Rr(Tb[:]), in_=PS[5][:, :]), reads=[psk(5)], writes=[Tk])
            yield
            nr = 1 - cr
            mm4(6, lambda s_: csr(Tb, s_), lambda s_: csr(Yb, s_), [Tk, Yk])
            P.op("dve", lambda e: e.tensor_tensor(out=Rr(Rb[nr][:]), in0=Rbd[:], in1=PS[6][:, :], op=ALU.add), reads=[psk(6), Rbk], writes=[Rk[nr]])
            cr = nr
            yield
            R, Rkey = Rb[cr], Rk[cr]
            mm4(4, lambda s_: csr(R, s_), lambda s_: csr(vb_tok, s_), [Rkey, "g6"])
            P.op("act", lambda e: e.copy(out=wv_[:], in_=PS[4][:, :]), reads=[psk(4)], writes=["g9"])
            mm4(5, lambda s_: csr(kb_tok, s_), lambda s_: csr(R, s_), [Rkey, "g4"])
            P.op("dve", lambda e: e.tensor_copy(out=Rr(wkT[:]), in_=PS[5][:, :]), reads=[psk(5)], writes=["g10"])
            yield
            P.op("act", lambda e: e.copy(out=Rr(Sr[:]), in_=Sst[:, h, :]), reads=["Sst"], writes=["Sr"])
            for s_ in range(4):
                u, uk = ub[s_ % 2], f"u{s_ % 2}"
                P.op("pe", lambda e, s_=s_: e.matmul(PS[7][:, 0:128], lhsT=csr(wkT, s_), rhs=Rr(Sr[:]), start=True, stop=True), reads=["g10", "Sr"], writes=[psk(7)])
                P.op("dve", lambda e, s_=s_, u=u: e.tensor_tensor(out=Rr(u[:, 0:128]), in0=cs(wv_, s_), in1=PS[7][:, 0:128], op=ALU.subtract), reads=["g9", psk(7)], writes=[uk])
                P.op("pe", lambda e, s_=s_: e.matmul(PS[1][:, s_ * 128:(s_ + 1) * 128], lhsT=Rr(Sr[:]), rhs=csr(qdT, s_), start=True, stop=False), reads=["Sr", "g8"], writes=[psk(1)])
                P.op("pe", lambda e, s_=s_, u=u: e.matmul(PS[1][:, s_ * 128:(s_ + 1) * 128], lhsT=Rr(u[:, 0:128]), rhs=csr(QKT, s_), start=False, stop=True), reads=[uk, "g7"], writes=[psk(1)])
                P.op("pe", lambda e, s_=s_, u=u: e.matmul(PS[7][:, 128:256], lhsT=csr(kt_tok, s_), rhs=Rr(u[:, 0:128]), start=True, stop=True), reads=["g5", uk], writes=[psk(7)])
                P.op("dve", lambda e, s_=s_: e.scalar_tensor_tensor(out=Sst[:, h, :], in0=Sst[:, h, :], scalar=v3(EGB)[:, s_, 127:128], in1=PS[7][:, 128:256], op0=ALU.mult, op1=ALU.add),
                     reads=["Sst", "g3", psk(7)], writes=["Sst"])
                if s_ < 3:
                    P.op("act", lambda e: e.copy(out=Rr(Sr[:]), in_=Sst[:, h, :]), reads=["Sst", "Sr"], writes=["Sr"])
                yield
            P.op("act", lambda e: e.activation(out=sq2[:].bitcast(F32R), in_=PS[1][:, :], func=AF.Square), reads=[psk(1)], writes=["n4"])
            P.op("pe", lambda e: e.matmul(PS[7][:, :], lhsT=ones_r[:].bitcast(F32R), rhs=sq2[:].bitcast(F32R), start=True, stop=True), reads=["ones_r", "n4"], writes=[psk(7)])
            rsqrt_ps(rq2, "n5", 7, 1.0 / 128.0)
            P.op("dve", lambda e: e.scalar_tensor_tensor(out=rq2[:], in0=PS[1][:, :], scalar=gnw[:, 0:1], in1=rq2[:], op0=ALU.mult, op1=ALU.mult),
                 reads=[psk(1), "gnw", "n5"], writes=["n5"])
            P.op("dve", lambda e: e.tensor_tensor(out=mixT[:, h, :], in0=rq2[:], in1=zs[:], op=ALU.mult), reads=["n5", zk], writes=["mixT"])
            yield

        def run_interleaved(gens, pending=None):
            gens = list(gens)
            while gens:
                for g_ in list(gens):
                    try:
                        r_ = next(g_)
                        if r_ == "B_DONE" and pending is not None:
                            gens.append(pending)
                            pending = None
                    except StopIteration:
                        gens.remove(g_)
            assert pending is None

        def out_proj_store(rows, nsub, dst_ap, mixT=None, xt=None, xkeys=None, next_src=None):
            if mixT is None:
                mixT, xt = mixT_p, xt_p
            if xkeys is None:
                xkeys = [f"xt{s_}" for s_ in range(nsub)]
            i = 0
            for s in range(nsub):
                xk = xkeys[s]
                for half in range(2):
                    bd = 2 + (i % 2)
                    i += 1
                    for c in range(8):
                        P.op("pe", lambda e, c=c, s=s, half=half, bd=bd: e.matmul(PS[bd][:rows, :], lhsT=mixT[:, c, s * rows:(s + 1) * rows],
                                                                                 rhs=wout[:, c, half * 512:(half + 1) * 512], start=(c == 0), stop=(c == 7)),
                             reads=["mixT", f"wo{c}"], writes=[psk(bd)])
                    P.op("dve", lambda e, s=s, half=half, bd=bd: e.tensor_tensor(out=xt[:rows, s, half * 512:(half + 1) * 512],
                                                                                in0=xt[:rows, s, half * 512:(half + 1) * 512],
                                                                                in1=PS[bd][:rows, :], op=ALU.add),
                         reads=[psk(bd), xk], writes=[xk])
                P.dma("sp", dst_ap[s * rows:(s + 1) * rows, :], xt[:rows, s, :], reads=[xk])
                if next_src is not None:
                    P.dma("sp", xt[:, s, :], next_src[s * 128:(s + 1) * 128, :], writes=[xk])

        mixT_p, xt_p = mixT, xt
        for b in range(NB if do_prompt else 0):
            P.op("dve", lambda e: e.memset(Hg[:], 0.0), reads=["Hg"], writes=["Hg"])
            P.op("dve", lambda e: e.memset(Hl[:], 0.0), reads=["Hl"], writes=["Hl"])
            P.op("dve", lambda e: e.memset(Sst[:], 0.0), reads=["Sst"], writes=["Sst"])
            P.op("dve", lambda e: e.memset(hst[:], 0.0), reads=["hst"], writes=["hst"])
            for st in range(SEQ // T):
                if b == 0 and st == 0:
                    for s in range(4):
                        P.dma("sp", xt[:, s, :], src_p[b, s * 128:(s + 1) * 128, :], writes=[f"xt{s}"])
                for s in range(4):
                    norm_transpose(xt[:, s, :], 128, f"xt{s}", xn, junk, ss, rstd, xnT, "xnT", s * 128, "m", wrow)
                gdn_common()
                run_interleaved([gdn_A(0)])
                for h in range(4):
                    run_interleaved([gdn_BCD(h), lru_group(h)], pending=(gdn_A(h + 1) if h < 3 else None))
                    _chk(f"gdn_b{b}_st{st}_h{h}")
                nb_, nst_ = (b, st + 1) if st + 1 < SEQ // T else (b + 1, 0)
                nxt_ = src_p[nb_, nst_ * T:(nst_ + 1) * T, :] if nb_ < NB else None
                out_proj_store(128, 4, dst_p[b, st * T:(st + 1) * T, :], next_src=nxt_)
            P.dma("sp", p_gdn[b].rearrange("h k v -> k h v"), Sst[:], reads=["Sst"])
            for ch in range(12):
                P.dma("sp", p_gdn_conv[b][:, ch * 128:(ch + 1) * 128].rearrange("r p -> p r"), Hg[:, ch, :], reads=["Hg"], allow_slow_non_contiguous=True)
            for c in range(4):
                P.dma("sp", p_lru_conv[b][:, c * 128:(c + 1) * 128].rearrange("r p -> p r"), Hl[:, c, :], reads=["Hl"], allow_slow_non_contiguous=True)
            P.dma("sp", p_lru[b].rearrange("(c p) -> p c", p=128), hst[:], reads=["hst"], allow_slow_non_contiguous=True)
        P.barrier()
        A.release(m1)
        if not do_samples:
            A.release(m0)
            return
        xs = A.alloc("xs", [NS, 1, D], F32)
        xnTs = A.alloc("xnTs", [128, 8, NS], BF16)
        mixTs = A.alloc("mixTs", [128, 8, NS], BF16)
        mix_tok = A.alloc("mix_tok", [NS, D], F32)
        mix_tb = A.alloc("mix_tb", [NS, D], BF16)
        zs_s = A.alloc("zs_s", [NS, 512], F32)
        gnwb = A.alloc("gnwb", [NS, 128], F32)
        mA = A.mark()
        proj = A.alloc("proj_s", [NS, W_IN], F32)
        histg = A.alloc("histg", [NS, 4, 1536], F32)
        cwgb = A.alloc("cwgb", [NS, 4, 1536], F32)
        qkvc = A.alloc("qkvc", [NS, 1536], F32)
        tq = A.alloc("tq", [NS, 1024], F32)
        sm8 = A.alloc("sm8", [NS, 8], F32)
        adb = A.alloc("adb", [NS, 4], F32)
        dtbb = A.alloc("dtbb", [NS, 4], F32)
        bgs = A.alloc("bgs", [NS, 4, 2], F32)
        tg = A.alloc("tg", [NS, 4], F32)
        histl = A.alloc("histl", [NS, 4, 512], F32)
        cwlb = A.alloc("cwlb", [NS, 4, 512], F32)
        rowp = {n: A.alloc(n, [NS, 512], F32) for n in ("cblb", "babb", "bxbb", "clb", "xr_s", "rg_s", "ig_s", "av_s", "a2_s", "h0_s", "gel_s")}
        xrT = A.alloc("xrT", [128, 4, NS], F32)

        P.dma("sp", xs[:, 0, :], src_s, writes=["xs"])
        P.dma("sp", gnwb[:], gdn_norm_w.partition_broadcast(NS), writes=["gnwb"])
        norm_transpose(xs[:, 0, :], NS, "xs", xn, junk, ss, rstd, xnTs, "xnTs", 0, "m", wrow)
        for gi, c0 in enumerate(range(0, W_IN, 512)):
            n = min(512, W_IN - c0)
            bank = 2 + gi % 2
            for kc in range(8):
                P.op("pe", lambda e, kc=kc, c0=c0, n=n, bank=bank: e.matmul(PS[bank][:NS, 0:n], lhsT=xnTs[:, kc, :], rhs=win[:, kc, c0:c0 + n],
                                                                         start=(kc == 0), stop=(kc == 7)), reads=WIN_ALL + ["xnTs"], writes=[psk(bank)])
            P.op("act" if gi % 2 == 0 else "dve",
                 (lambda e, c0=c0, n=n, bank=bank: e.copy(out=proj[:, c0:c0 + n], in_=PS[bank][:NS, 0:n])) if gi % 2 == 0 else
                 (lambda e, c0=c0, n=n, bank=bank: e.tensor_copy(out=proj[:, c0:c0 + n], in_=PS[bank][:NS, 0:n])),
                 reads=[psk(bank)], writes=["proj_s"])
        P.op("act", lambda e: e.activation(out=zs_s[:], in_=proj[:, 1536:2048], func=AF.Silu), reads=["proj_s"], writes=["zs_s"])
        P.dma("sp", histg[:, 0:3, :], state_gdn_conv, writes=["histg"])
        P.op("pool", lambda e: e.tensor_copy(out=histg[:, 3, :], in_=proj[:, 0:1536]), reads=["proj_s"], writes=["histg"])
        P.dma("sp", s_gdn_conv, histg[:, 1:4, :], reads=["histg"])
        P.dma("sp", cwgb[:].rearrange("p j c -> p (j c)"), conv_gdn_w.rearrange("j c -> (j c)").partition_broadcast(NS), writes=["cwgb"])
        P.op("dve", lambda e: e.tensor_tensor(out=cwgb[:], in0=histg[:], in1=cwgb[:], op=ALU.mult), reads=["histg", "cwgb"], writes=["cwgb"])
        P.op("dve", lambda e: e.tensor_tensor(out=qkvc[:], in0=cwgb[:, 0, :], in1=cwgb[:, 1, :], op=ALU.add), reads=["cwgb"], writes=["qkvc"])
        P.op("dve", lambda e: e.tensor_tensor(out=qkvc[:], in0=qkvc[:], in1=cwgb[:, 2, :], op=ALU.add), reads=["cwgb", "qkvc"], writes=["qkvc"])
        P.op("dve", lambda e: e.tensor_tensor(out=qkvc[:], in0=qkvc[:], in1=cwgb[:, 3, :], op=ALU.add), reads=["cwgb", "qkvc"], writes=["qkvc"])
        P.op("act", lambda e: e.activation(out=qkvc[:], in_=qkvc[:], func=AF.Silu), reads=["qkvc"], writes=["qkvc"])
        P.op("dve", lambda e: e.tensor_tensor(out=tq[:], in0=qkvc[:, 0:1024], in1=qkvc[:, 0:1024], op=ALU.mult), reads=["qkvc"], writes=["tq"])
        P.op("dve", lambda e: e.tensor_reduce(out=sm8[:], in_=tq[:].rearrange("p (a d) -> p a d", d=128), axis=AX.X, op=ALU.add), reads=["tq"], writes=["sm8"])
        P.op("act", lambda e: e.activation(out=sm8[:], in_=sm8[:], func=AF.Sqrt, bias=EPS), reads=["sm8"], writes=["sm8"])
        P.op("dve", lambda e: e.reciprocal(out=sm8[:], in_=sm8[:]), reads=["sm8"], writes=["sm8"])
        q3 = qkvc[:, 0:512].rearrange("p (h d) -> p h d", h=4)
        k3 = qkvc[:, 512:1024].rearrange("p (h d) -> p h d", h=4)
        P.op("dve", lambda e: e.scalar_tensor_tensor(out=q3, in0=q3, scalar=128.0 ** -0.5, in1=sm8[:, 0:4].unsqueeze(2).to_broadcast([NS, 4, 128]),
                                                     op0=ALU.mult, op1=ALU.mult), reads=["qkvc", "sm8"], writes=["qkvc"])
        P.op("dve", lambda e: e.tensor_tensor(out=k3, in0=k3, in1=sm8[:, 4:8].unsqueeze(2).to_broadcast([NS, 4, 128]), op=ALU.mult),
             reads=["qkvc", "sm8"], writes=["qkvc"])
        P.dma("sp", adb[:], gdn_a_log.partition_broadcast(NS), writes=["adb"])
        P.dma("sp", dtbb[:], gdn_dt_bias.partition_broadcast(NS), writes=["dtbb"])
        P.op("act", lambda e: e.activation(out=adb[:], in_=adb[:], func=AF.Exp), reads=["adb"], writes=["adb"])
        P.op("act", lambda e: e.activation(out=bgs[:, :, 0], in_=proj[:, 2048:2052], func=AF.Sigmoid), reads=["proj_s"], writes=["bgs"])
        P.op("dve", lambda e: e.tensor_tensor(out=tg[:], in0=proj[:, 2052:2056], in1=dtbb[:], op=ALU.add), reads=["proj_s", "dtbb"], writes=["tg"])
        P.op("act", lambda e: e.activation(out=tg[:], in_=tg[:], func=AF.Exp), reads=["tg"], writes=["tg"])
        P.op("act", lambda e: e.activation(out=tg[:], in_=tg[:], func=AF.Ln, bias=1.0), reads=["tg"], writes=["tg"])
        P.op("dve", lambda e: e.scalar_tensor_tensor(out=bgs[:, :, 1], in0=tg[:], scalar=-1.0, in1=adb[:], op0=ALU.mult, op1=ALU.mult),
             reads=["tg", "adb", "bgs"], writes=["bgs"])
        sgv = sg_qk.rearrange("(b h) x -> b h x", h=4)
        P.dma("sp", sgv[:, :, 0:128], q3, reads=["qkvc"], writes=["sg_qk"])
        P.dma("sp", sgv[:, :, 128:256], k3, reads=["qkvc"], writes=["sg_qk"])
        P.dma("sp", sg_v, qkvc[:, 1024:1536], reads=["qkvc"], writes=["sg_v"])
        P.dma("sp", sg_bg.rearrange("(b h) x -> b (h x)", h=4), bgs[:].rearrange("p h x -> p (h x)"), reads=["bgs"], writes=["sg_bg"])
        cblb, babb, bxbb, clb, xr_s, rg_s, ig_s, av_s, a2_s, h0_s, gel_s = (rowp[n] for n in ("cblb", "babb", "bxbb", "clb", "xr_s", "rg_s", "ig_s", "av_s", "a2_s", "h0_s", "gel_s"))
        P.dma("sp", histl[:, 0:3, :], state_lru_conv, writes=["histl"])
        P.op("pool", lambda e: e.tensor_copy(out=histl[:, 3, :], in_=proj[:, 2568:3080]), reads=["proj_s"], writes=["histl"])
        P.dma("sp", s_lru_conv, histl[:, 1:4, :], reads=["histl"])
        P.dma("sp", cwlb[:].rearrange("p j c -> p (j c)"), conv_lru_w.rearrange("j c -> (j c)").partition_broadcast(NS), writes=["cwlb"])
        P.dma("sp", cblb[:], conv_lru_b.partition_broadcast(NS), writes=["cblb"])
        P.dma("sp", babb[:], lru_ba.partition_broadcast(NS), writes=["babb"])
        P.dma("sp", bxbb[:], lru_bx.partition_broadcast(NS), writes=["bxbb"])
        P.dma("sp", clb[:], lru_lambda.partition_broadcast(NS), writes=["clb"])
        P.dma("sp", h0_s[:], state_lru, writes=["h0_s"])
        P.op("act", lambda e: e.activation(out=clb[:], in_=clb[:], func=AF.Exp, scale=-1.0), reads=["clb"], writes=["clb"])
        P.op("act", lambda e: e.activation(out=clb[:], in_=clb[:], func=AF.Ln, bias=1.0), reads=["clb"], writes=["clb"])
        P.op("dve", lambda e: e.tensor_scalar(out=clb[:], in0=clb[:], scalar1=-8.0, scalar2=None, op0=ALU.mult), reads=["clb"], writes=["clb"])
        P.op("dve", lambda e: e.tensor_tensor(out=cwlb[:], in0=histl[:], in1=cwlb[:], op=ALU.mult), reads=["histl", "cwlb"], writes=["cwlb"])
        P.op("dve", lambda e: e.tensor_tensor(out=xr_s[:], in0=cwlb[:, 0, :], in1=cblb[:], op=ALU.add), reads=["cwlb", "cblb"], writes=["xr_s"])
        for j in range(1, 4):
            P.op("dve", lambda e, j=j: e.tensor_tensor(out=xr_s[:], in0=xr_s[:], in1=cwlb[:, j, :], op=ALU.add), reads=["cwlb", "xr_s"], writes=["xr_s"])
        for c in range(4):
            P.op("pe", lambda e, c=c: e.transpose(out=PS[0][:, c * NS:(c + 1) * NS], in_=xr_s[:, c * 128:(c + 1) * 128], identity=identf[:NS, :NS]),
                 reads=["xr_s", "identf"], writes=[psk(0)])
        P.op("dve", lambda e: e.tensor_copy(out=xrT[:], in_=PS[0][:, 0:4 * NS].rearrange("p (c t) -> p c t", c=4)), reads=[psk(0)], writes=["xrT"])
        for (Wbd, wkey, bank, bb_, bkey, dst, dkey) in ((WAbd, "WAbd", 4, babb, "babb", rg_s, "rg_s"), (WXbd, "WXbd", 5, bxbb, "bxbb", ig_s, "ig_s")):
            for c in range(4):
                P.op("pe", lambda e, c=c, Wbd=Wbd, bank=bank: e.matmul(PS[bank][:NS, c * 128:(c + 1) * 128], lhsT=xrT[:, c, :], rhs=Wbd[:, c, :], start=True, stop=True),
                     reads=["xrT", wkey], writes=[psk(bank)])
            P.op("dve", lambda e, bank=bank, bb_=bb_, dst=dst: e.tensor_tensor(out=dst[:], in0=PS[bank][:NS, :], in1=bb_[:], op=ALU.add), reads=[psk(bank), bkey], writes=[dkey])
            P.op("act", lambda e, dst=dst: e.activation(out=dst[:], in_=dst[:], func=AF.Sigmoid), reads=[dkey], writes=[dkey])
        P.op("dve", lambda e: e.tensor_tensor(out=av_s[:], in0=clb[:], in1=rg_s[:], op=ALU.mult), reads=["clb", "rg_s"], writes=["av_s"])
        P.op("act", lambda e: e.activation(out=a2_s[:], in_=av_s[:], func=AF.Exp, scale=2.0), reads=["av_s"], writes=["a2_s"])
        P.op("act", lambda e: e.activation(out=av_s[:], in_=av_s[:], func=AF.Exp), reads=["av_s", "a2_s"], writes=["av_s"])
        P.op("dve", lambda e: e.tensor_scalar(out=a2_s[:], in0=a2_s[:], scalar1=1.0, scalar2=-1.0, op0=ALU.min, op1=ALU.mult), reads=["a2_s"], writes=["a2_s"])
        P.op("act", lambda e: e.activation(out=a2_s[:], in_=a2_s[:], func=AF.Sqrt, bias=1.0), reads=["a2_s"], writes=["a2_s"])
        P.op("dve", lambda e: e.tensor_tensor(out=ig_s[:], in0=ig_s[:], in1=xr_s[:], op=ALU.mult), reads=["ig_s", "xr_s"], writes=["ig_s"])
        P.op("dve", lambda e: e.tensor_tensor(out=ig_s[:], in0=ig_s[:], in1=a2_s[:], op=ALU.mult), reads=["ig_s", "a2_s"], writes=["ig_s"])
        P.op("dve", lambda e: e.tensor_tensor(out=h0_s[:], in0=h0_s[:], in1=av_s[:], op=ALU.mult), reads=["h0_s", "av_s"], writes=["h0_s"])
        P.op("dve", lambda e: e.tensor_tensor(out=h0_s[:], in0=h0_s[:], in1=ig_s[:], op=ALU.add), reads=["h0_s", "ig_s"], writes=["h0_s"])
        P.dma("sp", s_lru, h0_s[:], reads=["h0_s"])
        P.op("act", lambda e: e.activation(out=gel_s[:], in_=proj[:, 2056:2568], func=AF.Gelu_apprx_tanh), reads=["proj_s"], writes=["gel_s"])
        P.op("dve", lambda e: e.tensor_tensor(out=mix_tok[:, 512:1024], in0=gel_s[:], in1=h0_s[:], op=ALU.mult), reads=["gel_s", "h0_s"], writes=["mix_tok"])
        P.barrier()
        A.release(mA)
        S_p = A.alloc("S_p", [128, 128, 64], F32)
        T1g = A.alloc("T1g", [128, 128 * 64], F32)
        qk_p = A.alloc("qk_p", [128, 256], F32)
        v_p = A.alloc("v_p", [128, 64], F32)
        bg_p = A.alloc("bg_p", [128, 2], F32)
        eg_p = A.alloc("eg_p", [128, 2], F32)
        pred = A.alloc("pred", [128, 64], F32)
        dl = A.alloc("dl", [128, 64], F32)
        o_p = A.alloc("o_p", [128, 64], F32)
        o_tok = A.alloc("o_tokg", [NS, 512], F32)
        sm4 = A.alloc("sm4", [NS, 4], F32)
        sgs = state_gdn.rearrange("b h k (e d) -> e (b h) k d", e=2)
        sgo = s_gdn.rearrange("b h k (e d) -> e (b h) k d", e=2)
        svv = sg_v.rearrange("b (h e d) -> e (b h) d", h=4, e=2)
        sov = so_g.rearrange("b (h e d) -> e (b h) d", h=4, e=2)
        for e_ in range(2):
            sl = slice(e_ * 64, (e_ + 1) * 64)
            P.dma("sp", S_p[sl, :, :], sgs[e_], writes=["S_p"])
            P.dma("sp", qk_p[sl, :], sg_qk, reads=["sg_qk"], writes=["qk_p"])
            P.dma("sp", v_p[sl, :], svv[e_], reads=["sg_v"], writes=["v_p"])
            P.dma("sp", bg_p[sl, :], sg_bg, reads=["sg_bg"], writes=["bg_p"])
        P.op("act", lambda e: e.activation(out=eg_p[:, 0:1], in_=bg_p[:, 1:2], func=AF.Exp), reads=["bg_p"], writes=["eg_p"])
        P.op("dve", lambda e: e.tensor_scalar(out=eg_p[:, 1:2], in0=eg_p[:, 0:1], scalar1=-1.0, scalar2=None, op0=ALU.mult), reads=["eg_p"], writes=["eg_p"])
        T1vk = T1g[:].rearrange("p (v k) -> p v k", v=64)
        T1kv = T1g[:].rearrange("p (k v) -> p k v", k=128)
        Svk = S_p[:].rearrange("p k v -> p v k")
        P.op("dve", lambda e: e.tensor_tensor(out=T1vk, in0=Svk, in1=qk_p[:, 128:256].unsqueeze(1).to_broadcast([128, 64, 128]), op=ALU.mult),
             reads=["S_p", "qk_p"], writes=["T1g"])
        P.op("dve", lambda e: e.tensor_reduce(out=pred[:], in_=T1vk, axis=AX.X, op=ALU.add), reads=["T1g"], writes=["pred"])
        P.op("dve", lambda e: e.scalar_tensor_tensor(out=dl[:], in0=pred[:], scalar=eg_p[:, 1:2], in1=v_p[:], op0=ALU.mult, op1=ALU.add),
             reads=["pred", "eg_p", "v_p"], writes=["dl"])
        P.op("dve", lambda e: e.tensor_scalar(out=dl[:], in0=dl[:], scalar1=bg_p[:, 0:1], scalar2=None, op0=ALU.mult), reads=["dl", "bg_p"], writes=["dl"])
        P.op("dve", lambda e: e.tensor_tensor(out=T1kv, in0=qk_p[:, 128:256].unsqueeze(2).to_broadcast([128, 128, 64]),
                                               in1=dl[:].unsqueeze(1).to_broadcast([128, 128, 64]), op=ALU.mult), reads=["qk_p", "dl", "T1g"], writes=["T1g"])
        P.op("dve", lambda e: e.scalar_tensor_tensor(out=S_p[:].rearrange("p k v -> p (k v)"), in0=S_p[:].rearrange("p k v -> p (k v)"), scalar=eg_p[:, 0:1],
                                                     in1=T1g[:], op0=ALU.mult, op1=ALU.add), reads=["S_p", "eg_p", "T1g"], writes=["S_p"])
        for e_ in range(2):
            P.dma("sp", sgo[e_], S_p[e_ * 64:(e_ + 1) * 64, :, :], reads=["S_p"])
        P.op("dve", lambda e: e.tensor_tensor(out=T1vk, in0=Svk, in1=qk_p[:, 0:128].unsqueeze(1).to_broadcast([128, 64, 128]), op=ALU.mult),
             reads=["S_p", "qk_p", "T1g"], writes=["T1g"])
        P.op("dve", lambda e: e.tensor_reduce(out=o_p[:], in_=T1vk, axis=AX.X, op=ALU.add), reads=["T1g"], writes=["o_p"])
        for e_ in range(2):
            P.dma("sp", sov[e_], o_p[e_ * 64:(e_ + 1) * 64, :], reads=["o_p"], writes=["so_g"])
        P.dma("sp", o_tok[:], so_g, reads=["so_g"], writes=["o_tokg"])
        o3 = o_tok[:].rearrange("p (h d) -> p h d", h=4)
        m3 = mix_tok[:, 0:512].rearrange("p (h d) -> p h d", h=4)
        P.op("dve", lambda e: e.tensor_tensor(out=m3, in0=o3, in1=o3, op=ALU.mult), reads=["o_tokg"], writes=["mix_tok"])
        P.op("dve", lambda e: e.tensor_reduce(out=sm4[:], in_=m3, axis=AX.X, op=ALU.add), reads=["mix_tok"], writes=["sm4"])
        P.op("act", lambda e: e.activation(out=sm4[:], in_=sm4[:], func=AF.Sqrt, scale=1.0 / 128.0, bias=EPS), reads=["sm4"], writes=["sm4"])
        P.op("dve", lambda e: e.reciprocal(out=sm4[:], in_=sm4[:]), reads=["sm4"], writes=["sm4"])
        P.op("dve", lambda e: e.tensor_tensor(out=m3, in0=o3, in1=sm4[:].unsqueeze(2).to_broadcast([NS, 4, 128]), op=ALU.mult), reads=["o_tokg", "sm4", "mix_tok"], writes=["mix_tok"])
        P.op("dve", lambda e: e.tensor_tensor(out=m3, in0=m3, in1=gnwb[:].unsqueeze(1).to_broadcast([NS, 4, 128]), op=ALU.mult), reads=["gnwb", "mix_tok"], writes=["mix_tok"])
        P.op("dve", lambda e: e.tensor_tensor(out=mix_tok[:, 0:512], in0=mix_tok[:, 0:512], in1=zs_s[:], op=ALU.mult), reads=["zs_s", "mix_tok"], writes=["mix_tok"])
        transpose_to_fm(mix_tok[:, :], NS, "mix_tok", mixTs, "mixT", 8, mix_tb, "mix_tb", 0)
        out_proj_store(NS, 1, dst_s, mixT=mixTs, xt=xs, xkeys=["xs"])
        P.barrier()
        A.release(m0)

    SLOPES = [2.0 ** (-8.0 * (h + 1) / 16.0) for h in range(16)]
    NEG = -30000.0

    def transpose_to_fm(src_ap, rows, src_key, dstT, dst_key, nch, tmpb, tmp_key, bank):
        P.op("dve", lambda e: e.tensor_copy(out=tmpb[:rows, 0:nch * 128], in_=src_ap), reads=[src_key], writes=[tmp_key])
        pb = PS[bank][:].bitcast(BF16)
        for c0 in range(0, nch, 4):
            n = min(4, nch - c0)
            for j in range(n):
                c = c0 + j
                P.op("pe", lambda e, c=c, j=j: e.transpose(out=pb[:, j * 128:j * 128 + rows], in_=tmpb[:rows, c * 128:(c + 1) * 128],
                                                           identity=identb[:rows, :rows]),
                     reads=[tmp_key, "identb"], writes=[psk(bank)])
            src = pb[:, 0:512].rearrange("p (j t) -> p j t", j=4)[:, 0:n, 0:rows]
            P.op("dve", lambda e, src=src, c0=c0, n=n: e.tensor_copy(out=dstT[:, c0:c0 + n, 0:rows], in_=src), reads=[psk(bank)], writes=[dst_key])

    def attn_stage(src_p, src_s, dst_p, dst_s, do_samples=True, do_prompt=True):
        m0 = A.mark()
        wq = A.alloc("wq", [128, 8, 1024], BF16)
        wkd = A.alloc("wkd", [128, 8, 512], BF16)
        wk = A.alloc("wk", [128, 8, 256], BF16)
        wv = A.alloc("wv", [128, 8, 256], BF16)
        woc = A.alloc("woc", [128, 8, D], BF16)
        wrow = A.alloc("wrow", [128, D], F32)
        sinkb = A.alloc("sinkb", [128, 16], F32)
        xt = A.alloc("xt", [128, 4, D], F32)
        xn = A.alloc("xn", [128, D], BF16)
        junk = A.alloc("junk", [128, D], BF16)
        ss = A.alloc("ss", [128, 1], F32)
        rstd = A.alloc("rstd", [128, 1], F32)
        xnT = A.alloc("xnT", [128, 8, T], BF16)
        oT = A.alloc("oT", [128, 8, T], BF16)
        m1 = A.mark()
        qT = A.alloc("qT", [128, 16, T], BF16)
        kT2 = A.alloc("kT2", [128, 4, 128 + T], BF16)
        vtok = A.alloc("vtok", [128, 5, 256], BF16)
        klast = A.alloc("klast", [128, 256], F32)
        vlast = A.alloc("vlast", [128, 256], F32)
        REL = A.alloc("REL", [128, 256], F32)
        MB = A.alloc("MB", [128, 256], F32)
        AB = A.alloc("AB", [128, 16, 256], F32)
        NR = 3
        sc = [A.alloc(f"sc{i}", [128, 2, 258], F32) for i in range(NR)]
        pe_ = [A.alloc(f"pe{i}", [128, 2, 258], F32) for i in range(NR)]
        pn = [A.alloc(f"pn{i}", [128, 2, 256], BF16) for i in range(NR)]
        pT = [A.alloc(f"pT{i}", [128, 4, 128], BF16) for i in range(NR)]
        sm = [A.alloc(f"sm{i}", [128, 8], F32) for i in range(NR)]

        P.dma("sp", wrow[:], norm_mix[1].partition_broadcast(128), writes=["wrow"])
        P.dma("sp", sinkb[:], sinks_c.partition_broadcast(128), writes=["sinkb"])
        WQ_ALL = [f"wq{c}" for c in range(8)]
        wload(wk[:, :, :], w_qkv_c[:, 1024:1280].rearrange("(kc p) c -> p kc c", p=128), "wk")
        wload(wv[:, :, :], w_qkv_c[:, 1280:1536].rearrange("(kc p) c -> p kc c", p=128), "wv")
        for c in range(8):
            wload(wq[:, :, c * 128:(c + 1) * 128], w_qkv_c[:, c * 128:(c + 1) * 128].rearrange("(kc p) c -> p kc c", p=128), f"wq{c}")
        for c in range(8):
            wload(woc[:, c, :], w_out_c[c * 128:(c + 1) * 128, :], f"woc{c}")
        for kc in range(8):
            for e_ in range(2):
                o = wkd[:, kc, :].rearrange("p (g e d) -> p g e d", g=4, e=2)[:, :, e_, :]
                i_ = wk[:, kc, :].rearrange("p (g d) -> p g d", g=4)
                P.op("pool", lambda e, o=o, i_=i_: e.tensor_copy(out=o, in_=i_), reads=["wk"], writes=["wkd"])
        P.op("pool", lambda e: e.iota(REL[:], pattern=[[-1, 256]], base=128, channel_multiplier=1, allow_small_or_imprecise_dtypes=True),
             writes=["REL"])
        P.op("pool", lambda e: e.memset(MB[:], 0.0), writes=["MB"])
        P.op("pool", lambda e: e.affine_select(out=MB[:], in_=MB[:], pattern=[[-1, 256]], compare_op=ALU.is_ge, fill=NEG, base=128,
                                               channel_multiplier=1), reads=["MB"], writes=["MB"])
        P.op("pool", lambda e: e.affine_select(out=MB[:], in_=MB[:], pattern=[[1, 256]], compare_op=ALU.is_ge, fill=NEG, base=0,
                                               channel_multiplier=-1), reads=["MB"], writes=["MB"])
        for h in range(16):
            P.op("dve", lambda e, h=h: e.scalar_tensor_tensor(out=AB[:, h, :], in0=REL[:], scalar=-SLOPES[h], in1=MB[:], op0=ALU.mult, op1=ALU.add),
                 reads=["REL", "MB"], writes=["AB"])

        def out_proj_store(rows, nsub, dst_ap, xkeys=None, next_src=None):
            if xkeys is None:
                xkeys = [f"xt{s_}" for s_ in range(nsub)]
            i = 0
            for s in range(nsub):
                xk = xkeys[s]
                for half in range(2):
                    bd = 2 + (i % 2)
                    i += 1
                    for c in range(8):
                        P.op("pe", lambda e, c=c, s=s, half=half, bd=bd: e.matmul(PS[bd][:rows, :], lhsT=oT[:, c, s * rows:(s + 1) * rows],
                                                                                 rhs=woc[:, c, half * 512:(half + 1) * 512], start=(c == 0), stop=(c == 7)),
                             reads=["oT", f"woc{c}"], writes=[psk(bd)])
                    P.op("dve", lambda e, s=s, half=half, bd=bd: e.tensor_tensor(out=xt[:rows, s, half * 512:(half + 1) * 512],
                                                                                in0=xt[:rows, s, half * 512:(half + 1) * 512],
                                                                                in1=PS[bd][:rows, :], op=ALU.add),
                         reads=[psk(bd), xk], writes=[xk])
                P.dma("sp", dst_ap[s * rows:(s + 1) * rows, :], xt[:rows, s, :], reads=[xk])
                if next_src is not None:
                    P.dma("sp", xt[:, s, :], next_src[s * 128:(s + 1) * 128, :], writes=[xk])

        P.op("pool", lambda e: e.memset(qT[:], 0.0), writes=["qT"])
        _chk('consts')
        for b in range(NB if do_prompt else 0):
            for st in range(SEQ // T):
                if b == 0 and st == 0:
                    for s in range(4):
                        P.dma("sp", xt[:, s, :], src_p[b, s * 128:(s + 1) * 128, :], writes=[f"xt{s}"])
                for s in range(4):
                    norm_transpose(xt[:, s, :], 128, f"xt{s}", xn, junk, ss, rstd, xnT, "xnT", s * 128, "a", wrow)
                for c in range(8):
                    bq = 2 + (c % 2)
                    for kc in range(8):
                        P.op("pe", lambda e, c=c, kc=kc, bq=bq: e.matmul(PS[bq][:, :], lhsT=wq[:, kc, c * 128:(c + 1) * 128], rhs=xnT[:, kc, :],
                                                                        start=(kc == 0), stop=(kc == 7)), reads=[f"wq{c}", "xnT"], writes=[psk(bq)])
                    for j in range(2):
                        P.op("act", lambda e, c=c, bq=bq, j=j: e.activation(out=qT[64 * j:64 * j + 64, 2 * c + j, :], in_=PS[bq][64 * j:64 * j + 64, :], func=AF.Copy, scale=0.125),
                             reads=[psk(bq)], writes=["qT"])
                for g in range(4):
                    bq = 2 + (g % 2)
                    for kc in range(8):
                        P.op("pe", lambda e, g=g, kc=kc, bq=bq: e.matmul(PS[bq][:, :], lhsT=wkd[:, kc, g * 128:(g + 1) * 128], rhs=xnT[:, kc, :],
                                                                        start=(kc == 0), stop=(kc == 7)), reads=["wkd", "xnT"], writes=[psk(bq)])
                    P.op("dve", lambda e, g=g, bq=bq: e.tensor_copy(out=kT2[:, g, 128:128 + T], in_=PS[bq][:, :]), reads=[psk(bq)], writes=["kT2"])
                for s in range(4):
                    bq = 2 + (s % 2)
                    for kc in range(8):
                        P.op("pe", lambda e, s=s, kc=kc, bq=bq: e.matmul(PS[bq][:, 0:256], lhsT=xnT[:, kc, s * 128:(s + 1) * 128], rhs=wv[:, kc, :],
                                                                        start=(kc == 0), stop=(kc == 7)), reads=["wv", "xnT"], writes=[psk(bq)])
                    P.op("act", lambda e, s=s, bq=bq: e.copy(out=vtok[:, s + 1, :], in_=PS[bq][:, 0:256]), reads=[psk(bq)], writes=["vtok"])
                    if st == SEQ // T - 1 and s == 3 and not DBG.get('no_last'):
                        P.op("dve", lambda e, bq=bq: e.tensor_copy(out=vlast[:], in_=PS[bq][:, 0:256]), reads=[psk(bq)], writes=["vlast"])
                        if not DBG.get('no_vdma'):
                            P.dma("sp", p_swa_v[b], vlast[:], reads=["vlast"])
                        for kc in range(8 if not DBG.get('no_k') else 0):
                            P.op("pe", lambda e, kc=kc: e.matmul(PS[3][:, 0:256], lhsT=xnT[:, kc, 384:512], rhs=wk[:, kc, :],
                                                                 start=(kc == 0), stop=(kc == 7)), reads=["wk", "xnT"], writes=[psk(3)])
                        if not DBG.get('no_k'):
                            P.op("dve", lambda e: e.tensor_copy(out=klast[:], in_=PS[3][:, 0:256]), reads=[psk(3)], writes=["klast"])
                            P.dma("sp", p_swa_k[b], klast[:], reads=["klast"])
                _chk('proj')
                items = [(s_, c_) for s_ in range(4) for c_ in range(8)]

                def stage_a1(idx, pe_part):
                    s, c = items[idx]
                    first = (st == 0 and s == 0)
                    k0 = 128 if first else 0
                    nk = 128 if first else 256
                    g = c // 2
                    r = idx % NR
                    bs = 4 + (idx % 2)
                    scr, per, smr = sc[r], pe_[r], sm[r]
                    kS, kP, kM = f"sc{r}", f"pe{r}", f"sm{r}"
                    if pe_part:
                        for j in range(2):
                            P.op("pe", lambda e, j=j: e.matmul(PS[bs][:, j * 256:j * 256 + nk], lhsT=qT[:, 2 * c + j, s * 128:(s + 1) * 128],
                                                             rhs=kT2[:, g, s * 128 + k0:s * 128 + k0 + nk], start=True, stop=True),
                                 reads=["qT", "kT2"], writes=[psk(bs)])
                        return
                    P.op("dve", lambda e: e.tensor_copy(out=scr[:, :, nk], in_=sinkb[:, 2 * c:2 * c + 2]), reads=["sinkb", kS], writes=[kS])
                    P.op("dve", lambda e: e.tensor_tensor(out=scr[:, :, 0:nk], in0=PS[bs][:].rearrange("p (j k) -> p j k", j=2)[:, :, 0:nk],
                                                          in1=AB[:, 2 * c:2 * c + 2, k0:k0 + nk], op=ALU.add), reads=[psk(bs), "AB", kS], writes=[kS])
                    P.op("dve", lambda e: e.tensor_reduce(out=smr[:, 2:4], in_=scr[:, :, 0:nk + 1], axis=AX.X, op=ALU.max, negate=True), reads=[kS], writes=[kM + "b"])
                    for j in range(2):
                        P.op("act", lambda e, j=j: e.activation(out=per[:, j, 0:nk + 1], in_=scr[:, j, 0:nk + 1], func=AF.Exp, bias=smr[:, 2 + j:3 + j],
                                                             accum_out=smr[:, 4 + j:5 + j]), reads=[kS, kM + "b"], writes=[kP, kM + f"c{j}"])

                def stage_a2(idx):
                    s, c = items[idx]
                    first = (st == 0 and s == 0)
                    nk = 128 if first else 256
                    r = idx % NR
                    per, pnr, smr = pe_[r], pn[r], sm[r]
                    kP, kN, kM = f"pe{r}", f"pn{r}", f"sm{r}"
                    P.op("dve", lambda e: e.reciprocal(out=smr[:, 6:8], in_=smr[:, 4:6]), reads=[kM + "c0", kM + "c1"], writes=[kM + "d"])
                    P.op("dve", lambda e: e.tensor_tensor(out=pnr[:, :, 0:nk], in0=per[:, :, 0:nk], in1=smr[:, 6:8].unsqueeze(2).to_broadcast([128, 2, nk]), op=ALU.mult),
                         reads=[kP, kM + "d"], writes=[kN])

                def stage_b1(idx):
                    s, c = items[idx]
                    first = (st == 0 and s == 0)
                    nkb = 1 if first else 2
                    r = idx % NR
                    pnr, pTr = pn[r], pT[r]
                    kN, kT_ = f"pn{r}", f"pT{r}"
                    bt = 6 + (idx % 2)
                    pbk = PS[bt][:].bitcast(BF16)
                    for j in range(2):
                        for kb in range(nkb):
                            P.op("pe", lambda e, j=j, kb=kb: e.transpose(out=pbk[:, (j * 2 + kb) * 128:(j * 2 + kb + 1) * 128], in_=pnr[:, j, kb * 128:(kb + 1) * 128], identity=identb[:]),
                                 reads=[kN, "identb"], writes=[psk(bt)])
                    P.op("act", lambda e: e.copy(out=pTr[:].rearrange("p (j k) t -> p j k t", j=2)[:, :, 0:nkb, :],
                                                 in_=pbk[:, 0:512].rearrange("p (j k t) -> p j k t", j=2, k=2)[:, :, 0:nkb, :]), reads=[psk(bt)], writes=[kT_])

                def stage_b2(idx):
                    s, c = items[idx]
                    first = (st == 0 and s == 0)
                    nkb = 1 if first else 2
                    g = c // 2
                    r = idx % NR
                    pTr, kT_ = pT[r], f"pT{r}"
                    ob = idx % 2
                    for j in range(2):
                        for kb in range(nkb):
                            vs = s + kb + (1 if first else 0)
                            P.op("pe", lambda e, j=j, kb=kb, vs=vs: e.matmul(PS[ob][64 * j:64 * j + 64, 0:128], lhsT=vtok[:, vs, g * 64:(g + 1) * 64], rhs=pTr[:, j * 2 + kb, :],
                                                                         start=(kb == 0), stop=(kb == nkb - 1)), reads=["vtok", kT_], writes=[psk(ob)])
                    P.op("act", lambda e: e.copy(out=oT[:, c, s * 128:(s + 1) * 128], in_=PS[ob][:, 0:128]), reads=[psk(ob)], writes=["oT"])

                stage_a1(0, True)
                stage_a1(0, False)
                stage_a1(1, True)
                stage_a1(1, False)
                for idx in range(len(items)):
                    if idx + 2 < len(items):
                        stage_a1(idx + 2, True)
                    stage_a2(idx)
                    stage_b1(idx)
                    if idx + 2 < len(items):
                        stage_a1(idx + 2, False)
                    stage_b2(idx)
                P.op("pool", lambda e: e.tensor_copy(out=kT2[:, :, 0:128], in_=kT2[:, :, T:T + 128]), reads=["kT2"], writes=["kT2"])
                P.op("pool", lambda e: e.tensor_copy(out=vtok[:, 0, :], in_=vtok[:, 4, :]), reads=["vtok"], writes=["vtok"])
                _chk(f'attn_b{b}_st{st}')
                nb_, nst_ = (b, st + 1) if st + 1 < SEQ // T else (b + 1, 0)
                nxt_ = src_p[nb_, nst_ * T:(nst_ + 1) * T, :] if nb_ < NB else None
                out_proj_store(128, 4, dst_p[b, st * T:(st + 1) * T, :], next_src=nxt_)
                _chk(f'store_b{b}_st{st}')
        P.barrier()
        A.release(m1)
        if not do_samples:
            A.release(m0)
            return
        qkv_s = A.alloc("qkv_s", [NS, 1536], F32)
        o_tok = A.alloc("o_tok", [NS, 1024], F32)
        o_tb = A.alloc("o_tb", [NS, 1024], BF16)
        Kbg = A.alloc("Kbg", [64, 128, 64], F32)
        Vbg = A.alloc("Vbg", [64, 128, 64], F32)
        T1 = A.alloc("T1s", [64, 128 * 64], F32)
        qbg = A.alloc("qbg", [64, 4, 64], F32)
        knbg = A.alloc("knbg", [64, 64], F32)
        vnbg = A.alloc("vnbg", [64, 64], F32)
        T2 = A.alloc("T2s", [64, 4, 64], F32)
        Sall = A.alloc("Sall", [64, 4, 129], F32)
        Pm = A.alloc("Pm", [64, 4, 129], F32)
        RELs = A.alloc("RELs", [64, 128], F32)
        nslope = A.alloc("nslope", [64, 4], F32)
        sinkbg = A.alloc("sinkbg", [64, 4], F32)
        nsink = A.alloc("nsink", [64, 4], F32)
        sms = A.alloc("sms", [64, 6, 4], F32)
        Oall = A.alloc("Oall", [64, 4, 64], F32)
        slp = A.alloc("slp", [1, 16], F32)

        P.op("pool", lambda e: e.iota(slp[:], pattern=[[1, 16]], base=1, channel_multiplier=0, allow_small_or_imprecise_dtypes=True), writes=["slp"])
        P.op("act", lambda e: e.activation(out=slp[:], in_=slp[:], func=AF.Exp, scale=-0.5 * float(np.log(2.0))), reads=["slp"], writes=["slp"])
        P.op("dve", lambda e: e.tensor_scalar(out=slp[:], in0=slp[:], scalar1=-1.0, scalar2=None, op0=ALU.mult), reads=["slp"], writes=["slp"])
        P.dma("sp", slope_d.rearrange("(o h) -> o h", o=1), slp[:], reads=["slp"], writes=["slope_d"])
        for g_ in range(4):
            P.dma("sp", nslope[16 * g_:16 * g_ + 16, :], slope_d[4 * g_:4 * g_ + 4].partition_broadcast(NS), reads=["slope_d"], writes=["nslope"])
            P.dma("act", sinkbg[16 * g_:16 * g_ + 16, :], sinks_c[4 * g_:4 * g_ + 4].partition_broadcast(NS), writes=["sinkbg"])
        P.op("dve", lambda e: e.tensor_scalar(out=nsink[:], in0=sinkbg[:], scalar1=-1.0, scalar2=None, op0=ALU.mult), reads=["sinkbg"], writes=["nsink"])
        P.op("pool", lambda e: e.iota(RELs[:], pattern=[[-1, 128]], base=128, channel_multiplier=0, allow_small_or_imprecise_dtypes=True), writes=["RELs"])

        P.dma("sp", s_swa_k[:, 0:127, :], cache_k[:, 1:128, :])
        P.dma("sp", s_swa_v[:, 0:127, :], cache_v[:, 1:128, :])
        for g_ in range(4):
            P.dma("sp" if g_ % 2 == 0 else "act", Kbg[16 * g_:16 * g_ + 16, :, :], cache_k[:, :, g_ * 64:(g_ + 1) * 64], writes=["Kbg"])
            P.dma("act" if g_ % 2 == 0 else "sp", Vbg[16 * g_:16 * g_ + 16, :, :], cache_v[:, :, g_ * 64:(g_ + 1) * 64], writes=["Vbg"])

        P.dma("sp", xt[:NS, 0, :], src_s, writes=["xt0"])
        norm_transpose(xt[:NS, 0, :], NS, "xt0", xn, junk, ss, rstd, xnT, "xnT", 0, "a", wrow)
        groups = [(wq, WQ_ALL, 0, 512, 0, 0.125), (wq, WQ_ALL, 512, 512, 512, 0.125), (wk, ["wk"], 0, 256, 1024, 1.0), (wv, ["wv"], 0, 256, 1280, 1.0)]
        for gi, (wt, wkey, c0, ncol, o0, scl) in enumerate(groups):
            bq = 2 + (gi % 2)
            for kc in range(8):
                P.op("pe", lambda e, wt=wt, c0=c0, ncol=ncol, kc=kc, bq=bq: e.matmul(PS[bq][:NS, 0:ncol], lhsT=xnT[:, kc, 0:NS], rhs=wt[:, kc, c0:c0 + ncol],
                                                                                 start=(kc == 0), stop=(kc == 7)), reads=wkey + ["xnT"], writes=[psk(bq)])
            P.op("act", lambda e, bq=bq, ncol=ncol, o0=o0, scl=scl: e.activation(out=qkv_s[:, o0:o0 + ncol], in_=PS[bq][:NS, 0:ncol], func=AF.Copy, scale=scl),
                 reads=[psk(bq)], writes=["qkv_s"])
        P.dma("sp", sq_q, qkv_s[:, 0:1024], reads=["qkv_s"], writes=["sq_q"])
        P.dma("sp", sq_k, qkv_s[:, 1024:1280], reads=["qkv_s"], writes=["sq_k"])
        P.dma("sp", sq_v, qkv_s[:, 1280:1536], reads=["qkv_s"], writes=["sq_v"])
        P.dma("sp", s_swa_k[:, 127, :], qkv_s[:, 1024:1280], reads=["qkv_s"])
        P.dma("sp", s_swa_v[:, 127, :], qkv_s[:, 1280:1536], reads=["qkv_s"])
        for g_ in range(4):
            P.dma("sp", qbg[16 * g_:16 * g_ + 16].rearrange("p r d -> p (r d)"), sq_q[:, g_ * 256:(g_ + 1) * 256], reads=["sq_q"], writes=["qbg"])
            P.dma("act", knbg[16 * g_:16 * g_ + 16, :], sq_k[:, g_ * 64:(g_ + 1) * 64], reads=["sq_k"], writes=["knbg"])
            P.dma("act", vnbg[16 * g_:16 * g_ + 16, :], sq_v[:, g_ * 64:(g_ + 1) * 64], reads=["sq_v"], writes=["vnbg"])

        T1k = T1[:].rearrange("p (k d) -> p k d", k=128)
        T1d = T1[:].rearrange("p (d k) -> p d k", d=64)
        for r in range(4):
            eng = "dve"
            P.op(eng, lambda e, r=r: e.tensor_tensor(out=T1k, in0=Kbg[:], in1=qbg[:, r, :].unsqueeze(1).to_broadcast([64, 128, 64]), op=ALU.mult),
                 reads=["Kbg", "qbg"], writes=["T1s"])
            P.op("dve", lambda e, r=r: e.tensor_reduce(out=Sall[:, r, 0:128], in_=T1k, axis=AX.X, op=ALU.add), reads=["T1s"], writes=["Sall"])
        P.op("dve", lambda e: e.tensor_tensor(out=T2[:], in0=qbg[:], in1=knbg[:].unsqueeze(1).to_broadcast([64, 4, 64]), op=ALU.mult),
             reads=["qbg", "knbg"], writes=["T2s"])
        P.op("dve", lambda e: e.tensor_reduce(out=Sall[:, :, 128], in_=T2[:], axis=AX.X, op=ALU.add), reads=["T2s", "Sall"], writes=["Sall"])
        for r in range(4):
            P.op("dve", lambda e, r=r: e.scalar_tensor_tensor(out=Sall[:, r, 0:128], in0=RELs[:], scalar=nslope[:, r:r + 1], in1=Sall[:, r, 0:128],
                                                             op0=ALU.mult, op1=ALU.add), reads=["RELs", "nslope", "Sall"], writes=["Sall"])
        P.op("dve", lambda e: e.tensor_reduce(out=sms[:, 0, :], in_=Sall[:], axis=AX.X, op=ALU.max), reads=["Sall"], writes=["sms0"])
        P.op("dve", lambda e: e.scalar_tensor_tensor(out=sms[:, 1, :], in0=sms[:, 0, :], scalar=-1.0, in1=nsink[:], op0=ALU.mult, op1=ALU.min),
             reads=["sms0", "nsink"], writes=["sms1"])
        for r in range(4):
            P.op("act", lambda e, r=r: e.activation(out=Pm[:, r, :], in_=Sall[:, r, :], func=AF.Exp, bias=sms[:, 1, r:r + 1], accum_out=sms[:, 2, r:r + 1]),
                 reads=["Sall", "sms1"], writes=["Pm", "sms2"])
        P.op("dve", lambda e: e.tensor_tensor(out=sms[:, 3, :], in0=sinkbg[:], in1=sms[:, 1, :], op=ALU.add), reads=["sinkbg", "sms1"], writes=["sms3"])
        P.op("act", lambda e: e.activation(out=sms[:, 3, :], in_=sms[:, 3, :], func=AF.Exp), reads=["sms3"], writes=["sms3"])
        P.op("dve", lambda e: e.tensor_tensor(out=sms[:, 4, :], in0=sms[:, 2, :], in1=sms[:, 3, :], op=ALU.add), reads=["sms2", "sms3"], writes=["sms4"])
        P.op("dve", lambda e: e.reciprocal(out=sms[:, 5, :], in_=sms[:, 4, :]), reads=["sms4"], writes=["sms5"])
        Vperm = Vbg[:].rearrange("p k d -> p d k")
        for r in range(4):
            eng = "dve"
            P.op(eng, lambda e, r=r: e.tensor_tensor(out=T1d, in0=Vperm, in1=Pm[:, r, 0:128].unsqueeze(1).to_broadcast([64, 64, 128]), op=ALU.mult),
                 reads=["Vbg", "Pm"], writes=["T1s"])
            P.op("dve", lambda e, r=r: e.tensor_reduce(out=Oall[:, r, :], in_=T1d, axis=AX.X, op=ALU.add), reads=["T1s"], writes=["Oall"])
            P.op("dve", lambda e, r=r: e.scalar_tensor_tensor(out=Oall[:, r, :], in0=vnbg[:], scalar=Pm[:, r, 128:129], in1=Oall[:, r, :], op0=ALU.mult, op1=ALU.add),
                 reads=["vnbg", "Pm", "Oall"], writes=["Oall"])
            P.op("dve", lambda e, r=r: e.tensor_scalar(out=Oall[:, r, :], in0=Oall[:, r, :], scalar1=sms[:, 5, r:r + 1], scalar2=None, op0=ALU.mult),
                 reads=["Oall", "sms5"], writes=["Oall"])
        for g_ in range(4):
            P.dma("sp", so_s[:, g_ * 256:(g_ + 1) * 256], Oall[16 * g_:16 * g_ + 16].rearrange("p r d -> p (r d)"), reads=["Oall"], writes=["so_s"])
        P.dma("sp", o_tok[:], so_s, reads=["so_s"], writes=["o_tok"])
        transpose_to_fm(o_tok[:, :], NS, "o_tok", oT, "oT", 8, o_tb, "o_tb", 0)
        out_proj_store(NS, 1, dst_s)
        P.barrier()
        A.release(m0)

    if stages == "ffn0":
        ffn_stage(0, x_prompt, x_sample, y_prompt, y_sample, final=False)
    elif stages == "attn":
        attn_stage(x_prompt, x_sample, y_prompt, y_sample)
    elif stages == "attn_p":
        try:
            attn_stage(x_prompt, x_sample, y_prompt, y_sample, do_samples=False)
        except _Stop:
            pass
    elif stages == "attn_s":
        attn_stage(x_prompt, x_sample, y_prompt, y_sample, do_prompt=False)
    elif stages == "mix0":
        mix0_stage(x_prompt, x_sample, y_prompt, y_sample)
    elif stages == "mix0_p":
        try:
            mix0_stage(x_prompt, x_sample, y_prompt, y_sample, do_samples=False)
        except _Stop:
            pass
    else:
        mix0_stage(x_prompt, x_sample, xa_p, xa_s)
        ffn_stage(0, xa_p, xa_s, xb_p, xb_s, final=False)
        attn_stage(xb_p, xb_s, xa_p, xa_s)
        ffn_stage(1, xa_p, xa_s, y_prompt, y_sample, final=True)

    P.emit()
    return nc


_NC_CACHE = {}

OUT_NAMES = ["y_prompt", "y_sample", "p_gdn", "p_gdn_conv", "p_lru", "p_lru_conv", "p_swa_k", "p_swa_v",
             "s_gdn", "s_gdn_conv", "s_lru", "s_lru_conv", "s_swa_k", "s_swa_v"]


def make_in_maps(inputs, cores):
    f = lambda a: np.ascontiguousarray(np.asarray(a, dtype=np.float32))
    maps = []
    for c in cores:
        pb = slice(c * NB, (c + 1) * NB)
        sb = slice(c * NS, (c + 1) * NS)
        m = {
            "x_prompt": f(inputs["x_prompt"][pb]),
            "x_sample": f(inputs["x_sample"][sb].reshape(NS, D)),
            "state_gdn": f(inputs["state_gdn"][0, sb]),
            "state_gdn_conv": f(inputs["state_gdn_conv"][0, sb]),
            "state_lru": f(inputs["state_lru"][0, sb]),
            "state_lru_conv": f(inputs["state_lru_conv"][0, sb]),
            "cache_swa_k": f(inputs["cache_swa_k"][0, sb].reshape(NS, 128, 256)),
            "cache_swa_v": f(inputs["cache_swa_v"][0, sb].reshape(NS, 128, 256)),
            "norm_mix": f(inputs["norm_mix"]),
            "norm_ffn": f(inputs["norm_ffn"]),
            "norm_final": f(inputs["norm_final"]),
            "w_in_ab": f(inputs["w_in_ab"][0]),
            "conv_gdn_w": f(inputs["conv_gdn_w"][0]),
            "gdn_a_log": f(inputs["gdn_a_log"][0]),
            "gdn_dt_bias": f(inputs["gdn_dt_bias"][0]),
            "gdn_norm_w": f(inputs["gdn_norm_w"][0]),
            "conv_lru_w": f(inputs["conv_lru_w"][0]),
            "conv_lru_b": f(inputs["conv_lru_b"][0]),
            "lru_wa": f(inputs["lru_wa"][0]),
            "lru_ba": f(inputs["lru_ba"][0]),
            "lru_wx": f(inputs["lru_wx"][0]),
            "lru_bx": f(inputs["lru_bx"][0]),
            "lru_lambda": f(inputs["lru_lambda"][0]),
            "w_out_ab": f(inputs["w_out_ab"][0]),
            "w_qkv_c": f(inputs["w_qkv_c"][0]),
            "w_out_c": f(inputs["w_out_c"][0]),
            "sinks_c": f(inputs["sinks_c"][0]),
            "w_gate_up": f(inputs["w_gate_up"]),
            "w_down": f(inputs["w_down"]),
        }
        maps.append(m)
    return maps


def assemble(results):
    cat = lambda n: np.concatenate([np.asarray(r[n]) for r in results], axis=0)
    nb = NB * len(results)
    ns = NS * len(results)
    return (
        cat("y_prompt"),
        cat("y_sample").reshape(ns, 1, D),
        cat("p_gdn")[None],
        cat("p_gdn_conv")[None],
        cat("p_lru")[None],
        cat("p_lru_conv")[None],
        cat("p_swa_k").reshape(1, nb, 128, 4, 64),
        cat("p_swa_v").reshape(1, nb, 128, 4, 64),
        cat("s_gdn")[None],
        cat("s_gdn_conv")[None],
        cat("s_lru")[None],
        cat("s_lru_conv")[None],
        cat("s_swa_k").reshape(1, ns, 128, 4, 64),
        cat("s_swa_v").reshape(1, ns, 128, 4, 64),
    )


def kernel(**inputs):
    if "nc" not in _NC_CACHE:
        _NC_CACHE["nc"] = build_program()
    nc = _NC_CACHE["nc"]
    in_maps = make_in_maps(inputs, list(range(NCORES)))
    res = run_bass_kernel_spmd(nc, in_maps, core_ids=list(range(NCORES)))
    return assemble(res.results)
```

```python
import numpy as np
import concourse.bass as bass
import concourse.mybir as mybir
from concourse.bass_utils import run_bass_kernel_spmd

F32 = mybir.dt.float32
BF16 = mybir.dt.bfloat16
AF = mybir.ActivationFunctionType
ALU = mybir.AluOpType
AX = mybir.AxisListType

NCORES = 8
D = 1024
SEQ = 2048
NB = 2
NS = 16
DFF = 2816
NFC = DFF // 128
T = 512
DBG = {}


class _Stop(Exception):
    pass


def _chk(tag):
    if DBG.get('stop') == tag:
        raise _Stop()
EPS = 1e-6
W_IN = 3080


class Prog:
    SEM_MAX = 60000
    ENG = ("sp", "act", "dve", "pool", "pe")

    def __init__(self, nc):
        self.nc = nc
        self.q = {n: [] for n in self.ENG}
        self.semh = {}
        self.cur = {}
        self.cnt = {}
        self.gen = {n: 0 for n in self.ENG}
        for n in self.ENG:
            self._new_eng_sem(n)
        self.waited = {n: {} for n in self.ENG}
        self.lastw = {}
        self.readers = {}
        self.ndma = 12
        self.dma_slots = {}
        self.dma_rr = {}
        self.dma_val = {}
        for n in ("sp", "act", "pool"):
            names = []
            for i in range(self.ndma):
                nm = f"d_{n}_{i}"
                self.semh[nm] = nc.alloc_semaphore(nm)
                self.dma_val[nm] = 0
                names.append(nm)
            self.dma_slots[n] = names
            self.dma_rr[n] = 0

    def _new_eng_sem(self, n):
        nm = f"e_{n}_{self.gen[n]}"
        self.gen[n] += 1
        self.semh[nm] = self.nc.alloc_semaphore(nm)
        self.cur[n] = nm
        self.cnt[n] = 0

    def _deps(self, reads, writes):
        deps = {}

        def add(t):
            if t is not None and deps.get(t[0], 0) < t[1]:
                deps[t[0]] = t[1]
        for k in reads:
            add(self.lastw.get(k))
        for k in writes:
            add(self.lastw.get(k))
            for s, v in self.readers.get(k, {}).items():
                add((s, v))
        return deps

    def _filter(self, eng, deps):
        waits = []
        w = self.waited[eng]
        for s, v in deps.items():
            if eng == "pe" and s.startswith("e_pe_"):
                continue
            if w.get(s, 0) >= v:
                continue
            w[s] = v
            waits.append((s, v))
        return waits

    def _commit(self, tok, reads, writes):
        for k in reads:
            r = self.readers.setdefault(k, {})
            if r.get(tok[0], 0) < tok[1]:
                r[tok[0]] = tok[1]
        for k in writes:
            self.lastw[k] = tok
            self.readers[k] = {}

    def op(self, eng, fn, reads=(), writes=()):
        writes = list(writes) + [k for k in reads if k.startswith("ps") and k not in writes]
        deps = self._deps(reads, writes)
        waits = self._filter(eng, deps)
        if self.cnt[eng] >= self.SEM_MAX:
            self._new_eng_sem(eng)
        self.cnt[eng] += 1
        tok = (self.cur[eng], self.cnt[eng])
        self.q[eng].append((waits, fn, tok, 1))
        self._commit(tok, reads, writes)

    def dma(self, eng, out, in_, reads=(), writes=(), **kw):
        deps = self._deps(reads, writes)
        slot = self.dma_slots[eng][self.dma_rr[eng] % self.ndma]
        self.dma_rr[eng] += 1
        if self.dma_val[slot] >= self.SEM_MAX:
            raise RuntimeError("dma sem overflow")
        if self.dma_val[slot] > 0:
            if deps.get(slot, 0) < self.dma_val[slot]:
                deps[slot] = self.dma_val[slot]
        waits = self._filter(eng, deps)
        self.dma_val[slot] += 16
        tok = (slot, self.dma_val[slot])
        self.q[eng].append((waits, lambda e: e.dma_start(out=out, in_=in_, **kw), tok, 16))
        self._commit(tok, reads, writes)

    def barrier(self):
        deps = {}
        for n in self.ENG:
            if self.cnt[n] > 0:
                deps[self.cur[n]] = self.cnt[n]
        for s, v in self.dma_val.items():
            if v > 0:
                deps[s] = v
        for n in self.ENG:
            waits = self._filter(n, dict(deps))
            if waits:
                self.q[n].append((waits, None, None, 0))
        self.lastw = {}
        self.readers = {}

    def emit(self):
        self.barrier()
        nc = self.nc
        with nc.Block() as block:
            decos = {"sp": block.sync, "act": block.scalar, "dve": block.vector, "pool": block.gpsimd, "pe": block.tensor}
            for n in self.ENG:
                def body(e, n=n):
                    for waits, fn, tok, inc in self.q[n]:
                        for s, v in waits:
                            e.wait_ge(self.semh[s], v)
                        if fn is not None:
                            ins = fn(e)
                            ins.then_inc(self.semh[tok[0]], inc)
                decos[n](body)


class Arena:
    def __init__(self, nc, lo=16640, hi=229000):
        self.nc = nc
        self.lo = lo
        self.hi = hi
        self.off = lo
        self.n = 0

    def alloc(self, name, shape, dtype):
        per = 1
        for s in shape[1:]:
            per *= s
        nbytes = per * (2 if dtype == BF16 else 4)
        off = (self.off + 63) // 64 * 64
        if off + nbytes > self.hi:
            raise RuntimeError(f"SBUF arena overflow allocating {name}: {off}+{nbytes}")
        self.off = off + nbytes
        self.n += 1
        return self.nc.alloc_sbuf_tensor_at(f"{name}_{self.n}", list(shape), dtype, offset=off)

    def mark(self):
        return self.off

    def release(self, m):
        self.off = m


def build_program(stages=None):
    nc = bass.Bass("TRN2", target_bir_lowering=False)
    P = Prog(nc)
    A = Arena(nc)

    def din(name, shape):
        return nc.dram_tensor(name, list(shape), F32, kind="ExternalInput").ap()

    def dout(name, shape):
        return nc.dram_tensor(name, list(shape), F32, kind="ExternalOutput").ap()

    def dscr(name, shape):
        return nc.dram_tensor(name, list(shape), F32, kind="Internal").ap()

    x_prompt = din("x_prompt", [NB, SEQ, D])
    x_sample = din("x_sample", [NS, D])
    state_gdn = din("state_gdn", [NS, 4, 128, 128])
    state_gdn_conv = din("state_gdn_conv", [NS, 3, 1536])
    state_lru = din("state_lru", [NS, 512])
    state_lru_conv = din("state_lru_conv", [NS, 3, 512])
    cache_k = din("cache_swa_k", [NS, 128, 256])
    cache_v = din("cache_swa_v", [NS, 128, 256])
    norm_mix = din("norm_mix", [2, D])
    norm_ffn = din("norm_ffn", [2, D])
    norm_final = din("norm_final", [D])
    w_in_ab = din("w_in_ab", [D, W_IN])
    conv_gdn_w = din("conv_gdn_w", [4, 1536])
    gdn_a_log = din("gdn_a_log", [4])
    gdn_dt_bias = din("gdn_dt_bias", [4])
    gdn_norm_w = din("gdn_norm_w", [128])
    conv_lru_w = din("conv_lru_w", [4, 512])
    conv_lru_b = din("conv_lru_b", [512])
    lru_wa = din("lru_wa", [8, 64, 64])
    lru_ba = din("lru_ba", [512])
    lru_wx = din("lru_wx", [8, 64, 64])
    lru_bx = din("lru_bx", [512])
    lru_lambda = din("lru_lambda", [512])
    w_out_ab = din("w_out_ab", [D, D])
    w_qkv_c = din("w_qkv_c", [D, 1536])
    w_out_c = din("w_out_c", [D, D])
    sinks_c = din("sinks_c", [16])
    w_gate_up = din("w_gate_up", [2, D, 2 * DFF])
    w_down = din("w_down", [2, DFF, D])

    y_prompt = dout("y_prompt", [NB, SEQ, D])
    y_sample = dout("y_sample", [NS, D])
    p_gdn = dout("p_gdn", [NB, 4, 128, 128])
    p_gdn_conv = dout("p_gdn_conv", [NB, 3, 1536])
    p_lru = dout("p_lru", [NB, 512])
    p_lru_conv = dout("p_lru_conv", [NB, 3, 512])
    p_swa_k = dout("p_swa_k", [NB, 128, 256])
    p_swa_v = dout("p_swa_v", [NB, 128, 256])
    s_gdn = dout("s_gdn", [NS, 4, 128, 128])
    s_gdn_conv = dout("s_gdn_conv", [NS, 3, 1536])
    s_lru = dout("s_lru", [NS, 512])
    s_lru_conv = dout("s_lru_conv", [NS, 3, 512])
    s_swa_k = dout("s_swa_k", [NS, 128, 256])
    s_swa_v = dout("s_swa_v", [NS, 128, 256])

    xa_p = dscr("xa_p", [NB, SEQ, D])
    xb_p = dscr("xb_p", [NB, SEQ, D])
    xa_s = dscr("xa_s", [NS, D])
    xb_s = dscr("xb_s", [NS, D])
    sq_q = dscr("sq_q", [NS, 1024])
    sq_k = dscr("sq_k", [NS, 256])
    sq_v = dscr("sq_v", [NS, 256])
    so_s = dscr("so_s", [NS, 1024])
    slope_d = dscr("slope_d", [16])
    sg_qk = dscr("sg_qk", [NS * 4, 256])
    sg_v = dscr("sg_v", [NS, 512])
    sg_bg = dscr("sg_bg", [NS * 4, 2])
    so_g = dscr("so_g", [NS, 512])

    PS = [nc.alloc_psum_tensor(f"ps{i}", [128, 512], F32) for i in range(8)]

    def psk(i):
        return f"ps{i}"

    identf = A.alloc("identf", [128, 128], F32)
    identb = A.alloc("identb", [128, 128], BF16)
    ones_f = A.alloc("ones_f", [128, 128], F32)
    P.op("pool", lambda e: e.memset(ones_f[:], 1.0), writes=["ones_f"])
    P.op("pool", lambda e: e.affine_select(out=identf[:], in_=ones_f[:], pattern=[[-1, 128]], compare_op=ALU.is_equal,
                                           fill=0.0, base=0, channel_multiplier=1), reads=["ones_f"], writes=["identf"])
    P.op("dve", lambda e: e.tensor_copy(out=identb[:], in_=identf[:]), reads=["identf"], writes=["identb"])

    ctx = dict(nc=nc, P=P, A=A, PS=PS)

    def load_weight(dst, dst_key, src_rows, ncols, kchunks, stage, scale_tile=None, scale_key=None, col0=0):
        piece = int(stage[0].shape[1])
        i = 0
        for kc in range(kchunks):
            for c0 in range(0, ncols, piece):
                c1 = min(ncols, c0 + piece)
                st = stage[i % 2]
                sk = f"wstage{i % 2}"
                i += 1
                src = src_rows(kc)[:, col0 + c0:col0 + c1]
                P.dma("sp", st[:, 0:c1 - c0], src, writes=[sk])
                eng = ("dve", "pool", "act")[i % 3] if scale_tile is None else ("dve", "pool")[i % 2]
                o = dst[:, kc, c0:c1]
                s_in = st[:, 0:c1 - c0]
                if scale_tile is None:
                    if eng == "act":
                        P.op("act", lambda e, o=o, s_in=s_in: e.copy(out=o, in_=s_in), reads=[sk], writes=[dst_key])
                    else:
                        P.op(eng, lambda e, o=o, s_in=s_in: e.tensor_copy(out=o, in_=s_in), reads=[sk], writes=[dst_key])
                else:
                    sc = scale_tile[:, kc:kc + 1]
                    P.op(eng, lambda e, o=o, s_in=s_in, sc=sc: e.tensor_scalar(out=o, in0=s_in, scalar1=sc, scalar2=None, op0=ALU.mult),
                         reads=[sk, scale_key], writes=[dst_key])

    def wload(dst_ap, src_ap, key):
        P.dma("pool", dst_ap, src_ap, writes=[key])

    def load_colvec(dst, dst_key, src_1d, nchunk):
        P.dma("sp", dst[:, 0:nchunk], src_1d.rearrange("(c p) -> p c", p=128), writes=[dst_key], allow_slow_non_contiguous=True)

    def rms_rstd(xt_ap, rows, junk, ss, rstd, xkey, tag):
        P.op("act", lambda e: e.activation(out=junk[:rows, :], in_=xt_ap, func=AF.Square, accum_out=ss[:rows, :]),
             reads=[xkey], writes=[f"junk{tag}", f"ss{tag}"])
        P.op("act", lambda e: e.activation(out=ss[:rows, :], in_=ss[:rows, :], func=AF.Sqrt, scale=1.0 / D, bias=EPS),
             reads=[f"ss{tag}"], writes=[f"ss{tag}"])
        P.op("dve", lambda e: e.reciprocal(out=rstd[:rows, :], in_=ss[:rows, :]), reads=[f"ss{tag}"], writes=[f"rstd{tag}"])

    def norm_transpose(xt_ap, rows, xkey, xn, junk, ss, rstd, xnT, xnT_key, col0, tag, wrow, banks=(0, 1)):
        rms_rstd(xt_ap, rows, junk, ss, rstd, xkey, tag)
        P.op("dve", lambda e: e.scalar_tensor_tensor(out=xn[:rows, :], in0=xt_ap, scalar=rstd[:rows, :], in1=wrow[:rows, :], op0=ALU.mult, op1=ALU.mult),
             reads=[xkey, f"rstd{tag}", "wrow"], writes=[f"xn{tag}"])
        for half in range(2):
            b = banks[half]
            pb = PS[b][:].bitcast(BF16)
            for j in range(4):
                kc = half * 4 + j
                P.op("pe", lambda e, kc=kc, j=j, pb=pb: e.transpose(out=pb[:, j * 128:j * 128 + rows], in_=xn[:rows, kc * 128:(kc + 1) * 128],
                                                                   identity=identb[:rows, :rows]),
                     reads=[f"xn{tag}", "identb"], writes=[psk(b)])
            src = pb[:, 0:512].rearrange("p (j t) -> p j t", j=4)[:, :, 0:rows]
            dst = xnT[:, half * 4:half * 4 + 4, col0:col0 + rows]
            if half == 0:
                P.op("act", lambda e, src=src, dst=dst: e.copy(out=dst, in_=src), reads=[psk(b)], writes=[xnT_key])
            else:
                P.op("dve", lambda e, src=src, dst=dst: e.tensor_copy(out=dst, in_=src), reads=[psk(b)], writes=[xnT_key])

    def ffn_stage(li, src_p, src_s, dst_p, dst_s, final):
        m0 = A.mark()
        wgu = A.alloc("wgu", [128, 8, 2 * DFF], BF16)
        wdn = A.alloc("wdn", [128, NFC, D], BF16)
        wrow = A.alloc("wrow", [128, D], F32)
        xt = A.alloc("xt", [128, 4, D], F32)
        xn = A.alloc("xn", [128, D], BF16)
        junk = A.alloc("junk", [128, D], BF16)
        ss = A.alloc("ss", [128, 1], F32)
        rstd = A.alloc("rstd", [128, 1], F32)
        xnT = A.alloc("xnT", [128, 8, T], BF16)
        actT = A.alloc("actT", [128, NFC, T], BF16)
        sg = [A.alloc(f"sg{i}", [128, T], F32) for i in range(2)]
        wfin = A.alloc("wfin", [128, D], F32) if final else None
        P.dma("sp", wrow[:], norm_ffn[li].partition_broadcast(128), writes=["wrow"])
        for fc in range(NFC):
            wload(wgu[:, :, fc * 128:(fc + 1) * 128], w_gate_up[li][:, fc * 128:(fc + 1) * 128].rearrange("(kc p) c -> p kc c", p=128), f"wg{fc}")
            wload(wgu[:, :, DFF + fc * 128:DFF + (fc + 1) * 128], w_gate_up[li][:, DFF + fc * 128:DFF + (fc + 1) * 128].rearrange("(kc p) c -> p kc c", p=128), f"wu{fc}")
        for fc in range(NFC):
            wload(wdn[:, fc, :], w_down[li][fc * 128:(fc + 1) * 128, :], f"wd{fc}")
        if final:
            P.dma("sp", wfin[:], norm_final.partition_broadcast(128), writes=["wfin"])

        def load_sub(src_ap, rows, s):
            P.dma("sp", xt[:rows, s, :], src_ap[s * rows:(s + 1) * rows, :], writes=[f"xt{s}"])

        def do_tile(src_ap, dst_ap, rows, nsub, next_src=None, next_rows=None, next_nsub=0, preloaded=False):
            ntok = rows * nsub
            if not preloaded:
                for s in range(nsub):
                    load_sub(src_ap, rows, s)
            xkeys = [f"xnT{s}" for s in range(nsub)]
            for s in range(nsub):
                norm_transpose(xt[:rows, s, :], rows, f"xt{s}", xn, junk, ss, rstd, xnT, f"xnT{s}", s * rows, "f", wrow)
            for fc in range(NFC):
                bg = 2 + (fc % 2)
                bu = 4 + (fc % 2)
                for kc in range(8):
                    P.op("pe", lambda e, fc=fc, kc=kc, bg=bg: e.matmul(PS[bg][:, 0:ntok], lhsT=wgu[:, kc, fc * 128:(fc + 1) * 128],
                                                                      rhs=xnT[:, kc, 0:ntok], start=(kc == 0), stop=(kc == 7)),
                         reads=[f"wg{fc}"] + xkeys, writes=[psk(bg)])
                for kc in range(8):
                    P.op("pe", lambda e, fc=fc, kc=kc, bu=bu: e.matmul(PS[bu][:, 0:ntok], lhsT=wgu[:, kc, DFF + fc * 128:DFF + (fc + 1) * 128],
                                                                      rhs=xnT[:, kc, 0:ntok], start=(kc == 0), stop=(kc == 7)),
                         reads=[f"wu{fc}"] + xkeys, writes=[psk(bu)])
                sgt = sg[fc % 2]
                sgk = f"sg{fc % 2}"
                P.op("act", lambda e, bg=bg, sgt=sgt: e.activation(out=sgt[:, 0:ntok], in_=PS[bg][:, 0:ntok], func=AF.Silu),
                     reads=[psk(bg)], writes=[sgk])
                P.op("dve", lambda e, bu=bu, sgt=sgt, fc=fc: e.tensor_tensor(out=actT[:, fc, 0:ntok], in0=sgt[:, 0:ntok], in1=PS[bu][:, 0:ntok], op=ALU.mult),
                     reads=[sgk, psk(bu)], writes=["actT"])
            i = 0
            for s in range(nsub):
                xk = f"xt{s}"
                for half in range(2):
                    bd = 6 + (i % 2)
                    i += 1
                    for fc in range(NFC):
                        P.op("pe", lambda e, fc=fc, s=s, half=half, bd=bd: e.matmul(PS[bd][:rows, :], lhsT=actT[:, fc, s * rows:(s + 1) * rows],
                                                                                  rhs=wdn[:, fc, half * 512:(half + 1) * 512],
                                                                                  start=(fc == 0), stop=(fc == NFC - 1)),
                             reads=["actT", f"wd{fc}"], writes=[psk(bd)])
                    P.op("dve", lambda e, s=s, half=half, bd=bd: e.tensor_tensor(out=xt[:rows, s, half * 512:(half + 1) * 512],
                                                                                in0=xt[:rows, s, half * 512:(half + 1) * 512],
                                                                                in1=PS[bd][:rows, :], op=ALU.add),
                         reads=[psk(bd), xk], writes=[xk])
                if final:
                    rms_rstd(xt[:rows, s, :], rows, junk, ss, rstd, xk, "f")
                    P.op("dve", lambda e, s=s: e.scalar_tensor_tensor(out=xt[:rows, s, :], in0=xt[:rows, s, :], scalar=rstd[:rows, :],
                                                                     in1=wfin[:rows, :], op0=ALU.mult, op1=ALU.mult),
                         reads=[xk, "rstdf", "wfin"], writes=[xk])
                P.dma("sp", dst_ap[s * rows:(s + 1) * rows, :], xt[:rows, s, :], reads=[xk])
                if next_src is not None and s < next_nsub:
                    load_sub(next_src, next_rows, s)

        tiles = [(src_p[b, st * T:(st + 1) * T, :], dst_p[b, st * T:(st + 1) * T, :], 128, 4) for b in range(NB) for st in range(SEQ // T)]
        tiles.append((src_s, dst_s, NS, 1))
        for ti, (sa, da, rows_, nsub_) in enumerate(tiles):
            nxt = tiles[ti + 1] if ti + 1 < len(tiles) else None
            do_tile(sa, da, rows_, nsub_, next_src=(nxt[0] if nxt else None), next_rows=(nxt[2] if nxt else None),
                    next_nsub=(nxt[3] if nxt else 0), preloaded=(ti > 0))
        P.barrier()
        A.release(m0)


    def mix0_stage(src_p, src_s, dst_p, dst_s, do_samples=True, do_prompt=True):
        m0 = A.mark()
        win = A.alloc("win", [128, 8, W_IN], BF16)
        wout = A.alloc("wout", [128, 8, D], BF16)
        wrow = A.alloc("wrow", [128, D], F32)
        cwg = A.alloc("cwg", [128, 12, 4], F32)
        cwl = A.alloc("cwl", [128, 4, 4], F32)
        cbl = A.alloc("cbl", [128, 4], F32)
        bab = A.alloc("bab", [128, 4], F32)
        bxb = A.alloc("bxb", [128, 4], F32)
        lam = A.alloc("lam", [128, 4], F32)
        cl = A.alloc("cl", [128, 4], F32)
        cl2 = A.alloc("cl2", [128, 4], F32)
        WAbd = A.alloc("WAbd", [128, 4, 128], F32)
        WXbd = A.alloc("WXbd", [128, 4, 128], F32)
        gnw = A.alloc("gnw", [128, 1], F32)
        alog = A.alloc("alog", [4, 1], F32)
        dtb = A.alloc("dtb", [4, 1], F32)
        nega = A.alloc("nega", [4, 1], F32)
        TriU = A.alloc("TriU", [128, 128], F32)
        MASKB = A.alloc("MASKB", [128, 128], F32)
        MSTR = A.alloc("MSTR", [128, 128], F32)
        sel4 = A.alloc("sel4", [4, 4, 128], F32)
        BDm = A.alloc("BDm", [128, 128], F32)
        NBD = A.alloc("NBD", [128, 128], F32)
        xn = A.alloc("xn", [128, D], BF16)
        junk = A.alloc("junk", [128, D], BF16)
        ss = A.alloc("ss", [128, 1], F32)
        rstd = A.alloc("rstd", [128, 1], F32)
        m1 = A.mark()
        xt = A.alloc("xt", [128, 4, D], F32)
        xnT = A.alloc("xnT", [128, 8, T], BF16)
        mixT = A.alloc("mixT", [128, 8, T], BF16)
        Hg = A.alloc("Hg", [128, 12, 3], F32)
        Hl = A.alloc("Hl", [128, 4, 3], F32)
        ones_r = A.alloc("ones_r", [128, 128], F32)
        Sst = A.alloc("Sst", [128, 4, 128], F32)
        hst = A.alloc("hst", [128, 4], F32)
        wbufs = {}
        SMALL = {"bt": 16, "gt": 16, "gam": 16, "ngam": 16, "egam": 16, "cfk": 16, "u0": 128, "u1": 128, "Sr": 128}

        def wb(name, dtype=F32, n=None):
            if name not in wbufs:
                wbufs[name] = A.alloc(name, [128, n or SMALL.get(name, 512)], dtype)
            return wbufs[name]

        def v3(t):
            return t[:].rearrange("p (s i) -> p s i", s=4)

        P.dma("sp", wrow[:], norm_mix[0].partition_broadcast(128), writes=["wrow"])

        def winkey(col0):
            return "win2048" if 2048 <= col0 < 2056 else f"win{col0}"
        WIN_ALL = []

        def win_unit(col0, n=128):
            wload(win[:, :, col0:col0 + n], w_in_ab[:, col0:col0 + n].rearrange("(kc p) c -> p kc c", p=128), winkey(col0))
            WIN_ALL.append(winkey(col0))
        for c in range(4):
            win_unit(2056 + c * 128)
            win_unit(2568 + c * 128)
        win_unit(2048, 8)
        for h in range(4):
            for base in (0, 512, 1024, 1536):
                win_unit(base + h * 128)
        for c in range(8):
            wload(wout[:, c, :], w_out_ab[c * 128:(c + 1) * 128, :], f"wo{c}")
        for c in range(12):
            P.dma("sp", cwg[:, c, :], conv_gdn_w[:, c * 128:(c + 1) * 128].rearrange("j p -> p j"), writes=["cwg"], allow_slow_non_contiguous=True)
        for c in range(4):
            P.dma("sp", cwl[:, c, :], conv_lru_w[:, c * 128:(c + 1) * 128].rearrange("j p -> p j"), writes=["cwl"], allow_slow_non_contiguous=True)
        load_colvec(cbl, "cbl", conv_lru_b, 4)
        load_colvec(bab, "bab", lru_ba, 4)
        load_colvec(bxb, "bxb", lru_bx, 4)
        load_colvec(lam, "lam", lru_lambda, 4)
        load_colvec(gnw, "gnw", gdn_norm_w, 1)
        P.dma("sp", alog[:], gdn_a_log.rearrange("(p o) -> p o", o=1), writes=["alog"])
        P.dma("sp", dtb[:], gdn_dt_bias.rearrange("(p o) -> p o", o=1), writes=["dtb"])
        P.op("act", lambda e: e.activation(out=nega[:], in_=alog[:], func=AF.Exp), reads=["alog"], writes=["nega"])
        P.op("dve", lambda e: e.tensor_scalar(out=nega[:], in0=nega[:], scalar1=-1.0, scalar2=None, op0=ALU.mult), reads=["nega"], writes=["nega"])
        P.op("act", lambda e: e.activation(out=cl[:], in_=lam[:], func=AF.Exp, scale=-1.0), reads=["lam"], writes=["cl"])
        P.op("act", lambda e: e.activation(out=cl[:], in_=cl[:], func=AF.Ln, bias=1.0), reads=["cl"], writes=["cl"])
        P.op("dve", lambda e: e.tensor_scalar(out=cl2[:], in0=cl[:], scalar1=-16.0, scalar2=None, op0=ALU.mult), reads=["cl"], writes=["cl2"])
        P.op("dve", lambda e: e.tensor_scalar(out=cl[:], in0=cl[:], scalar1=-8.0, scalar2=None, op0=ALU.mult), reads=["cl", "cl2"], writes=["cl"])
        P.op("pool", lambda e: e.memset(WAbd[:], 0.0), writes=["WAbd"])
        P.op("pool", lambda e: e.memset(WXbd[:], 0.0), writes=["WXbd"])
        for n in range(8):
            c, o = n // 2, 64 * (n % 2)
            P.dma("sp", WAbd[o:o + 64, c, o:o + 64], lru_wa[n], writes=["WAbd"])
            P.dma("sp", WXbd[o:o + 64, c, o:o + 64], lru_wx[n], writes=["WXbd"])
        P.op("pool", lambda e: e.affine_select(out=TriU[:], in_=ones_f[:], pattern=[[1, 128]], compare_op=ALU.is_ge, fill=0.0, base=0,
                                               channel_multiplier=-1), reads=["ones_f"], writes=["TriU"])
        P.op("pool", lambda e: e.memset(MASKB[:], 0.0), writes=["MASKB"])
        P.op("pool", lambda e: e.affine_select(out=MASKB[:], in_=MASKB[:], pattern=[[1, 128]], compare_op=ALU.is_ge, fill=NEG, base=0,
                                               channel_multiplier=-1), reads=["MASKB"], writes=["MASKB"])
        P.op("pool", lambda e: e.affine_select(out=MSTR[:], in_=ones_f[:], pattern=[[1, 128]], compare_op=ALU.is_ge, fill=0.0, base=-1,
                                               channel_multiplier=-1), reads=["ones_f"], writes=["MSTR"])
        P.op("dve", lambda e: e.tensor_copy(out=sel4[:], in_=identf[0:4, 0:4].unsqueeze(2).to_broadcast([4, 4, 128])), reads=["identf"], writes=["sel4"])
        P.op("dve", lambda e: e.tensor_copy(out=ones_r[:].bitcast(mybir.dt.float32r), in_=ones_f[:]), reads=["ones_f"], writes=["ones_r"])
        P.op("pool", lambda e: e.memset(BDm[:], 0.0), writes=["BDm"])
        P.op("pool", lambda e: e.memset(BDm[0:64, 0:64], 1.0), reads=["BDm"], writes=["BDm"])
        P.op("pool", lambda e: e.memset(BDm[64:128, 64:128], 1.0), reads=["BDm"], writes=["BDm"])
        P.op("dve", lambda e: e.tensor_scalar(out=NBD[:], in0=BDm[:], scalar1=-1.0, scalar2=1.0, op0=ALU.mult, op1=ALU.add), reads=["BDm"], writes=["NBD"])

        def proj_fm(col0, bank, m=128):
            for kc in range(8):
                P.op("pe", lambda e, kc=kc: e.matmul(PS[bank][:m, 0:T], lhsT=win[:, kc, col0:col0 + m], rhs=xnT[:, kc, :], start=(kc == 0), stop=(kc == 7)),
                     reads=[winkey(col0), "xnT"], writes=[psk(bank)])

        def conv4(dst, dkey, hist, hkey, wts, wkey, bias=None):
            if bias is None:
                P.op("dve", lambda e: e.tensor_scalar(out=dst, in0=hist[:, 0:T], scalar1=wts[:, 0:1], scalar2=None, op0=ALU.mult),
                     reads=[hkey, wkey], writes=[dkey])
            else:
                P.op("dve", lambda e: e.tensor_scalar(out=dst, in0=hist[:, 0:T], scalar1=wts[:, 0:1], scalar2=bias, op0=ALU.mult, op1=ALU.add),
                     reads=[hkey, wkey, "cbl"], writes=[dkey])
            for j in range(1, 4):
                P.op("dve", lambda e, j=j: e.scalar_tensor_tensor(out=dst, in0=hist[:, j:j + T], scalar=wts[:, j:j + 1], in1=dst, op0=ALU.mult, op1=ALU.add),
                     reads=[hkey, wkey, dkey], writes=[dkey])

        def mm4(bank, lhs_fn, rhs_fn, reads):
            for s_ in range(4):
                P.op("pe", lambda e, s_=s_: e.matmul(PS[bank][:, s_ * 128:(s_ + 1) * 128], lhsT=lhs_fn(s_), rhs=rhs_fn(s_), start=True, stop=True),
                     reads=reads, writes=[psk(bank)])

        def ps3(bank):
            return PS[bank][:].rearrange("p (s i) -> p s i", s=4)

        F32R = mybir.dt.float32r

        def Rr(ap):
            return ap.bitcast(F32R)

        def hist_in(ptb, pk, H, hkey, ch, bank):
            P.op("act", lambda e: e.copy(out=ptb[:, 3:515], in_=PS[bank][:, :]), reads=[psk(bank)], writes=[pk])
            P.op("dve", lambda e: e.tensor_copy(out=ptb[:, 0:3], in_=H[:, ch, :]), reads=[hkey, pk], writes=[pk])
            P.op("dve", lambda e: e.tensor_copy(out=H[:, ch, :], in_=ptb[:, 512:515]), reads=[pk, hkey], writes=[hkey])

        def rsqrt_ps(dst, dkey, bank, scale):
            P.op("act", lambda e: e.activation(out=dst[:], in_=PS[bank][:, :], func=AF.Ln, scale=scale, bias=EPS), reads=[psk(bank)], writes=[dkey])
            P.op("act", lambda e: e.activation(out=dst[:], in_=dst[:], func=AF.Exp, scale=-0.5), reads=[dkey], writes=[dkey])

        def lru_group(c):
            gel, xr, rg, ig, av, a2, bv, hs = (wb(n) for n in ("l0", "l1", "l2", "l3", "l4", "l5", "l6", "l7"))
            ptb, pk = wb("pt2", n=515), "pt2"
            proj_fm(2056 + c * 128, 2)
            P.op("act", lambda e: e.activation(out=gel[:], in_=PS[2][:, :], func=AF.Gelu_apprx_tanh), reads=[psk(2)], writes=["l0"])
            yield
            proj_fm(2568 + c * 128, 3)
            hist_in(ptb, pk, Hl, "Hl", c, 3)
            conv4(xr[:], "l1", ptb, pk, cwl[:, c, :], "cwl", bias=cbl[:, c:c + 1])
            yield
            P.op("pe", lambda e: e.matmul(PS[2][:, :], lhsT=WAbd[:, c, :], rhs=xr[:], start=True, stop=True), reads=["WAbd", "l1"], writes=[psk(2)])
            P.op("act", lambda e: e.activation(out=rg[:], in_=PS[2][:, :], func=AF.Sigmoid, bias=bab[:, c:c + 1]), reads=[psk(2), "bab"], writes=["l2"])
            P.op("pe", lambda e: e.matmul(PS[3][:, :], lhsT=WXbd[:, c, :], rhs=xr[:], start=True, stop=True), reads=["WXbd", "l1"], writes=[psk(3)])
            P.op("act", lambda e: e.activation(out=ig[:], in_=PS[3][:, :], func=AF.Sigmoid, bias=bxb[:, c:c + 1]), reads=[psk(3), "bxb"], writes=["l3"])
            yield
            P.op("act", lambda e: e.activation(out=av[:], in_=rg[:], func=AF.Exp, scale=cl[:, c:c + 1]), reads=["l2", "cl"], writes=["l4"])
            P.op("act", lambda e: e.activation(out=a2[:], in_=rg[:], func=AF.Exp, scale=cl2[:, c:c + 1]), reads=["l2", "cl2"], writes=["l5"])
            P.op("dve", lambda e: e.tensor_scalar(out=a2[:], in0=a2[:], scalar1=1.0, scalar2=-1.0, op0=ALU.min, op1=ALU.mult), reads=["l5"], writes=["l5"])
            P.op("act", lambda e: e.activation(out=a2[:], in_=a2[:], func=AF.Sqrt, bias=1.0), reads=["l5"], writes=["l5"])
            P.op("dve", lambda e: e.tensor_tensor(out=bv[:], in0=ig[:], in1=xr[:], op=ALU.mult), reads=["l3", "l1"], writes=["l6"])
            yield
            P.op("dve", lambda e: e.tensor_tensor(out=bv[:], in0=bv[:], in1=a2[:], op=ALU.mult), reads=["l6", "l5"], writes=["l6"])
            P.op("dve", lambda e: e.tensor_tensor_scan(out=hs[:], data0=av[:], data1=bv[:], initial=hst[:, c:c + 1], op0=ALU.mult, op1=ALU.add),
                 reads=["l4", "l6", "hst"], writes=["l7"])
            P.op("dve", lambda e: e.tensor_copy(out=hst[:, c:c + 1], in_=hs[:, T - 1:T]), reads=["l7"], writes=["hst"])
            P.op("dve", lambda e: e.tensor_tensor(out=mixT[:, 4 + c, :], in0=gel[:], in1=hs[:], op=ALU.mult), reads=["l0", "l7"], writes=["mixT"])
            yield

        def gdn_common():
            beta_f, g_f = wb("bf"), wb("gf")
            bt, gt, gam, ngam, egam, cfk = (wb(n) for n in ("bt", "gt", "gam", "ngam", "egam", "cfk"))
            proj_fm(2048, 0, m=4)
            P.op("act", lambda e: e.activation(out=beta_f[0:4, :], in_=PS[0][0:4, :], func=AF.Sigmoid), reads=[psk(0)], writes=["bf"])
            proj_fm(2052, 1, m=4)
            P.op("act", lambda e: e.activation(out=g_f[0:4, :], in_=PS[1][0:4, :], func=AF.Exp, bias=dtb[:, 0:1]), reads=[psk(1), "dtb"], writes=["gf"])
            P.op("act", lambda e: e.activation(out=g_f[0:4, :], in_=g_f[0:4, :], func=AF.Ln, bias=1.0), reads=["gf"], writes=["gf"])
            P.op("dve", lambda e: e.tensor_scalar(out=g_f[0:4, :], in0=g_f[0:4, :], scalar1=nega[:, 0:1], scalar2=None, op0=ALU.mult),
                 reads=["gf", "nega"], writes=["gf"])
            for s_ in range(4):
                P.op("pe", lambda e, s_=s_: e.transpose(out=PS[0][:, s_ * 4:(s_ + 1) * 4], in_=beta_f[0:4, s_ * 128:(s_ + 1) * 128], identity=identf[0:4, 0:4]),
                     reads=["bf", "identf"], writes=[psk(0)])
                P.op("pe", lambda e, s_=s_: e.transpose(out=PS[0][:, 16 + s_ * 4:16 + (s_ + 1) * 4], in_=g_f[0:4, s_ * 128:(s_ + 1) * 128], identity=identf[0:4, 0:4]),
                     reads=["gf", "identf"], writes=[psk(0)])
            hs_view = lambda t: t[:, 0:16].rearrange("p (h s) -> p h s", h=4)
            P.op("dve", lambda e: e.tensor_copy(out=hs_view(bt), in_=PS[0][:, 0:16].rearrange("p (s h) -> p h s", s=4)), reads=[psk(0)], writes=["bt"])
            P.op("dve", lambda e: e.tensor_copy(out=hs_view(gt), in_=PS[0][:, 16:32].rearrange("p (s h) -> p h s", s=4)), reads=[psk(0)], writes=["gt"])
            for s_ in range(4):
                P.op("pe", lambda e, s_=s_: e.matmul(PS[1][:, s_ * 4:(s_ + 1) * 4], lhsT=TriU[:], rhs=hs_view(gt)[:, :, s_], start=True, stop=True),
                     reads=["TriU", "gt"], writes=[psk(1)])
            P.op("dve", lambda e: e.tensor_copy(out=hs_view(gam), in_=PS[1][:, 0:16].rearrange("p (s h) -> p h s", s=4)), reads=[psk(1)], writes=["gam"])
            P.op("dve", lambda e: e.tensor_scalar(out=ngam[:, 0:16], in0=gam[:, 0:16], scalar1=-1.0, scalar2=None, op0=ALU.mult), reads=["gam"], writes=["ngam"])
            P.op("act", lambda e: e.activation(out=egam[:, 0:16], in_=gam[:, 0:16], func=AF.Exp), reads=["gam"], writes=["egam"])
            P.op("dve", lambda e: e.tensor_tensor(out=cfk[:, 0:16], in0=egam[:, 0:16], in1=bt[:, 0:16], op=ALU.mult), reads=["egam", "bt"], writes=["cfk"])

        def gdn_A(h):
            beta_f = wb("bf")
            bt, gt, gam, ngam, egam, cfk = (wb(n) for n in ("bt", "gt", "gam", "ngam", "egam", "cfk"))
            qc, kc, vc, sq, rq = (wb(n) for n in ("w0", "w1", "w2", "w4", "w5"))
            zk = f"z{h % 2}"
            zs = wb(zk)
            sq2, rq2 = wb("n4"), wb("n5")
            GD = F32 if DBG.get('gdf32r') else BF16
            qn, kn, kbT = wb("q0", GD), wb("k0", GD), wb("kb0", GD)
            Gb, tmpGB, DT, DTs, EGB = (wb(n) for n in ("w7", "g0", "g1", "g2", "g3"))
            kb_tok, kt_tok, vb_tok, QKT, qdT, wkT = (wb(n, GD) for n in ("g4", "g5", "g6", "g7", "g8", "g10"))
            wv_ = wb("g9")
            vcb = wb("vcb", GD) if GD == BF16 else None
            Pb = [wb("pa", GD), wb("pb", GD)]
            Qb = [wb("qa", GD), wb("qb", GD)]
            Rb = [wb("ra", GD), wb("rb", GD)]
            Pk, Qk, Rk = ["pa", "pb"], ["qa", "qb"], ["ra", "rb"]
            ub = [wb("u0", GD), wb("u1", GD)]
            Sr = wb("Sr", GD)
            hsl = lambda t, s_=None: (t[:, h * 4:(h + 1) * 4] if s_ is None else t[:, h * 4 + s_:h * 4 + s_ + 1])
            cs = lambda t, s_: t[:, s_ * 128:(s_ + 1) * 128]
            if GD == BF16:
                csr = lambda t, s_: t[:, s_ * 128:(s_ + 1) * 128]
                Rr = lambda ap: ap
                psb = lambda bank: PS[bank][:].bitcast(BF16)
                psb3 = lambda bank: PS[bank][:].bitcast(BF16)[:, 0:512].rearrange("p (s i) -> p s i", s=4)
                identx, identk = identb, "identb"
            else:
                csr = lambda t, s_: t[:, s_ * 128:(s_ + 1) * 128].bitcast(F32R)
                Rr = lambda ap: ap.bitcast(F32R)
                psb = lambda bank: PS[bank][:]
                psb3 = lambda bank: PS[bank][:].rearrange("p (s i) -> p s i", s=4)
                identx, identk = identf, "identf"

            def warm():
                for _ in range(DBG.get("warm", 0)):
                    P.op("pe", lambda e: e.matmul(PS[0][:, :], lhsT=win[:, 0, 0:128], rhs=xnT[:, 0, :], start=True, stop=True), reads=["win0", "xnT"], writes=[psk(0)])
            for idx, (ch, dst, dk) in enumerate([(h, qc, "w0"), (4 + h, kc, "w1"), (8 + h, vc, "w2")]):
                bank = 2 + idx % 2
                ptb, pk = wb(f"pt{idx % 2}", n=515), f"pt{idx % 2}"
                proj_fm(ch * 128, bank)
                hist_in(ptb, pk, Hg, "Hg", ch, bank)
                conv4(dst[:], dk, ptb, pk, cwg[:, ch, :], "cwg")
                P.op("act", lambda e, dst=dst: e.activation(out=dst[:], in_=dst[:], func=AF.Silu), reads=[dk], writes=[dk])
                yield
            proj_fm(1536 + h * 128, 3)
            P.op("act", lambda e: e.activation(out=zs[:], in_=PS[3][:, :], func=AF.Silu), reads=[psk(3)], writes=[zk])
            for x, xk, xo, xok, scl, bank in ((qc, "w0", qn, "q0", 128.0 ** -0.5, 2), (kc, "w1", kn, "k0", 1.0, 3)):
                P.op("dve", lambda e, x=x: e.tensor_tensor(out=sq[:].bitcast(F32R), in0=x[:], in1=x[:], op=ALU.mult), reads=[xk], writes=["w4"])
                P.op("pe", lambda e, bank=bank: e.matmul(PS[bank][:, :], lhsT=ones_r[:].bitcast(F32R), rhs=sq[:].bitcast(F32R), start=True, stop=True), reads=["ones_r", "w4"], writes=[psk(bank)])
                rsqrt_ps(rq, "w5", bank, 1.0)
                P.op("dve", lambda e, x=x, xo=xo, scl=scl: e.scalar_tensor_tensor(out=Rr(xo[:]), in0=x[:], scalar=scl, in1=rq[:], op0=ALU.mult, op1=ALU.mult),
                     reads=[xk, "w5"], writes=[xok])
                yield

        def gdn_BCD(h):
            beta_f = wb("bf")
            bt, gt, gam, ngam, egam, cfk = (wb(n) for n in ("bt", "gt", "gam", "ngam", "egam", "cfk"))
            qc, kc, vc, sq, rq = (wb(n) for n in ("w0", "w1", "w2", "w4", "w5"))
            zk = f"z{h % 2}"
            zs = wb(zk)
            sq2, rq2 = wb("n4"), wb("n5")
            GD = F32 if DBG.get('gdf32r') else BF16
            qn, kn, kbT = wb("q0", GD), wb("k0", GD), wb("kb0", GD)
            Gb, tmpGB, DT, DTs, EGB = (wb(n) for n in ("w7", "g0", "g1", "g2", "g3"))
            kb_tok, kt_tok, vb_tok, QKT, qdT, wkT = (wb(n, GD) for n in ("g4", "g5", "g6", "g7", "g8", "g10"))
            wv_ = wb("g9")
            vcb = wb("vcb", GD) if GD == BF16 else None
            Pb = [wb("pa", GD), wb("pb", GD)]
            Qb = [wb("qa", GD), wb("qb", GD)]
            Rb = [wb("ra", GD), wb("rb", GD)]
            Pk, Qk, Rk = ["pa", "pb"], ["qa", "qb"], ["ra", "rb"]
            ub = [wb("u0", GD), wb("u1", GD)]
            Sr = wb("Sr", GD)
            hsl = lambda t, s_=None: (t[:, h * 4:(h + 1) * 4] if s_ is None else t[:, h * 4 + s_:h * 4 + s_ + 1])
            cs = lambda t, s_: t[:, s_ * 128:(s_ + 1) * 128]
            if GD == BF16:
                csr = lambda t, s_: t[:, s_ * 128:(s_ + 1) * 128]
                Rr = lambda ap: ap
                psb = lambda bank: PS[bank][:].bitcast(BF16)
                psb3 = lambda bank: PS[bank][:].bitcast(BF16)[:, 0:512].rearrange("p (s i) -> p s i", s=4)
                identx, identk = identb, "identb"
            else:
                csr = lambda t, s_: t[:, s_ * 128:(s_ + 1) * 128].bitcast(F32R)
                Rr = lambda ap: ap.bitcast(F32R)
                psb = lambda bank: PS[bank][:]
                psb3 = lambda bank: PS[bank][:].rearrange("p (s i) -> p s i", s=4)
                identx, identk = identf, "identf"

            def warm():
                for _ in range(DBG.get("warm", 0)):
                    P.op("pe", lambda e: e.matmul(PS[0][:, :], lhsT=win[:, 0, 0:128], rhs=xnT[:, 0, :], start=True, stop=True), reads=["win0", "xnT"], writes=[psk(0)])
            P.op("pe", lambda e: e.matmul(PS[1][:, :], lhsT=sel4[0:4, h, :], rhs=beta_f[0:4, :], start=True, stop=True), reads=["sel4", "bf"], writes=[psk(1)])
            P.op("dve", lambda e: e.tensor_tensor(out=Rr(kbT[:]), in0=kn[:], in1=PS[1][:, :], op=ALU.mult), reads=["k0", psk(1)], writes=["kb0"])
            P.op("dve", lambda e: e.tensor_tensor(out=v3(Gb), in0=ones_f[:].unsqueeze(1).to_broadcast([128, 4, 128]), in1=hsl(gam).unsqueeze(2).to_broadcast([128, 4, 128]), op=ALU.mult),
                 reads=["ones_f", "gam"], writes=["w7"])
            for s_ in range(4):
                P.op("pe", lambda e, s_=s_: e.transpose(out=PS[4][:, s_ * 128:(s_ + 1) * 128], in_=cs(Gb, s_), identity=identf[:]), reads=["w7", "identf"], writes=[psk(4)])
            P.op("dve", lambda e: e.tensor_tensor(out=v3(tmpGB), in0=ps3(4), in1=MASKB[:].unsqueeze(1).to_broadcast([128, 4, 128]), op=ALU.add),
                 reads=[psk(4), "MASKB"], writes=["g0"])
            P.op("act", lambda e: e.activation(out=EGB[:], in_=PS[4][:, :], func=AF.Exp), reads=[psk(4)], writes=["g3"])
            for s_ in range(4):
                P.op("act", lambda e, s_=s_: e.activation(out=cs(DT, s_), in_=cs(tmpGB, s_), func=AF.Exp, bias=hsl(ngam, s_)), reads=["g0", "ngam"], writes=["g1"])
            P.op("dve", lambda e: e.tensor_tensor(out=v3(DTs), in0=v3(DT), in1=MSTR[:].unsqueeze(1).to_broadcast([128, 4, 128]), op=ALU.mult),
                 reads=["g1", "MSTR"], writes=["g2"])
            yield
            mm_t = lambda bank, src, skey: [P.op("pe", lambda e, s_=s_: e.transpose(out=psb(bank)[:, s_ * 128:(s_ + 1) * 128], in_=cs(src, s_), identity=identx[:]),
                                                 reads=[skey, identk], writes=[psk(bank)]) for s_ in range(4)]
            mm_t(5, kn, "k0")
            P.op("dve", lambda e: e.tensor_tensor(out=Rr(v3(kb_tok)), in0=psb3(5), in1=hsl(cfk).unsqueeze(2).to_broadcast([128, 4, 128]), op=ALU.mult),
                 reads=[psk(5), "cfk"], writes=["g4"])
            P.op("dve", lambda e: e.tensor_tensor(out=Rr(v3(kt_tok)), in0=psb3(5), in1=v3(DT)[:, :, 127:128].to_broadcast([128, 4, 128]), op=ALU.mult),
                 reads=[psk(5), "g1"], writes=["g5"])
            if GD == BF16:
                P.op("act", lambda e: e.copy(out=vcb[:], in_=vc[:]), reads=["w2"], writes=["vcb"])
                mm_t(6, vcb, "vcb")
            else:
                mm_t(6, vc, "w2")
            P.op("dve", lambda e: e.tensor_tensor(out=Rr(v3(vb_tok)), in0=psb3(6), in1=hsl(bt).unsqueeze(2).to_broadcast([128, 4, 128]), op=ALU.mult),
                 reads=[psk(6), "bt"], writes=["g6"])
            yield
            mm4(5, lambda s_: csr(kn, s_), lambda s_: csr(kbT, s_), ["k0", "kb0"])
            mfull, mfk = (wb("mfb", GD), "mfb") if GD == BF16 else (wv_, "g9")
            P.op("dve", lambda e: e.scalar_tensor_tensor(out=mfull[:], in0=PS[5][:, :], scalar=-1.0, in1=DTs[:], op0=ALU.mult, op1=ALU.mult),
                 reads=[psk(5), "g2"], writes=[mfk])
            P.op("dve", lambda e: e.tensor_tensor(out=Rr(v3(Pb[0])), in0=v3(mfull), in1=BDm[:].unsqueeze(1).to_broadcast([128, 4, 128]), op=ALU.mult),
                 reads=[mfk, "BDm"], writes=[Pk[0]])
            mm4(6, lambda s_: csr(kn, s_), lambda s_: csr(qn, s_), ["k0", "q0"])
            P.op("dve", lambda e: e.tensor_tensor(out=Rr(QKT[:]), in0=PS[6][:, :], in1=DT[:], op=ALU.mult), reads=[psk(6), "g1"], writes=["g7"])
            P.op("dve", lambda e: e.tensor_tensor(out=Rr(qdT[:]), in0=qn[:], in1=EGB[:], op=ALU.mult), reads=["q0", "g3"], writes=["g8"])
            yield "B_DONE"
            mm_t(4, mfull, mfk)
            P.op("dve", lambda e: e.tensor_tensor(out=Rr(v3(Qb[0])), in0=psb3(4), in1=BDm[:].unsqueeze(1).to_broadcast([128, 4, 128]), op=ALU.mult),
                 reads=[psk(4), "BDm"], writes=[Qk[0]])
            P.op("dve", lambda e: e.tensor_tensor(out=Rr(v3(wkT)), in0=psb3(4), in1=NBD[:].unsqueeze(1).to_broadcast([128, 4, 128]), op=ALU.mult),
                 reads=[psk(4), "NBD"], writes=["g10"])
            P.op("dve", lambda e: e.tensor_tensor(out=Rr(v3(Rb[0])), in0=v3(Pb[0]), in1=identf[:].unsqueeze(1).to_broadcast([128, 4, 128]), op=ALU.add),
                 reads=[Pk[0], "identf"], writes=[Rk[0]])
            cp, cq, cr = 0, 0, 0
            for m in range(1, 6):
                nq = 1 - cq
                mm4(4, lambda s_, cp=cp: csr(Pb[cp], s_), lambda s_, cq=cq: csr(Qb[cq], s_), [Pk[cp], Qk[cq]])
                P.op("act", lambda e, nq=nq: e.copy(out=Rr(Qb[nq][:]), in_=PS[4][:, :]), reads=[psk(4)], writes=[Qk[nq]])
                if m <= 4:
                    np_ = 1 - cp
                    mm4(5, lambda s_, cq=cq: csr(Qb[cq], s_), lambda s_, cp=cp: csr(Pb[cp], s_), [Pk[cp], Qk[cq]])
                    P.op("dve", lambda e, np_=np_: e.tensor_copy(out=Rr(Pb[np_][:]), in_=PS[5][:, :]), reads=[psk(5)], writes=[Pk[np_]])
                    cp = np_
                cq = nq
                warm()
                yield
                nr = 1 - cr
                mm4(6, lambda s_, cq=cq: csr(Qb[cq], s_), lambda s_, cr=cr: csr(Rb[cr], s_), [Qk[cq], Rk[cr]])
                P.op("dve", lambda e, cr=cr, nr=nr: e.tensor_tensor(out=Rr(Rb[nr][:]), in0=Rb[cr][:], in1=PS[6][:, :], op=ALU.add), reads=[psk(6), Rk[cr]], writes=[Rk[nr]])
                cr = nr
                warm()
                yield
            Rbd, Rbk = Rb[cr], Rk[cr]
            Yb, Yk = Pb[1 - cp], Pk[1 - cp]
            Tb, Tk = Qb[1 - cq], Qk[1 - cq]
            mm4(4, lambda s_: csr(wkT, s_), lambda s_: csr(Rbd, s_), ["g10", Rbk])
            P.op("act", lambda e: e.copy(out=Rr(Yb[:]), in_=PS[4][:, :]), reads=[psk(4)], writes=[Yk])
            mm_t(5, Rbd, Rbk)
            P.op("dve", lambda e: e.tensor_copy(out=Rr(Tb[:]), in_=psb(5)[:, 0:512]), reads=[psk(5)], writes=[Tk])
            yield
            nr = 1 - cr
            mm4(6, lambda s_: csr(Tb, s_), lambda s_: csr(Yb, s_), [Tk, Yk])
            P.op("dve", lambda e: e.tensor_tensor(out=Rr(Rb[nr][:]), in0=Rbd[:], in1=PS[6][:, :], op=ALU.add), reads=[psk(6), Rbk], writes=[Rk[nr]])
            cr = nr
            yield
            R, Rkey = Rb[cr], Rk[cr]
            mm4(4, lambda s_: csr(R, s_), lambda s_: csr(vb_tok, s_), [Rkey, "g6"])
            P.op("act", lambda e: e.copy(out=wv_[:], in_=PS[4][:, :]), reads=[psk(4)], writes=["g9"])
            mm4(5, lambda s_: csr(kb_tok, s_), lambda s_: csr(R, s_), [Rkey, "g4"])
            P.op("dve", lambda e: e.tensor_copy(out=Rr(wkT[:]), in_=PS[5][:, :]), reads=[psk(5)], writes=["g10"])
            yield
            P.op("act", lambda e: e.copy(out=Rr(Sr[:]), in_=Sst[:, h, :]), reads=["Sst"], writes=["Sr"])
            for s_ in range(4):
                u, uk = ub[s_ % 2], f"u{s_ % 2}"
                P.op("pe", lambda e, s_=s_: e.matmul(PS[7][:, 0:128], lhsT=csr(wkT, s_), rhs=Rr(Sr[:]), start=True, stop=True), reads=["g10", "Sr"], writes=[psk(7)])
                P.op("dve", lambda e, s_=s_, u=u: e.tensor_tensor(out=Rr(u[:, 0:128]), in0=cs(wv_, s_), in1=PS[7][:, 0:128], op=ALU.subtract), reads=["g9", psk(7)], writes=[uk])
                P.op("pe", lambda e, s_=s_: e.matmul(PS[1][:, s_ * 128:(s_ + 1) * 128], lhsT=Rr(Sr[:]), rhs=csr(qdT, s_), start=True, stop=False), reads=["Sr", "g8"], writes=[psk(1)])
                P.op("pe", lambda e, s_=s_, u=u: e.matmul(PS[1][:, s_ * 128:(s_ + 1) * 128], lhsT=Rr(u[:, 0:128]), rhs=csr(QKT, s_), start=False, stop=True), reads=[uk, "g7"], writes=[psk(1)])
                P.op("pe", lambda e, s_=s_, u=u: e.matmul(PS[7][:, 128:256], lhsT=csr(kt_tok, s_), rhs=Rr(u[:, 0:128]), start=True, stop=True), reads=["g5", uk], writes=[psk(7)])
                P.op("dve", lambda e, s_=s_: e.scalar_tensor_tensor(out=Sst[:, h, :], in0=Sst[:, h, :], scalar=v3(EGB)[:, s_, 127:128], in1=PS[7][:, 128:256], op0=ALU.mult, op1=ALU.add),
                     reads=["Sst", "g3", psk(7)], writes=["Sst"])
                if s_ < 3:
                    P.op("act", lambda e: e.copy(out=Rr(Sr[:]), in_=Sst[:, h, :]), reads=["Sst", "Sr"], writes=["Sr"])
                yield
            P.op("act", lambda e: e.activation(out=sq2[:].bitcast(F32R), in_=PS[1][:, :], func=AF.Square), reads=[psk(1)], writes=["n4"])
            P.op("pe", lambda e: e.matmul(PS[7][:, :], lhsT=ones_r[:].bitcast(F32R), rhs=sq2[:].bitcast(F32R), start=True, stop=True), reads=["ones_r", "n4"], writes=[psk(7)])
            rsqrt_ps(rq2, "n5", 7, 1.0 / 128.0)
            P.op("dve", lambda e: e.scalar_tensor_tensor(out=rq2[:], in0=PS[1][:, :], scalar=gnw[:, 0:1], in1=rq2[:], op0=ALU.mult, op1=ALU.mult),
                 reads=[psk(1), "gnw", "n5"], writes=["n5"])
            P.op("dve", lambda e: e.tensor_tensor(out=mixT[:, h, :], in0=rq2[:], in1=zs[:], op=ALU.mult), reads=["n5", zk], writes=["mixT"])
            yield

        def run_interleaved(gens, pending=None):
            gens = list(gens)
            while gens:
                for g_ in list(gens):
                    try:
                        r_ = next(g_)
                        if r_ == "B_DONE" and pending is not None:
                            gens.append(pending)
                            pending = None
                    except StopIteration:
                        gens.remove(g_)
            assert pending is None

        def out_proj_store(rows, nsub, dst_ap, mixT=None, xt=None, xkeys=None, next_src=None):
            if mixT is None:
                mixT, xt = mixT_p, xt_p
            if xkeys is None:
                xkeys = [f"xt{s_}" for s_ in range(nsub)]
            i = 0
            for s in range(nsub):
                xk = xkeys[s]
                for half in range(2):
                    bd = 2 + (i % 2)
                    i += 1
                    for c in range(8):
                        P.op("pe", lambda e, c=c, s=s, half=half, bd=bd: e.matmul(PS[bd][:rows, :], lhsT=mixT[:, c, s * rows:(s + 1) * rows],
                                                                                 rhs=wout[:, c, half * 512:(half + 1) * 512], start=(c == 0), stop=(c == 7)),
                             reads=["mixT", f"wo{c}"], writes=[psk(bd)])
                    P.op("dve", lambda e, s=s, half=half, bd=bd: e.tensor_tensor(out=xt[:rows, s, half * 512:(half + 1) * 512],
                                                                                in0=xt[:rows, s, half * 512:(half + 1) * 512],
                                                                                in1=PS[bd][:rows, :], op=ALU.add),
                         reads=[psk(bd), xk], writes=[xk])
                P.dma("sp", dst_ap[s * rows:(s + 1) * rows, :], xt[:rows, s, :], reads=[xk])
                if next_src is not None:
                    P.dma("sp", xt[:, s, :], next_src[s * 128:(s + 1) * 128, :], writes=[xk])

        mixT_p, xt_p = mixT, xt
        for b in range(NB if do_prompt else 0):
            P.op("dve", lambda e: e.memset(Hg[:], 0.0), reads=["Hg"], writes=["Hg"])
            P.op("dve", lambda e: e.memset(Hl[:], 0.0), reads=["Hl"], writes=["Hl"])
            P.op("dve", lambda e: e.memset(Sst[:], 0.0), reads=["Sst"], writes=["Sst"])
            P.op("dve", lambda e: e.memset(hst[:], 0.0), reads=["hst"], writes=["hst"])
            for st in range(SEQ // T):
                if b == 0 and st == 0:
                    for s in range(4):
                        P.dma("sp", xt[:, s, :], src_p[b, s * 128:(s + 1) * 128, :], writes=[f"xt{s}"])
                for s in range(4):
                    norm_transpose(xt[:, s, :], 128, f"xt{s}", xn, junk, ss, rstd, xnT, "xnT", s * 128, "m", wrow)
                gdn_common()
                run_interleaved([gdn_A(0)])
                for h in range(4):
                    run_interleaved([gdn_BCD(h), lru_group(h)], pending=(gdn_A(h + 1) if h < 3 else None))
                    _chk(f"gdn_b{b}_st{st}_h{h}")
                nb_, nst_ = (b, st + 1) if st + 1 < SEQ // T else (b + 1, 0)
                nxt_ = src_p[nb_, nst_ * T:(nst_ + 1) * T, :] if nb_ < NB else None
                out_proj_store(128, 4, dst_p[b, st * T:(st + 1) * T, :], next_src=nxt_)
            P.dma("sp", p_gdn[b].rearrange("h k v -> k h v"), Sst[:], reads=["Sst"])
            for ch in range(12):
                P.dma("sp", p_gdn_conv[b][:, ch * 128:(ch + 1) * 128].rearrange("r p -> p r"), Hg[:, ch, :], reads=["Hg"], allow_slow_non_contiguous=True)
            for c in range(4):
                P.dma("sp", p_lru_conv[b][:, c * 128:(c + 1) * 128].rearrange("r p -> p r"), Hl[:, c, :], reads=["Hl"], allow_slow_non_contiguous=True)
            P.dma("sp", p_lru[b].rearrange("(c p) -> p c", p=128), hst[:], reads=["hst"], allow_slow_non_contiguous=True)
        P.barrier()
        A.release(m1)
        if not do_samples:
            A.release(m0)
            return
        xs = A.alloc("xs", [NS, 1, D], F32)
        xnTs = A.alloc("xnTs", [128, 8, NS], BF16)
        mixTs = A.alloc("mixTs", [128, 8, NS], BF16)
        mix_tok = A.alloc("mix_tok", [NS, D], F32)
        mix_tb = A.alloc("mix_tb", [NS, D], BF16)
        zs_s = A.alloc("zs_s", [NS, 512], F32)
        gnwb = A.alloc("gnwb", [NS, 128], F32)
        mA = A.mark()
        proj = A.alloc("proj_s", [NS, W_IN], F32)
        histg = A.alloc("histg", [NS, 4, 1536], F32)
        cwgb = A.alloc("cwgb", [NS, 4, 1536], F32)
        qkvc = A.alloc("qkvc", [NS, 1536], F32)
        tq = A.alloc("tq", [NS, 1024], F32)
        sm8 = A.alloc("sm8", [NS, 8], F32)
        adb = A.alloc("adb", [NS, 4], F32)
        dtbb = A.alloc("dtbb", [NS, 4], F32)
        bgs = A.alloc("bgs", [NS, 4, 2], F32)
        tg = A.alloc("tg", [NS, 4], F32)
        histl = A.alloc("histl", [NS, 4, 512], F32)
        cwlb = A.alloc("cwlb", [NS, 4, 512], F32)
        rowp = {n: A.alloc(n, [NS, 512], F32) for n in ("cblb", "babb", "bxbb", "clb", "xr_s", "rg_s", "ig_s", "av_s", "a2_s", "h0_s", "gel_s")}
        xrT = A.alloc("xrT", [128, 4, NS], F32)

        P.dma("sp", xs[:, 0, :], src_s, writes=["xs"])
        P.dma("sp", gnwb[:], gdn_norm_w.partition_broadcast(NS), writes=["gnwb"])
        norm_transpose(xs[:, 0, :], NS, "xs", xn, junk, ss, rstd, xnTs, "xnTs", 0, "m", wrow)
        for gi, c0 in enumerate(range(0, W_IN, 512)):
            n = min(512, W_IN - c0)
            bank = 2 + gi % 2
            for kc in range(8):
                P.op("pe", lambda e, kc=kc, c0=c0, n=n, bank=bank: e.matmul(PS[bank][:NS, 0:n], lhsT=xnTs[:, kc, :], rhs=win[:, kc, c0:c0 + n],
                                                                         start=(kc == 0), stop=(kc == 7)), reads=WIN_ALL + ["xnTs"], writes=[psk(bank)])
            P.op("act" if gi % 2 == 0 else "dve",
                 (lambda e, c0=c0, n=n, bank=bank: e.copy(out=proj[:, c0:c0 + n], in_=PS[bank][:NS, 0:n])) if gi % 2 == 0 else
                 (lambda e, c0=c0, n=n, bank=bank: e.tensor_copy(out=proj[:, c0:c0 + n], in_=PS[bank][:NS, 0:n])),
                 reads=[psk(bank)], writes=["proj_s"])
        P.op("act", lambda e: e.activation(out=zs_s[:], in_=proj[:, 1536:2048], func=AF.Silu), reads=["proj_s"], writes=["zs_s"])
        P.dma("sp", histg[:, 0:3, :], state_gdn_conv, writes=["histg"])
        P.op("pool", lambda e: e.tensor_copy(out=histg[:, 3, :], in_=proj[:, 0:1536]), reads=["proj_s"], writes=["histg"])
        P.dma("sp", s_gdn_conv, histg[:, 1:4, :], reads=["histg"])
        P.dma("sp", cwgb[:].rearrange("p j c -> p (j c)"), conv_gdn_w.rearrange("j c -> (j c)").partition_broadcast(NS), writes=["cwgb"])
        P.op("dve", lambda e: e.tensor_tensor(out=cwgb[:], in0=histg[:], in1=cwgb[:], op=ALU.mult), reads=["histg", "cwgb"], writes=["cwgb"])
        P.op("dve", lambda e: e.tensor_tensor(out=qkvc[:], in0=cwgb[:, 0, :], in1=cwgb[:, 1, :], op=ALU.add), reads=["cwgb"], writes=["qkvc"])
        P.op("dve", lambda e: e.tensor_tensor(out=qkvc[:], in0=qkvc[:], in1=cwgb[:, 2, :], op=ALU.add), reads=["cwgb", "qkvc"], writes=["qkvc"])
        P.op("dve", lambda e: e.tensor_tensor(out=qkvc[:], in0=qkvc[:], in1=cwgb[:, 3, :], op=ALU.add), reads=["cwgb", "qkvc"], writes=["qkvc"])
        P.op("act", lambda e: e.activation(out=qkvc[:], in_=qkvc[:], func=AF.Silu), reads=["qkvc"], writes=["qkvc"])
        P.op("dve", lambda e: e.tensor_tensor(out=tq[:], in0=qkvc[:, 0:1024], in1=qkvc[:, 0:1024], op=ALU.mult), reads=["qkvc"], writes=["tq"])
        P.op("dve", lambda e: e.tensor_reduce(out=sm8[:], in_=tq[:].rearrange("p (a d) -> p a d", d=128), axis=AX.X, op=ALU.add), reads=["tq"], writes=["sm8"])
        P.op("act", lambda e: e.activation(out=sm8[:], in_=sm8[:], func=AF.Sqrt, bias=EPS), reads=["sm8"], writes=["sm8"])
        P.op("dve", lambda e: e.reciprocal(out=sm8[:], in_=sm8[:]), reads=["sm8"], writes=["sm8"])
        q3 = qkvc[:, 0:512].rearrange("p (h d) -> p h d", h=4)
        k3 = qkvc[:, 512:1024].rearrange("p (h d) -> p h d", h=4)
        P.op("dve", lambda e: e.scalar_tensor_tensor(out=q3, in0=q3, scalar=128.0 ** -0.5, in1=sm8[:, 0:4].unsqueeze(2).to_broadcast([NS, 4, 128]),
                                                     op0=ALU.mult, op1=ALU.mult), reads=["qkvc", "sm8"], writes=["qkvc"])
        P.op("dve", lambda e: e.tensor_tensor(out=k3, in0=k3, in1=sm8[:, 4:8].unsqueeze(2).to_broadcast([NS, 4, 128]), op=ALU.mult),
             reads=["qkvc", "sm8"], writes=["qkvc"])
        P.dma("sp", adb[:], gdn_a_log.partition_broadcast(NS), writes=["adb"])
        P.dma("sp", dtbb[:], gdn_dt_bias.partition_broadcast(NS), writes=["dtbb"])
        P.op("act", lambda e: e.activation(out=adb[:], in_=adb[:], func=AF.Exp), reads=["adb"], writes=["adb"])
        P.op("act", lambda e: e.activation(out=bgs[:, :, 0], in_=proj[:, 2048:2052], func=AF.Sigmoid), reads=["proj_s"], writes=["bgs"])
        P.op("dve", lambda e: e.tensor_tensor(out=tg[:], in0=proj[:, 2052:2056], in1=dtbb[:], op=ALU.add), reads=["proj_s", "dtbb"], writes=["tg"])
        P.op("act", lambda e: e.activation(out=tg[:], in_=tg[:], func=AF.Exp), reads=["tg"], writes=["tg"])
        P.op("act", lambda e: e.activation(out=tg[:], in_=tg[:], func=AF.Ln, bias=1.0), reads=["tg"], writes=["tg"])
        P.op("dve", lambda e: e.scalar_tensor_tensor(out=bgs[:, :, 1], in0=tg[:], scalar=-1.0, in1=adb[:], op0=ALU.mult, op1=ALU.mult),
             reads=["tg", "adb", "bgs"], writes=["bgs"])
        sgv = sg_qk.rearrange("(b h) x -> b h x", h=4)
        P.dma("sp", sgv[:, :, 0:128], q3, reads=["qkvc"], writes=["sg_qk"])
        P.dma("sp", sgv[:, :, 128:256], k3, reads=["qkvc"], writes=["sg_qk"])
        P.dma("sp", sg_v, qkvc[:, 1024:1536], reads=["qkvc"], writes=["sg_v"])
        P.dma("sp", sg_bg.rearrange("(b h) x -> b (h x)", h=4), bgs[:].rearrange("p h x -> p (h x)"), reads=["bgs"], writes=["sg_bg"])
        cblb, babb, bxbb, clb, xr_s, rg_s, ig_s, av_s, a2_s, h0_s, gel_s = (rowp[n] for n in ("cblb", "babb", "bxbb", "clb", "xr_s", "rg_s", "ig_s", "av_s", "a2_s", "h0_s", "gel_s"))
        P.dma("sp", histl[:, 0:3, :], state_lru_conv, writes=["histl"])
        P.op("pool", lambda e: e.tensor_copy(out=histl[:, 3, :], in_=proj[:, 2568:3080]), reads=["proj_s"], writes=["histl"])
        P.dma("sp", s_lru_conv, histl[:, 1:4, :], reads=["histl"])
        P.dma("sp", cwlb[:].rearrange("p j c -> p (j c)"), conv_lru_w.rearrange("j c -> (j c)").partition_broadcast(NS), writes=["cwlb"])
        P.dma("sp", cblb[:], conv_lru_b.partition_broadcast(NS), writes=["cblb"])
        P.dma("sp", babb[:], lru_ba.partition_broadcast(NS), writes=["babb"])
        P.dma("sp", bxbb[:], lru_bx.partition_broadcast(NS), writes=["bxbb"])
        P.dma("sp", clb[:], lru_lambda.partition_broadcast(NS), writes=["clb"])
        P.dma("sp", h0_s[:], state_lru, writes=["h0_s"])
        P.op("act", lambda e: e.activation(out=clb[:], in_=clb[:], func=AF.Exp, scale=-1.0), reads=["clb"], writes=["clb"])
        P.op("act", lambda e: e.activation(out=clb[:], in_=clb[:], func=AF.Ln, bias=1.0), reads=["clb"], writes=["clb"])
        P.op("dve", lambda e: e.tensor_scalar(out=clb[:], in0=clb[:], scalar1=-8.0, scalar2=None, op0=ALU.mult), reads=["clb"], writes=["clb"])
        P.op("dve", lambda e: e.tensor_tensor(out=cwlb[:], in0=histl[:], in1=cwlb[:], op=ALU.mult), reads=["histl", "cwlb"], writes=["cwlb"])
        P.op("dve", lambda e: e.tensor_tensor(out=xr_s[:], in0=cwlb[:, 0, :], in1=cblb[:], op=ALU.add), reads=["cwlb", "cblb"], writes=["xr_s"])
        for j in range(1, 4):
            P.op("dve", lambda e, j=j: e.tensor_tensor(out=xr_s[:], in0=xr_s[:], in1=cwlb[:, j, :], op=ALU.add), reads=["cwlb", "xr_s"], writes=["xr_s"])
        for c in range(4):
            P.op("pe", lambda e, c=c: e.transpose(out=PS[0][:, c * NS:(c + 1) * NS], in_=xr_s[:, c * 128:(c + 1) * 128], identity=identf[:NS, :NS]),
                 reads=["xr_s", "identf"], writes=[psk(0)])
        P.op("dve", lambda e: e.tensor_copy(out=xrT[:], in_=PS[0][:, 0:4 * NS].rearrange("p (c t) -> p c t", c=4)), reads=[psk(0)], writes=["xrT"])
        for (Wbd, wkey, bank, bb_, bkey, dst, dkey) in ((WAbd, "WAbd", 4, babb, "babb", rg_s, "rg_s"), (WXbd, "WXbd", 5, bxbb, "bxbb", ig_s, "ig_s")):
            for c in range(4):
                P.op("pe", lambda e, c=c, Wbd=Wbd, bank=bank: e.matmul(PS[bank][:NS, c * 128:(c + 1) * 128], lhsT=xrT[:, c, :], rhs=Wbd[:, c, :], start=True, stop=True),
                     reads=["xrT", wkey], writes=[psk(bank)])
            P.op("dve", lambda e, bank=bank, bb_=bb_, dst=dst: e.tensor_tensor(out=dst[:], in0=PS[bank][:NS, :], in1=bb_[:], op=ALU.add), reads=[psk(bank), bkey], writes=[dkey])
            P.op("act", lambda e, dst=dst: e.activation(out=dst[:], in_=dst[:], func=AF.Sigmoid), reads=[dkey], writes=[dkey])
        P.op("dve", lambda e: e.tensor_tensor(out=av_s[:], in0=clb[:], in1=rg_s[:], op=ALU.mult), reads=["clb", "rg_s"], writes=["av_s"])
        P.op("act", lambda e: e.activation(out=a2_s[:], in_=av_s[:], func=AF.Exp, scale=2.0), reads=["av_s"], writes=["a2_s"])
        P.op("act", lambda e: e.activation(out=av_s[:], in_=av_s[:], func=AF.Exp), reads=["av_s", "a2_s"], writes=["av_s"])
        P.op("dve", lambda e: e.tensor_scalar(out=a2_s[:], in0=a2_s[:], scalar1=1.0, scalar2=-1.0, op0=ALU.min, op1=ALU.mult), reads=["a2_s"], writes=["a2_s"])
        P.op("act", lambda e: e.activation(out=a2_s[:], in_=a2_s[:], func=AF.Sqrt, bias=1.0), reads=["a2_s"], writes=["a2_s"])
        P.op("dve", lambda e: e.tensor_tensor(out=ig_s[:], in0=ig_s[:], in1=xr_s[:], op=ALU.mult), reads=["ig_s", "xr_s"], writes=["ig_s"])
        P.op("dve", lambda e: e.tensor_tensor(out=ig_s[:], in0=ig_s[:], in1=a2_s[:], op=ALU.mult), reads=["ig_s", "a2_s"], writes=["ig_s"])
        P.op("dve", lambda e: e.tensor_tensor(out=h0_s[:], in0=h0_s[:], in1=av_s[:], op=ALU.mult), reads=["h0_s", "av_s"], writes=["h0_s"])
        P.op("dve", lambda e: e.tensor_tensor(out=h0_s[:], in0=h0_s[:], in1=ig_s[:], op=ALU.add), reads=["h0_s", "ig_s"], writes=["h0_s"])
        P.dma("sp", s_lru, h0_s[:], reads=["h0_s"])
        P.op("act", lambda e: e.activation(out=gel_s[:], in_=proj[:, 2056:2568], func=AF.Gelu_apprx_tanh), reads=["proj_s"], writes=["gel_s"])
        P.op("dve", lambda e: e.tensor_tensor(out=mix_tok[:, 512:1024], in0=gel_s[:], in1=h0_s[:], op=ALU.mult), reads=["gel_s", "h0_s"], writes=["mix_tok"])
        P.barrier()
        A.release(mA)
        S_p = A.alloc("S_p", [128, 128, 64], F32)
        T1g = A.alloc("T1g", [128, 128 * 64], F32)
        qk_p = A.alloc("qk_p", [128, 256], F32)
        v_p = A.alloc("v_p", [128, 64], F32)
        bg_p = A.alloc("bg_p", [128, 2], F32)
        eg_p = A.alloc("eg_p", [128, 2], F32)
        pred = A.alloc("pred", [128, 64], F32)
        dl = A.alloc("dl", [128, 64], F32)
        o_p = A.alloc("o_p", [128, 64], F32)
        o_tok = A.alloc("o_tokg", [NS, 512], F32)
        sm4 = A.alloc("sm4", [NS, 4], F32)
        sgs = state_gdn.rearrange("b h k (e d) -> e (b h) k d", e=2)
        sgo = s_gdn.rearrange("b h k (e d) -> e (b h) k d", e=2)
        svv = sg_v.rearrange("b (h e d) -> e (b h) d", h=4, e=2)
        sov = so_g.rearrange("b (h e d) -> e (b h) d", h=4, e=2)
        for e_ in range(2):
            sl = slice(e_ * 64, (e_ + 1) * 64)
            P.dma("sp", S_p[sl, :, :], sgs[e_], writes=["S_p"])
            P.dma("sp", qk_p[sl, :], sg_qk, reads=["sg_qk"], writes=["qk_p"])
            P.dma("sp", v_p[sl, :], svv[e_], reads=["sg_v"], writes=["v_p"])
            P.dma("sp", bg_p[sl, :], sg_bg, reads=["sg_bg"], writes=["bg_p"])
        P.op("act", lambda e: e.activation(out=eg_p[:, 0:1], in_=bg_p[:, 1:2], func=AF.Exp), reads=["bg_p"], writes=["eg_p"])
        P.op("dve", lambda e: e.tensor_scalar(out=eg_p[:, 1:2], in0=eg_p[:, 0:1], scalar1=-1.0, scalar2=None, op0=ALU.mult), reads=["eg_p"], writes=["eg_p"])
        T1vk = T1g[:].rearrange("p (v k) -> p v k", v=64)
        T1kv = T1g[:].rearrange("p (k v) -> p k v", k=128)
        Svk = S_p[:].rearrange("p k v -> p v k")
        P.op("dve", lambda e: e.tensor_tensor(out=T1vk, in0=Svk, in1=qk_p[:, 128:256].unsqueeze(1).to_broadcast([128, 64, 128]), op=ALU.mult),
             reads=["S_p", "qk_p"], writes=["T1g"])
        P.op("dve", lambda e: e.tensor_reduce(out=pred[:], in_=T1vk, axis=AX.X, op=ALU.add), reads=["T1g"], writes=["pred"])
        P.op("dve", lambda e: e.scalar_tensor_tensor(out=dl[:], in0=pred[:], scalar=eg_p[:, 1:2], in1=v_p[:], op0=ALU.mult, op1=ALU.add),
             reads=["pred", "eg_p", "v_p"], writes=["dl"])
        P.op("dve", lambda e: e.tensor_scalar(out=dl[:], in0=dl[:], scalar1=bg_p[:, 0:1], scalar2=None, op0=ALU.mult), reads=["dl", "bg_p"], writes=["dl"])
        P.op("dve", lambda e: e.tensor_tensor(out=T1kv, in0=qk_p[:, 128:256].unsqueeze(2).to_broadcast([128, 128, 64]),
                                               in1=dl[:].unsqueeze(1).to_broadcast([128, 128, 64]), op=ALU.mult), reads=["qk_p", "dl", "T1g"], writes=["T1g"])
        P.op("dve", lambda e: e.scalar_tensor_tensor(out=S_p[:].rearrange("p k v -> p (k v)"), in0=S_p[:].rearrange("p k v -> p (k v)"), scalar=eg_p[:, 0:1],
                                                     in1=T1g[:], op0=ALU.mult, op1=ALU.add), reads=["S_p", "eg_p", "T1g"], writes=["S_p"])
        for e_ in range(2):
            P.dma("sp", sgo[e_], S_p[e_ * 64:(e_ + 1) * 64, :, :], reads=["S_p"])
        P.op("dve", lambda e: e.tensor_tensor(out=T1vk, in0=Svk, in1=qk_p[:, 0:128].unsqueeze(1).to_broadcast([128, 64, 128]), op=ALU.mult),
             reads=["S_p", "qk_p", "T1g"], writes=["T1g"])
        P.op("dve", lambda e: e.tensor_reduce(out=o_p[:], in_=T1vk, axis=AX.X, op=ALU.add), reads=["T1g"], writes=["o_p"])
        for e_ in range(2):
            P.dma("sp", sov[e_], o_p[e_ * 64:(e_ + 1) * 64, :], reads=["o_p"], writes=["so_g"])
        P.dma("sp", o_tok[:], so_g, reads=["so_g"], writes=["o_tokg"])
        o3 = o_tok[:].rearrange("p (h d) -> p h d", h=4)
        m3 = mix_tok[:, 0:512].rearrange("p (h d) -> p h d", h=4)
        P.op("dve", lambda e: e.tensor_tensor(out=m3, in0=o3, in1=o3, op=ALU.mult), reads=["o_tokg"], writes=["mix_tok"])
        P.op("dve", lambda e: e.tensor_reduce(out=sm4[:], in_=m3, axis=AX.X, op=ALU.add), reads=["mix_tok"], writes=["sm4"])
        P.op("act", lambda e: e.activation(out=sm4[:], in_=sm4[:], func=AF.Sqrt, scale=1.0 / 128.0, bias=EPS), reads=["sm4"], writes=["sm4"])
        P.op("dve", lambda e: e.reciprocal(out=sm4[:], in_=sm4[:]), reads=["sm4"], writes=["sm4"])
        P.op("dve", lambda e: e.tensor_tensor(out=m3, in0=o3, in1=sm4[:].unsqueeze(2).to_broadcast([NS, 4, 128]), op=ALU.mult), reads=["o_tokg", "sm4", "mix_tok"], writes=["mix_tok"])
        P.op("dve", lambda e: e.tensor_tensor(out=m3, in0=m3, in1=gnwb[:].unsqueeze(1).to_broadcast([NS, 4, 128]), op=ALU.mult), reads=["gnwb", "mix_tok"], writes=["mix_tok"])
        P.op("dve", lambda e: e.tensor_tensor(out=mix_tok[:, 0:512], in0=mix_tok[:, 0:512], in1=zs_s[:], op=ALU.mult), reads=["zs_s", "mix_tok"], writes=["mix_tok"])
        transpose_to_fm(mix_tok[:, :], NS, "mix_tok", mixTs, "mixT", 8, mix_tb, "mix_tb", 0)
        out_proj_store(NS, 1, dst_s, mixT=mixTs, xt=xs, xkeys=["xs"])
        P.barrier()
        A.release(m0)

    SLOPES = [2.0 ** (-8.0 * (h + 1) / 16.0) for h in range(16)]
    NEG = -30000.0

    def transpose_to_fm(src_ap, rows, src_key, dstT, dst_key, nch, tmpb, tmp_key, bank):
        P.op("dve", lambda e: e.tensor_copy(out=tmpb[:rows, 0:nch * 128], in_=src_ap), reads=[src_key], writes=[tmp_key])
        pb = PS[bank][:].bitcast(BF16)
        for c0 in range(0, nch, 4):
            n = min(4, nch - c0)
            for j in range(n):
                c = c0 + j
                P.op("pe", lambda e, c=c, j=j: e.transpose(out=pb[:, j * 128:j * 128 + rows], in_=tmpb[:rows, c * 128:(c + 1) * 128],
                                                           identity=identb[:rows, :rows]),
                     reads=[tmp_key, "identb"], writes=[psk(bank)])
            src = pb[:, 0:512].rearrange("p (j t) -> p j t", j=4)[:, 0:n, 0:rows]
            P.op("dve", lambda e, src=src, c0=c0, n=n: e.tensor_copy(out=dstT[:, c0:c0 + n, 0:rows], in_=src), reads=[psk(bank)], writes=[dst_key])

    def attn_stage(src_p, src_s, dst_p, dst_s, do_samples=True, do_prompt=True):
        m0 = A.mark()
        wq = A.alloc("wq", [128, 8, 1024], BF16)
        wkd = A.alloc("wkd", [128, 8, 512], BF16)
        wk = A.alloc("wk", [128, 8, 256], BF16)
        wv = A.alloc("wv", [128, 8, 256], BF16)
        woc = A.alloc("woc", [128, 8, D], BF16)
        wrow = A.alloc("wrow", [128, D], F32)
        sinkb = A.alloc("sinkb", [128, 16], F32)
        xt = A.alloc("xt", [128, 4, D], F32)
        xn = A.alloc("xn", [128, D], BF16)
        junk = A.alloc("junk", [128, D], BF16)
        ss = A.alloc("ss", [128, 1], F32)
        rstd = A.alloc("rstd", [128, 1], F32)
        xnT = A.alloc("xnT", [128, 8, T], BF16)
        oT = A.alloc("oT", [128, 8, T], BF16)
        m1 = A.mark()
        qT = A.alloc("qT", [128, 16, T], BF16)
        kT2 = A.alloc("kT2", [128, 4, 128 + T], BF16)
        vtok = A.alloc("vtok", [128, 5, 256], BF16)
        klast = A.alloc("klast", [128, 256], F32)
        vlast = A.alloc("vlast", [128, 256], F32)
        REL = A.alloc("REL", [128, 256], F32)
        MB = A.alloc("MB", [128, 256], F32)
        AB = A.alloc("AB", [128, 16, 256], F32)
        NR = 3
        sc = [A.alloc(f"sc{i}", [128, 2, 258], F32) for i in range(NR)]
        pe_ = [A.alloc(f"pe{i}", [128, 2, 258], F32) for i in range(NR)]
        pn = [A.alloc(f"pn{i}", [128, 2, 256], BF16) for i in range(NR)]
        pT = [A.alloc(f"pT{i}", [128, 4, 128], BF16) for i in range(NR)]
        sm = [A.alloc(f"sm{i}", [128, 8], F32) for i in range(NR)]

        P.dma("sp", wrow[:], norm_mix[1].partition_broadcast(128), writes=["wrow"])
        P.dma("sp", sinkb[:], sinks_c.partition_broadcast(128), writes=["sinkb"])
        WQ_ALL = [f"wq{c}" for c in range(8)]
        wload(wk[:, :, :], w_qkv_c[:, 1024:1280].rearrange("(kc p) c -> p kc c", p=128), "wk")
        wload(wv[:, :, :], w_qkv_c[:, 1280:1536].rearrange("(kc p) c -> p kc c", p=128), "wv")
        for c in range(8):
            wload(wq[:, :, c * 128:(c + 1) * 128], w_qkv_c[:, c * 128:(c + 1) * 128].rearrange("(kc p) c -> p kc c", p=128), f"wq{c}")
        for c in range(8):
            wload(woc[:, c, :], w_out_c[c * 128:(c + 1) * 128, :], f"woc{c}")
        for kc in range(8):
            for e_ in range(2):
                o = wkd[:, kc, :].rearrange("p (g e d) -> p g e d", g=4, e=2)[:, :, e_, :]
                i_ = wk[:, kc, :].rearrange("p (g d) -> p g d", g=4)
                P.op("pool", lambda e, o=o, i_=i_: e.tensor_copy(out=o, in_=i_), reads=["wk"], writes=["wkd"])
        P.op("pool", lambda e: e.iota(REL[:], pattern=[[-1, 256]], base=128, channel_multiplier=1, allow_small_or_imprecise_dtypes=True),
             writes=["REL"])
        P.op("pool", lambda e: e.memset(MB[:], 0.0), writes=["MB"])
        P.op("pool", lambda e: e.affine_select(out=MB[:], in_=MB[:], pattern=[[-1, 256]], compare_op=ALU.is_ge, fill=NEG, base=128,
                                               channel_multiplier=1), reads=["MB"], writes=["MB"])
        P.op("pool", lambda e: e.affine_select(out=MB[:], in_=MB[:], pattern=[[1, 256]], compare_op=ALU.is_ge, fill=NEG, base=0,
                                               channel_multiplier=-1), reads=["MB"], writes=["MB"])
        for h in range(16):
            P.op("dve", lambda e, h=h: e.scalar_tensor_tensor(out=AB[:, h, :], in0=REL[:], scalar=-SLOPES[h], in1=MB[:], op0=ALU.mult, op1=ALU.add),
                 reads=["REL", "MB"], writes=["AB"])

        def out_proj_store(rows, nsub, dst_ap, xkeys=None, next_src=None):
            if xkeys is None:
                xkeys = [f"xt{s_}" for s_ in range(nsub)]
            i = 0
            for s in range(nsub):
                xk = xkeys[s]
                for half in range(2):
                    bd = 2 + (i % 2)
                    i += 1
                    for c in range(8):
                        P.op("pe", lambda e, c=c, s=s, half=half, bd=bd: e.matmul(PS[bd][:rows, :], lhsT=oT[:, c, s * rows:(s + 1) * rows],
                                                                                 rhs=woc[:, c, half * 512:(half + 1) * 512], start=(c == 0), stop=(c == 7)),
                             reads=["oT", f"woc{c}"], writes=[psk(bd)])
                    P.op("dve", lambda e, s=s, half=half, bd=bd: e.tensor_tensor(out=xt[:rows, s, half * 512:(half + 1) * 512],
                                                                                in0=xt[:rows, s, half * 512:(half + 1) * 512],
                                                                                in1=PS[bd][:rows, :], op=ALU.add),
                         reads=[psk(bd), xk], writes=[xk])
                P.dma("sp", dst_ap[s * rows:(s + 1) * rows, :], xt[:rows, s, :], reads=[xk])
                if next_src is not None:
                    P.dma("sp", xt[:, s, :], next_src[s * 128:(s + 1) * 128, :], writes=[xk])

        P.op("pool", lambda e: e.memset(qT[:], 0.0), writes=["qT"])
        _chk('consts')
        for b in range(NB if do_prompt else 0):
            for st in range(SEQ // T):
                if b == 0 and st == 0:
                    for s in range(4):
                        P.dma("sp", xt[:, s, :], src_p[b, s * 128:(s + 1) * 128, :], writes=[f"xt{s}"])
                for s in range(4):
                    norm_transpose(xt[:, s, :], 128, f"xt{s}", xn, junk, ss, rstd, xnT, "xnT", s * 128, "a", wrow)
                for c in range(8):
                    bq = 2 + (c % 2)
                    for kc in range(8):
                        P.op("pe", lambda e, c=c, kc=kc, bq=bq: e.matmul(PS[bq][:, :], lhsT=wq[:, kc, c * 128:(c + 1) * 128], rhs=xnT[:, kc, :],
                                                                        start=(kc == 0), stop=(kc == 7)), reads=[f"wq{c}", "xnT"], writes=[psk(bq)])
                    for j in range(2):
                        P.op("act", lambda e, c=c, bq=bq, j=j: e.activation(out=qT[64 * j:64 * j + 64, 2 * c + j, :], in_=PS[bq][64 * j:64 * j + 64, :], func=AF.Copy, scale=0.125),
                             reads=[psk(bq)], writes=["qT"])
                for g in range(4):
                    bq = 2 + (g % 2)
                    for kc in range(8):
                        P.op("pe", lambda e, g=g, kc=kc, bq=bq: e.matmul(PS[bq][:, :], lhsT=wkd[:, kc, g * 128:(g + 1) * 128], rhs=xnT[:, kc, :],
                                                                        start=(kc == 0), stop=(kc == 7)), reads=["wkd", "xnT"], writes=[psk(bq)])
                    P.op("dve", lambda e, g=g, bq=bq: e.tensor_copy(out=kT2[:, g, 128:128 + T], in_=PS[bq][:, :]), reads=[psk(bq)], writes=["kT2"])
                for s in range(4):
                    bq = 2 + (s % 2)
                    for kc in range(8):
                        P.op("pe", lambda e, s=s, kc=kc, bq=bq: e.matmul(PS[bq][:, 0:256], lhsT=xnT[:, kc, s * 128:(s + 1) * 128], rhs=wv[:, kc, :],
                                                                        start=(kc == 0), stop=(kc == 7)), reads=["wv", "xnT"], writes=[psk(bq)])
                    P.op("act", lambda e, s=s, bq=bq: e.copy(out=vtok[:, s + 1, :], in_=PS[bq][:, 0:256]), reads=[psk(bq)], writes=["vtok"])
                    if st == SEQ // T - 1 and s == 3 and not DBG.get('no_last'):
                        P.op("dve", lambda e, bq=bq: e.tensor_copy(out=vlast[:], in_=PS[bq][:, 0:256]), reads=[psk(bq)], writes=["vlast"])
                        if not DBG.get('no_vdma'):
                            P.dma("sp", p_swa_v[b], vlast[:], reads=["vlast"])
                        for kc in range(8 if not DBG.get('no_k') else 0):
                            P.op("pe", lambda e, kc=kc: e.matmul(PS[3][:, 0:256], lhsT=xnT[:, kc, 384:512], rhs=wk[:, kc, :],
                                                                 start=(kc == 0), stop=(kc == 7)), reads=["wk", "xnT"], writes=[psk(3)])
                        if not DBG.get('no_k'):
                            P.op("dve", lambda e: e.tensor_copy(out=klast[:], in_=PS[3][:, 0:256]), reads=[psk(3)], writes=["klast"])
                            P.dma("sp", p_swa_k[b], klast[:], reads=["klast"])
                _chk('proj')
                items = [(s_, c_) for s_ in range(4) for c_ in range(8)]

                def stage_a1(idx, pe_part):
                    s, c = items[idx]
                    first = (st == 0 and s == 0)
                    k0 = 128 if first else 0
                    nk = 128 if first else 256
                    g = c // 2
                    r = idx % NR
                    bs = 4 + (idx % 2)
                    scr, per, smr = sc[r], pe_[r], sm[r]
                    kS, kP, kM = f"sc{r}", f"pe{r}", f"sm{r}"
                    if pe_part:
                        for j in range(2):
                            P.op("pe", lambda e, j=j: e.matmul(PS[bs][:, j * 256:j * 256 + nk], lhsT=qT[:, 2 * c + j, s * 128:(s + 1) * 128],
                                                             rhs=kT2[:, g, s * 128 + k0:s * 128 + k0 + nk], start=True, stop=True),
                                 reads=["qT", "kT2"], writes=[psk(bs)])
                        return
                    P.op("dve", lambda e: e.tensor_copy(out=scr[:, :, nk], in_=sinkb[:, 2 * c:2 * c + 2]), reads=["sinkb", kS], writes=[kS])
                    P.op("dve", lambda e: e.tensor_tensor(out=scr[:, :, 0:nk], in0=PS[bs][:].rearrange("p (j k) -> p j k", j=2)[:, :, 0:nk],
                                                          in1=AB[:, 2 * c:2 * c + 2, k0:k0 + nk], op=ALU.add), reads=[psk(bs), "AB", kS], writes=[kS])
                    P.op("dve", lambda e: e.tensor_reduce(out=smr[:, 2:4], in_=scr[:, :, 0:nk + 1], axis=AX.X, op=ALU.max, negate=True), reads=[kS], writes=[kM + "b"])
                    for j in range(2):
                        P.op("act", lambda e, j=j: e.activation(out=per[:, j, 0:nk + 1], in_=scr[:, j, 0:nk + 1], func=AF.Exp, bias=smr[:, 2 + j:3 + j],
                                                             accum_out=smr[:, 4 + j:5 + j]), reads=[kS, kM + "b"], writes=[kP, kM + f"c{j}"])

                def stage_a2(idx):
                    s, c = items[idx]
                    first = (st == 0 and s == 0)
                    nk = 128 if first else 256
                    r = idx % NR
                    per, pnr, smr = pe_[r], pn[r], sm[r]
                    kP, kN, kM = f"pe{r}", f"pn{r}", f"sm{r}"
                    P.op("dve", lambda e: e.reciprocal(out=smr[:, 6:8], in_=smr[:, 4:6]), reads=[kM + "c0", kM + "c1"], writes=[kM + "d"])
                    P.op("dve", lambda e: e.tensor_tensor(out=pnr[:, :, 0:nk], in0=per[:, :, 0:nk], in1=smr[:, 6:8].unsqueeze(2).to_broadcast([128, 2, nk]), op=ALU.mult),
                         reads=[kP, kM + "d"], writes=[kN])

                def stage_b1(idx):
                    s, c = items[idx]
                    first = (st == 0 and s == 0)
                    nkb = 1 if first else 2
                    r = idx % NR
                    pnr, pTr = pn[r], pT[r]
                    kN, kT_ = f"pn{r}", f"pT{r}"
                    bt = 6 + (idx % 2)
                    pbk = PS[bt][:].bitcast(BF16)
                    for j in range(2):
                        for kb in range(nkb):
                            P.op("pe", lambda e, j=j, kb=kb: e.transpose(out=pbk[:, (j * 2 + kb) * 128:(j * 2 + kb + 1) * 128], in_=pnr[:, j, kb * 128:(kb + 1) * 128], identity=identb[:]),
                                 reads=[kN, "identb"], writes=[psk(bt)])
                    P.op("act", lambda e: e.copy(out=pTr[:].rearrange("p (j k) t -> p j k t", j=2)[:, :, 0:nkb, :],
                                                 in_=pbk[:, 0:512].rearrange("p (j k t) -> p j k t", j=2, k=2)[:, :, 0:nkb, :]), reads=[psk(bt)], writes=[kT_])

                def stage_b2(idx):
                    s, c = items[idx]
                    first = (st == 0 and s == 0)
                    nkb = 1 if first else 2
                    g = c // 2
                    r = idx % NR
                    pTr, kT_ = pT[r], f"pT{r}"
                    ob = idx % 2
                    for j in range(2):
                        for kb in range(nkb):
                            vs = s + kb + (1 if first else 0)
                            P.op("pe", lambda e, j=j, kb=kb, vs=vs: e.matmul(PS[ob][64 * j:64 * j + 64, 0:128], lhsT=vtok[:, vs, g * 64:(g + 1) * 64], rhs=pTr[:, j * 2 + kb, :],
                                                                         start=(kb == 0), stop=(kb == nkb - 1)), reads=["vtok", kT_], writes=[psk(ob)])
                    P.op("act", lambda e: e.copy(out=oT[:, c, s * 128:(s + 1) * 128], in_=PS[ob][:, 0:128]), reads=[psk(ob)], writes=["oT"])

                stage_a1(0, True)
                stage_a1(0, False)
                stage_a1(1, True)
                stage_a1(1, False)
                for idx in range(len(items)):
                    if idx + 2 < len(items):
                        stage_a1(idx + 2, True)
                    stage_a2(idx)
                    stage_b1(idx)
                    if idx + 2 < len(items):
                        stage_a1(idx + 2, False)
                    stage_b2(idx)
                P.op("pool", lambda e: e.tensor_copy(out=kT2[:, :, 0:128], in_=kT2[:, :, T:T + 128]), reads=["kT2"], writes=["kT2"])
                P.op("pool", lambda e: e.tensor_copy(out=vtok[:, 0, :], in_=vtok[:, 4, :]), reads=["vtok"], writes=["vtok"])
                _chk(f'attn_b{b}_st{st}')
                nb_, nst_ = (b, st + 1) if st + 1 < SEQ // T else (b + 1, 0)
                nxt_ = src_p[nb_, nst_ * T:(nst_ + 1) * T, :] if nb_ < NB else None
                out_proj_store(128, 4, dst_p[b, st * T:(st + 1) * T, :], next_src=nxt_)
                _chk(f'store_b{b}_st{st}')
        P.barrier()
        A.release(m1)
        if not do_samples:
            A.release(m0)
            return
        qkv_s = A.alloc("qkv_s", [NS, 1536], F32)
        o_tok = A.alloc("o_tok", [NS, 1024], F32)
        o_tb = A.alloc("o_tb", [NS, 1024], BF16)
        Kbg = A.alloc("Kbg", [64, 128, 64], F32)
        Vbg = A.alloc("Vbg", [64, 128, 64], F32)
        T1 = A.alloc("T1s", [64, 128 * 64], F32)
        qbg = A.alloc("qbg", [64, 4, 64], F32)
        knbg = A.alloc("knbg", [64, 64], F32)
        vnbg = A.alloc("vnbg", [64, 64], F32)
        T2 = A.alloc("T2s", [64, 4, 64], F32)
        Sall = A.alloc("Sall", [64, 4, 129], F32)
        Pm = A.alloc("Pm", [64, 4, 129], F32)
        RELs = A.alloc("RELs", [64, 128], F32)
        nslope = A.alloc("nslope", [64, 4], F32)
        sinkbg = A.alloc("sinkbg", [64, 4], F32)
        nsink = A.alloc("nsink", [64, 4], F32)
        sms = A.alloc("sms", [64, 6, 4], F32)
        Oall = A.alloc("Oall", [64, 4, 64], F32)
        slp = A.alloc("slp", [1, 16], F32)

        P.op("pool", lambda e: e.iota(slp[:], pattern=[[1, 16]], base=1, channel_multiplier=0, allow_small_or_imprecise_dtypes=True), writes=["slp"])
        P.op("act", lambda e: e.activation(out=slp[:], in_=slp[:], func=AF.Exp, scale=-0.5 * float(np.log(2.0))), reads=["slp"], writes=["slp"])
        P.op("dve", lambda e: e.tensor_scalar(out=slp[:], in0=slp[:], scalar1=-1.0, scalar2=None, op0=ALU.mult), reads=["slp"], writes=["slp"])
        P.dma("sp", slope_d.rearrange("(o h) -> o h", o=1), slp[:], reads=["slp"], writes=["slope_d"])
        for g_ in range(4):
            P.dma("sp", nslope[16 * g_:16 * g_ + 16, :], slope_d[4 * g_:4 * g_ + 4].partition_broadcast(NS), reads=["slope_d"], writes=["nslope"])
            P.dma("act", sinkbg[16 * g_:16 * g_ + 16, :], sinks_c[4 * g_:4 * g_ + 4].partition_broadcast(NS), writes=["sinkbg"])
        P.op("dve", lambda e: e.tensor_scalar(out=nsink[:], in0=sinkbg[:], scalar1=-1.0, scalar2=None, op0=ALU.mult), reads=["sinkbg"], writes=["nsink"])
        P.op("pool", lambda e: e.iota(RELs[:], pattern=[[-1, 128]], base=128, channel_multiplier=0, allow_small_or_imprecise_dtypes=True), writes=["RELs"])

        P.dma("sp", s_swa_k[:, 0:127, :], cache_k[:, 1:128, :])
        P.dma("sp", s_swa_v[:, 0:127, :], cache_v[:, 1:128, :])
        for g_ in range(4):
            P.dma("sp" if g_ % 2 == 0 else "act", Kbg[16 * g_:16 * g_ + 16, :, :], cache_k[:, :, g_ * 64:(g_ + 1) * 64], writes=["Kbg"])
            P.dma("act" if g_ % 2 == 0 else "sp", Vbg[16 * g_:16 * g_ + 16, :, :], cache_v[:, :, g_ * 64:(g_ + 1) * 64], writes=["Vbg"])

        P.dma("sp", xt[:NS, 0, :], src_s, writes=["xt0"])
        norm_transpose(xt[:NS, 0, :], NS, "xt0", xn, junk, ss, rstd, xnT, "xnT", 0, "a", wrow)
        groups = [(wq, WQ_ALL, 0, 512, 0, 0.125), (wq, WQ_ALL, 512, 512, 512, 0.125), (wk, ["wk"], 0, 256, 1024, 1.0), (wv, ["wv"], 0, 256, 1280, 1.0)]
        for gi, (wt, wkey, c0, ncol, o0, scl) in enumerate(groups):
            bq = 2 + (gi % 2)
            for kc in range(8):
                P.op("pe", lambda e, wt=wt, c0=c0, ncol=ncol, kc=kc, bq=bq: e.matmul(PS[bq][:NS, 0:ncol], lhsT=xnT[:, kc, 0:NS], rhs=wt[:, kc, c0:c0 + ncol],
                                                                                 start=(kc == 0), stop=(kc == 7)), reads=wkey + ["xnT"], writes=[psk(bq)])
            P.op("act", lambda e, bq=bq, ncol=ncol, o0=o0, scl=scl: e.activation(out=qkv_s[:, o0:o0 + ncol], in_=PS[bq][:NS, 0:ncol], func=AF.Copy, scale=scl),
                 reads=[psk(bq)], writes=["qkv_s"])
        P.dma("sp", sq_q, qkv_s[:, 0:1024], reads=["qkv_s"], writes=["sq_q"])
        P.dma("sp", sq_k, qkv_s[:, 1024:1280], reads=["qkv_s"], writes=["sq_k"])
        P.dma("sp", sq_v, qkv_s[:, 1280:1536], reads=["qkv_s"], writes=["sq_v"])
        P.dma("sp", s_swa_k[:, 127, :], qkv_s[:, 1024:1280], reads=["qkv_s"])
        P.dma("sp", s_swa_v[:, 127, :], qkv_s[:, 1280:1536], reads=["qkv_s"])
        for g_ in range(4):
            P.dma("sp", qbg[16 * g_:16 * g_ + 16].rearrange("p r d -> p (r d)"), sq_q[:, g_ * 256:(g_ + 1) * 256], reads=["sq_q"], writes=["qbg"])
            P.dma("act", knbg[16 * g_:16 * g_ + 16, :], sq_k[:, g_ * 64:(g_ + 1) * 64], reads=["sq_k"], writes=["knbg"])
            P.dma("act", vnbg[16 * g_:16 * g_ + 16, :], sq_v[:, g_ * 64:(g_ + 1) * 64], reads=["sq_v"], writes=["vnbg"])

        T1k = T1[:].rearrange("p (k d) -> p k d", k=128)
        T1d = T1[:].rearrange("p (d k) -> p d k", d=64)
        for r in range(4):
            eng = "dve"
            P.op(eng, lambda e, r=r: e.tensor_tensor(out=T1k, in0=Kbg[:], in1=qbg[:, r, :].unsqueeze(1).to_broadcast([64, 128, 64]), op=ALU.mult),
                 reads=["Kbg", "qbg"], writes=["T1s"])
            P.op("dve", lambda e, r=r: e.tensor_reduce(out=Sall[:, r, 0:128], in_=T1k, axis=AX.X, op=ALU.add), reads=["T1s"], writes=["Sall"])
        P.op("dve", lambda e: e.tensor_tensor(out=T2[:], in0=qbg[:], in1=knbg[:].unsqueeze(1).to_broadcast([64, 4, 64]), op=ALU.mult),
             reads=["qbg", "knbg"], writes=["T2s"])
        P.op("dve", lambda e: e.tensor_reduce(out=Sall[:, :, 128], in_=T2[:], axis=AX.X, op=ALU.add), reads=["T2s", "Sall"], writes=["Sall"])
        for r in range(4):
            P.op("dve", lambda e, r=r: e.scalar_tensor_tensor(out=Sall[:, r, 0:128], in0=RELs[:], scalar=nslope[:, r:r + 1], in1=Sall[:, r, 0:128],
                                                             op0=ALU.mult, op1=ALU.add), reads=["RELs", "nslope", "Sall"], writes=["Sall"])
        P.op("dve", lambda e: e.tensor_reduce(out=sms[:, 0, :], in_=Sall[:], axis=AX.X, op=ALU.max), reads=["Sall"], writes=["sms0"])
        P.op("dve", lambda e: e.scalar_tensor_tensor(out=sms[:, 1, :], in0=sms[:, 0, :], scalar=-1.0, in1=nsink[:], op0=ALU.mult, op1=ALU.min),
             reads=["sms0", "nsink"], writes=["sms1"])
        for r in range(4):
            P.op("act", lambda e, r=r: e.activation(out=Pm[:, r, :], in_=Sall[:, r, :], func=AF.Exp, bias=sms[:, 1, r:r + 1], accum_out=sms[:, 2, r:r + 1]),
                 reads=["Sall", "sms1"], writes=["Pm", "sms2"])
        P.op("dve", lambda e: e.tensor_tensor(out=sms[:, 3, :], in0=sinkbg[:], in1=sms[:, 1, :], op=ALU.add), reads=["sinkbg", "sms1"], writes=["sms3"])
        P.op("act", lambda e: e.activation(out=sms[:, 3, :], in_=sms[:, 3, :], func=AF.Exp), reads=["sms3"], writes=["sms3"])
        P.op("dve", lambda e: e.tensor_tensor(out=sms[:, 4, :], in0=sms[:, 2, :], in1=sms[:, 3, :], op=ALU.add), reads=["sms2", "sms3"], writes=["sms4"])
        P.op("dve", lambda e: e.reciprocal(out=sms[:, 5, :], in_=sms[:, 4, :]), reads=["sms4"], writes=["sms5"])
        Vperm = Vbg[:].rearrange("p k d -> p d k")
        for r in range(4):
            eng = "dve"
            P.op(eng, lambda e, r=r: e.tensor_tensor(out=T1d, in0=Vperm, in1=Pm[:, r, 0:128].unsqueeze(1).to_broadcast([64, 64, 128]), op=ALU.mult),
                 reads=["Vbg", "Pm"], writes=["T1s"])
            P.op("dve", lambda e, r=r: e.tensor_reduce(out=Oall[:, r, :], in_=T1d, axis=AX.X, op=ALU.add), reads=["T1s"], writes=["Oall"])
            P.op("dve", lambda e, r=r: e.scalar_tensor_tensor(out=Oall[:, r, :], in0=vnbg[:], scalar=Pm[:, r, 128:129], in1=Oall[:, r, :], op0=ALU.mult, op1=ALU.add),
                 reads=["vnbg", "Pm", "Oall"], writes=["Oall"])
            P.op("dve", lambda e, r=r: e.tensor_scalar(out=Oall[:, r, :], in0=Oall[:, r, :], scalar1=sms[:, 5, r:r + 1], scalar2=None, op0=ALU.mult),
                 reads=["Oall", "sms5"], writes=["Oall"])
        for g_ in range(4):
            P.dma("sp", so_s[:, g_ * 256:(g_ + 1) * 256], Oall[16 * g_:16 * g_ + 16].rearrange("p r d -> p (r d)"), reads=["Oall"], writes=["so_s"])
        P.dma("sp", o_tok[:], so_s, reads=["so_s"], writes=["o_tok"])
        transpose_to_fm(o_tok[:, :], NS, "o_tok", oT, "oT", 8, o_tb, "o_tb", 0)
        out_proj_store(NS, 1, dst_s)
        P.barrier()
        A.release(m0)

    if stages == "ffn0":
        ffn_stage(0, x_prompt, x_sample, y_prompt, y_sample, final=False)
    elif stages == "attn":
        attn_stage(x_prompt, x_sample, y_prompt, y_sample)
    elif stages == "attn_p":
        try:
            attn_stage(x_prompt, x_sample, y_prompt, y_sample, do_samples=False)
        except _Stop:
            pass
    elif stages == "attn_s":
        attn_stage(x_prompt, x_sample, y_prompt, y_sample, do_prompt=False)
    elif stages == "mix0":
        mix0_stage(x_prompt, x_sample, y_prompt, y_sample)
    elif stages == "mix0_p":
        try:
            mix0_stage(x_prompt, x_sample, y_prompt, y_sample, do_samples=False)
        except _Stop:
            pass
    else:
        mix0_stage(x_prompt, x_sample, xa_p, xa_s)
        ffn_stage(0, xa_p, xa_s, xb_p, xb_s, final=False)
        attn_stage(xb_p, xb_s, xa_p, xa_s)
        ffn_stage(1, xa_p, xa_s, y_prompt, y_sample, final=True)

    P.emit()
    return nc


_NC_CACHE = {}

OUT_NAMES = ["y_prompt", "y_sample", "p_gdn", "p_gdn_conv", "p_lru", "p_lru_conv", "p_swa_k", "p_swa_v",
             "s_gdn", "s_gdn_conv", "s_lru", "s_lru_conv", "s_swa_k", "s_swa_v"]


def make_in_maps(inputs, cores):
    f = lambda a: np.ascontiguousarray(np.asarray(a, dtype=np.float32))
    maps = []
    for c in cores:
        pb = slice(c * NB, (c + 1) * NB)
        sb = slice(c * NS, (c + 1) * NS)
        m = {
            "x_prompt": f(inputs["x_prompt"][pb]),
            "x_sample": f(inputs["x_sample"][sb].reshape(NS, D)),
            "state_gdn": f(inputs["state_gdn"][0, sb]),
            "state_gdn_conv": f(inputs["state_gdn_conv"][0, sb]),
            "state_lru": f(inputs["state_lru"][0, sb]),
            "state_lru_conv": f(inputs["state_lru_conv"][0, sb]),
            "cache_swa_k": f(inputs["cache_swa_k"][0, sb].reshape(NS, 128, 256)),
            "cache_swa_v": f(inputs["cache_swa_v"][0, sb].reshape(NS, 128, 256)),
            "norm_mix": f(inputs["norm_mix"]),
            "norm_ffn": f(inputs["norm_ffn"]),
            "norm_final": f(inputs["norm_final"]),
            "w_in_ab": f(inputs["w_in_ab"][0]),
            "conv_gdn_w": f(inputs["conv_gdn_w"][0]),
            "gdn_a_log": f(inputs["gdn_a_log"][0]),
            "gdn_dt_bias": f(inputs["gdn_dt_bias"][0]),
            "gdn_norm_w": f(inputs["gdn_norm_w"][0]),
            "conv_lru_w": f(inputs["conv_lru_w"][0]),
            "conv_lru_b": f(inputs["conv_lru_b"][0]),
            "lru_wa": f(inputs["lru_wa"][0]),
            "lru_ba": f(inputs["lru_ba"][0]),
            "lru_wx": f(inputs["lru_wx"][0]),
            "lru_bx": f(inputs["lru_bx"][0]),
            "lru_lambda": f(inputs["lru_lambda"][0]),
            "w_out_ab": f(inputs["w_out_ab"][0]),
            "w_qkv_c": f(inputs["w_qkv_c"][0]),
            "w_out_c": f(inputs["w_out_c"][0]),
            "sinks_c": f(inputs["sinks_c"][0]),
            "w_gate_up": f(inputs["w_gate_up"]),
            "w_down": f(inputs["w_down"]),
        }
        maps.append(m)
    return maps


def assemble(results):
    cat = lambda n: np.concatenate([np.asarray(r[n]) for r in results], axis=0)
    nb = NB * len(results)
    ns = NS * len(results)
    return (
        cat("y_prompt"),
        cat("y_sample").reshape(ns, 1, D),
        cat("p_gdn")[None],
        cat("p_gdn_conv")[None],
        cat("p_lru")[None],
        cat("p_lru_conv")[None],
        cat("p_swa_k").reshape(1, nb, 128, 4, 64),
        cat("p_swa_v").reshape(1, nb, 128, 4, 64),
        cat("s_gdn")[None],
        cat("s_gdn_conv")[None],
        cat("s_lru")[None],
        cat("s_lru_conv")[None],
        cat("s_swa_k").reshape(1, ns, 128, 4, 64),
        cat("s_swa_v").reshape(1, ns, 128, 4, 64),
    )


def kernel(**inputs):
    if "nc" not in _NC_CACHE:
        _NC_CACHE["nc"] = build_program()
    nc = _NC_CACHE["nc"]
    in_maps = make_in_maps(inputs, list(range(NCORES)))
    res = run_bass_kernel_spmd(nc, in_maps, core_ids=list(range(NCORES)))
    return assemble(res.results)
```
